# Optimizing a Trainium2 kernel written in Bass

```python
import math
import jax, jax.numpy as jnp
from jax import lax
import numpy as np

D_MODEL = 1024
BATCH = 8
SEQ = 2048
DEPTH = 1
DEC_BATCH = 128
DEC_SEQ = 8
PAST_LEN = 16384
PAGE_SIZE = 128

CHUNK = 128
D_A = D_MODEL
A_GROUPS = 8
A_GROUP_DIM = D_A // A_GROUPS
EXPAND = 2
D_INNER = EXPAND * D_MODEL
SSD_HEAD_DIM = 64
SSD_HEADS = D_INNER // SSD_HEAD_DIM
SSD_GROUPS = 8
SSD_STATE = 128
CONV_W = 4
CONV_DIM = D_INNER + 2 * SSD_GROUPS * SSD_STATE
SSD_CHUNK = 128
N_MEM = 256
X_HEADS = 4
X_HEAD_DIM = 128
D_X = X_HEADS * X_HEAD_DIM
N_BRANCH = 3
EPS = 1e-6
IN_SIZES = (D_A, D_A, D_A, D_INNER, CONV_DIM, SSD_HEADS, D_X, D_X, N_BRANCH * D_MODEL)
IN_DIM = sum(IN_SIZES)

kernel_name = "hybrid_gmlp_ssd_memattn_step"


def _split_points(sizes):
    pts, acc = [], 0
    for s in sizes[:-1]:
        acc += s
        pts.append(acc)
    return pts


def _rmsnorm(x, g):
    xf = x.astype(jnp.float32)
    xf = xf * lax.rsqrt(jnp.mean(xf * xf, axis=-1, keepdims=True) + EPS)
    return (xf * g.astype(jnp.float32)).astype(x.dtype)


def _layernorm(x, g, b):
    xf = x.astype(jnp.float32)
    mu = jnp.mean(xf, axis=-1, keepdims=True)
    xc = xf - mu
    xf = xc * lax.rsqrt(jnp.mean(xc * xc, axis=-1, keepdims=True) + EPS)
    return (xf * g.astype(jnp.float32) + b.astype(jnp.float32)).astype(x.dtype)


def _gated_group_rmsnorm(y, z, g):
    yf = (y * jax.nn.silu(z)).astype(jnp.float32)
    sh = yf.shape
    yg = yf.reshape(sh[:-1] + (SSD_GROUPS, D_INNER // SSD_GROUPS))
    yg = yg * lax.rsqrt(jnp.mean(yg * yg, axis=-1, keepdims=True) + EPS)
    return (yg.reshape(sh) * g.astype(jnp.float32)).astype(y.dtype)


def _gmlp_spatial(v, w_s, b_s):
    bsz, L, _ = v.shape
    n_c = -(-L // CHUNK)
    pad = n_c * CHUNK - L
    vp = jnp.pad(v, ((0, 0), (0, pad), (0, 0)))
    vr = vp.reshape(bsz, n_c, CHUNK, A_GROUPS, A_GROUP_DIM)
    mask = jnp.tril(jnp.ones((CHUNK, CHUNK), dtype=bool))
    w = jnp.where(mask[None], w_s, jnp.zeros_like(w_s))
    s = jnp.einsum('gts,bcsgd->bctgd', w, vr) + b_s.T[None, None, :, :, None]
    return s.reshape(bsz, n_c * CHUNK, D_A)[:, :L]


def _causal_conv(xbc, conv_state, w, b):
    L = xbc.shape[1]
    xpad = jnp.concatenate([conv_state.astype(xbc.dtype), xbc], axis=1)
    out = b + sum(xpad[:, k:k + L, :] * w[k] for k in range(CONV_W))
    return jax.nn.silu(out), xpad[:, -(CONV_W - 1):, :]


def _segsum_decay(a_cs):
    q = a_cs.shape[-1]
    diff = a_cs[..., :, None] - a_cs[..., None, :]
    mask = jnp.tril(jnp.ones((q, q), dtype=bool))
    return jnp.exp(jnp.where(mask, diff, -jnp.inf))


def _ssd(x, dt, a, b_in, c_in, h0):
    f32 = jnp.float32
    bsz, L = x.shape[:2]
    q = min(SSD_CHUNK, L)
    n_c = -(-L // q)
    pad = n_c * q - L
    R = SSD_HEADS // SSD_GROUPS
    x_dt = x.astype(f32) * dt[..., None]
    adt = dt * a
    pad_t = lambda t: jnp.pad(t, ((0, 0), (0, pad)) + ((0, 0),) * (t.ndim - 2))
    xc = pad_t(x_dt).reshape(bsz, n_c, q, SSD_GROUPS, R, SSD_HEAD_DIM)
    ac = pad_t(adt).reshape(bsz, n_c, q, SSD_GROUPS, R).transpose(0, 3, 4, 1, 2)
    bc = pad_t(b_in.astype(f32)).reshape(bsz, n_c, q, SSD_GROUPS, SSD_STATE)
    cc = pad_t(c_in.astype(f32)).reshape(bsz, n_c, q, SSD_GROUPS, SSD_STATE)
    a_cs = jnp.cumsum(ac, axis=-1)
    decay = _segsum_decay(a_cs)
    cb = jnp.einsum('bclgn,bcsgn->bgcls', cc, bc)
    y_diag = jnp.einsum('bgcls,bgrcls,bcsgrp->bclgrp', cb, decay, xc)
    decay_states = jnp.exp(a_cs[..., -1:] - a_cs)
    states = jnp.einsum('bcsgn,bgrcs,bcsgrp->cbgrpn', bc, decay_states, xc)
    chunk_decay = jnp.exp(a_cs[..., -1]).transpose(3, 0, 1, 2)
    h_init = h0.astype(f32).reshape(bsz, SSD_GROUPS, R, SSD_HEAD_DIM, SSD_STATE)

    def step(h, inp):
        s, d = inp
        return h * d[..., None, None] + s, h

    h_final, h_prev = lax.scan(step, h_init, (states, chunk_decay))
    y_off = jnp.einsum('bclgn,cbgrpn,bgrcl->bclgrp', cc, h_prev, jnp.exp(a_cs))
    y = (y_diag + y_off).reshape(bsz, n_c * q, SSD_HEADS, SSD_HEAD_DIM)[:, :L]
    return y, h_final.reshape(bsz, SSD_HEADS, SSD_HEAD_DIM, SSD_STATE)


def _mem_attention(q, mem_k, mem_v):
    bsz, L, _ = q.shape
    qh = q.reshape(bsz, L, X_HEADS, X_HEAD_DIM)
    s = jnp.einsum('blhd,bmhd->bhlm', qh, mem_k.astype(q.dtype)).astype(jnp.float32) * (X_HEAD_DIM ** -0.5)
    pr = jax.nn.softmax(s, axis=-1).astype(q.dtype)
    o = jnp.einsum('bhlm,bmhd->blhd', pr, mem_v.astype(q.dtype))
    return o.reshape(bsz, L, D_X)


def _mem_kv(mem, mem_norm_g, w_mem_kv):
    bsz = mem.shape[0]
    kv = _rmsnorm(mem, mem_norm_g) @ w_mem_kv
    k, v = jnp.split(kv, 2, axis=-1)
    return (k.reshape(bsz, N_MEM, X_HEADS, X_HEAD_DIM), v.reshape(bsz, N_MEM, X_HEADS, X_HEAD_DIM))


def _layer(x, mem_k, mem_v, h0, conv0, norm_g, w_in, conv_w, conv_b, dt_bias, a_log, d_skip, ssd_norm_g,
           ln_v_g, ln_v_b, w_spatial, b_spatial, w_proj_a, w_proj_b, w_proj_x, w_out):
    bsz, L, _ = x.shape
    hn = _rmsnorm(x, norm_g)
    proj = hn @ w_in
    u, v, gate_a, z, xbc, dt_raw, q, gate_x, merge = jnp.split(proj, _split_points(IN_SIZES), axis=-1)
    u = jax.nn.gelu(u, approximate=False)
    v = _layernorm(jax.nn.gelu(v, approximate=False), ln_v_g, ln_v_b)
    h_a = u * _gmlp_spatial(v, w_spatial, b_spatial) * jax.nn.silu(gate_a)
    xbc, conv_new = _causal_conv(xbc, conv0, conv_w, conv_b)
    xs, bm, cm = jnp.split(xbc, [D_INNER, D_INNER + SSD_GROUPS * SSD_STATE], axis=-1)
    dt = jax.nn.softplus(dt_raw.astype(jnp.float32) + dt_bias.astype(jnp.float32))
    a = -jnp.exp(a_log.astype(jnp.float32))
    xh = xs.reshape(bsz, L, SSD_HEADS, SSD_HEAD_DIM)
    y, h_new = _ssd(xh, dt, a,
                    bm.reshape(bsz, L, SSD_GROUPS, SSD_STATE),
                    cm.reshape(bsz, L, SSD_GROUPS, SSD_STATE), h0)
    y = y + xh.astype(jnp.float32) * d_skip.astype(jnp.float32)[:, None]
    h_b = _gated_group_rmsnorm(y.reshape(bsz, L, D_INNER).astype(x.dtype), z, ssd_norm_g)
    h_x = _mem_attention(q, mem_k, mem_v) * jax.nn.silu(gate_x)
    g_a, g_b, g_x = jnp.split(jax.nn.sigmoid(merge), N_BRANCH, axis=-1)
    m = g_a * (h_a @ w_proj_a) + g_b * (h_b @ w_proj_b) + g_x * (h_x @ w_proj_x)
    return x + m @ w_out, h_new.astype(h0.dtype), conv_new, v


def setup_inputs(seed: int = 0) -> dict:
    key = jax.random.key(seed)
    ks = jax.random.split(key, 32)
    f32 = jnp.float32
    nrm = lambda k, shape, s: jax.random.normal(k, shape, f32) * s
    Dp = DEPTH
    dt0 = jnp.exp(jax.random.uniform(ks[10], (Dp, SSD_HEADS), f32) * (math.log(0.1) - math.log(0.001)) + math.log(0.001))
    return {
        "x_prompt": nrm(ks[0], (BATCH, SEQ, D_MODEL), 1.0),
        "x_sample": nrm(ks[1], (DEC_BATCH, DEC_SEQ, D_MODEL), 1.0),
        "mem_prompt": nrm(ks[2], (BATCH, N_MEM, D_MODEL), 1.0),
        "cache_mem_k": nrm(ks[3], (Dp, DEC_BATCH, N_MEM, X_HEADS, X_HEAD_DIM), 1.0),
        "cache_mem_v": nrm(ks[4], (Dp, DEC_BATCH, N_MEM, X_HEADS, X_HEAD_DIM), 1.0),
        "state_ssm": nrm(ks[5], (Dp, DEC_BATCH, SSD_HEADS, SSD_HEAD_DIM, SSD_STATE), 0.1),
        "state_conv": nrm(ks[6], (Dp, DEC_BATCH, CONV_W - 1, CONV_DIM), 1.0),
        "norm_g": 1.0 + nrm(ks[7], (Dp, D_MODEL), 0.02),
        "w_in": nrm(ks[8], (Dp, D_MODEL, IN_DIM), D_MODEL ** -0.5),
        "conv_w": nrm(ks[9], (Dp, CONV_W, CONV_DIM), CONV_W ** -0.5),
        "conv_b": nrm(ks[11], (Dp, CONV_DIM), 0.02),
        "dt_bias": dt0 + jnp.log(-jnp.expm1(-dt0)),
        "a_log": jnp.log(jax.random.uniform(ks[12], (Dp, SSD_HEADS), f32, 1.0, 16.0)),
        "d_skip": 1.0 + nrm(ks[13], (Dp, SSD_HEADS), 0.1),
        "ssd_norm_g": 1.0 + nrm(ks[14], (Dp, D_INNER), 0.02),
        "ln_v_g": 1.0 + nrm(ks[15], (Dp, D_A), 0.02),
        "ln_v_b": nrm(ks[16], (Dp, D_A), 0.02),
        "w_spatial": nrm(ks[17], (Dp, A_GROUPS, CHUNK, CHUNK), CHUNK ** -0.5),
        "b_spatial": nrm(ks[18], (Dp, A_GROUPS, CHUNK), 0.02),
        "mem_norm_g": 1.0 + nrm(ks[19], (Dp, D_MODEL), 0.02),
        "w_mem_kv": nrm(ks[20], (Dp, D_MODEL, 2 * D_X), D_MODEL ** -0.5),
        "w_proj_a": nrm(ks[21], (Dp, D_A, D_MODEL), D_A ** -0.5),
        "w_proj_b": nrm(ks[22], (Dp, D_INNER, D_MODEL), D_INNER ** -0.5),
        "w_proj_x": nrm(ks[23], (Dp, D_X, D_MODEL), D_X ** -0.5),
        "w_out": nrm(ks[24], (Dp, D_MODEL, D_MODEL), D_MODEL ** -0.5),
        "final_norm_g": 1.0 + nrm(ks[25], (D_MODEL,), 0.02),
    }


def reference(x_prompt, x_sample, mem_prompt, cache_mem_k, cache_mem_v, state_ssm, state_conv, norm_g, w_in,
              conv_w, conv_b, dt_bias, a_log, d_skip, ssd_norm_g, ln_v_g, ln_v_b, w_spatial, b_spatial,
              mem_norm_g, w_mem_kv, w_proj_a, w_proj_b, w_proj_x, w_out, final_norm_g):
    yp, ys = x_prompt, x_sample
    bsz = x_prompt.shape[0]
    mk_l, mv_l, hp_l, cp_l, hs_l, cs_l, vs_l = [], [], [], [], [], [], []
    for l in range(DEPTH):
        lw = (norm_g[l], w_in[l], conv_w[l], conv_b[l], dt_bias[l], a_log[l], d_skip[l], ssd_norm_g[l],
              ln_v_g[l], ln_v_b[l], w_spatial[l], b_spatial[l], w_proj_a[l], w_proj_b[l], w_proj_x[l], w_out[l])
        mk, mv = _mem_kv(mem_prompt, mem_norm_g[l], w_mem_kv[l])
        h0 = jnp.zeros((bsz, SSD_HEADS, SSD_HEAD_DIM, SSD_STATE), state_ssm.dtype)
        c0 = jnp.zeros((bsz, CONV_W - 1, CONV_DIM), x_prompt.dtype)
        yp, hp, cp, _ = _layer(yp, mk, mv, h0, c0, *lw)
        ys, hs, cs, vs = _layer(ys, cache_mem_k[l], cache_mem_v[l], state_ssm[l], state_conv[l], *lw)
        mk_l.append(mk); mv_l.append(mv); hp_l.append(hp); cp_l.append(cp)
        hs_l.append(hs); cs_l.append(cs); vs_l.append(vs)
    y_prompt = _rmsnorm(yp, final_norm_g)
    y_sample = _rmsnorm(ys, final_norm_g)
    return (y_prompt, y_sample, jnp.stack(mk_l), jnp.stack(mv_l), jnp.stack(hp_l), jnp.stack(cp_l),
            jnp.stack(hs_l), jnp.stack(cs_l), jnp.stack(vs_l))
```

```python
import os
import numpy as np
from contextlib import ExitStack
import concourse.bass as bass
import concourse.mybir as mybir
from concourse.bass_utils import run_bass_kernel_spmd

F32 = mybir.dt.float32
BF16 = mybir.dt.bfloat16
AF = mybir.ActivationFunctionType
ALU = mybir.AluOpType

NCORES = 8
STOP = int(os.environ.get("KSTOP", "99"))
SUB = int(os.environ.get("KSUB", "99"))
KVAR = os.environ.get("KVAR", "")
D = 1024
IN_DIM = 13344
NH = 32
EPS = 1e-6
C_U, C_V, C_GA, C_Z, C_XBC, C_DT, C_Q, C_GX, C_MG = 0, 1024, 2048, 3072, 5120, 9216, 9248, 9760, 10272
SEQ = 2048
NSEQ_S = 16
LS = 8

ENGS = ("pe", "act", "dve", "pool", "sp")


class Op:
    __slots__ = ("eng", "fn", "deps", "dma", "signal", "sem", "val", "prewait")

    def __init__(self, eng, fn, deps, dma):
        self.eng = eng
        self.fn = fn
        self.deps = deps
        self.dma = dma
        self.signal = dma
        self.sem = None
        self.val = None
        self.prewait = None


class Prog:
    def __init__(self, nc, n_dma_sems=8):
        self.nc = nc
        self.ops = []
        self.last_w = {}
        self.readers = {}
        self.n_dma_sems = n_dma_sems
        self.bar_start = 0
        self.bufs = {}
        self.live = {}

    def register(self, name, start, end):
        old = self.bufs.get(name)
        if old is not None and old != (start, end) and name in self.live:
            raise RuntimeError(f"buffer {name} re-registered with a new range while live")
        self.bufs[name] = (start, end)

    def _touch(self, key, idx, eng, dma, deps):
        bname = key if isinstance(key, str) else key[0]
        rng = self.bufs.get(bname)
        if rng is None:
            return
        if bname not in self.live:
            s, e = rng
            for other in list(self.live):
                os_, oe = self.bufs[other]
                if os_ < e and s < oe:
                    acc = self.live.pop(other)
                    for d in list(acc["eng"].values()) + acc["dma"]:
                        o = self.ops[d]
                        if o.eng == eng and eng == "pe" and not o.dma and not dma:
                            continue
                        deps.add(d)
            self.live[bname] = {"eng": {}, "dma": []}
        a = self.live[bname]
        if dma:
            a["dma"].append(idx)
        else:
            a["eng"][eng] = idx

    def add(self, eng, fn, reads=(), writes=(), dma=False):
        idx = len(self.ops)
        deps = set()
        ps_r = [k for k in reads if isinstance(k, tuple) and k[0] == "ps"]
        if ps_r:
            reads = [k for k in reads if not (isinstance(k, tuple) and k[0] == "ps")]
            writes = list(writes) + ps_r

        def consider(d, raw):
            o = self.ops[d]
            if o.eng == eng and not o.dma and not dma:
                if eng == "pe":
                    return
            deps.add(d)

        for k in reads:
            w = self.last_w.get(k)
            if w is not None:
                consider(w, True)
        for k in writes:
            w = self.last_w.get(k)
            if w is not None:
                consider(w, False)
            for r in self.readers.get(k, ()):
                consider(r, False)
        for k in writes:
            self.last_w[k] = idx
            self.readers[k] = []
        for k in reads:
            self.readers.setdefault(k, []).append(idx)
        for k in list(reads) + list(writes):
            self._touch(k, idx, eng, dma, deps)
        deps.discard(idx)
        best = {}
        red = set()
        for d in deps:
            o = self.ops[d]
            if o.dma:
                red.add(d)
            elif best.get(o.eng, -1) < d:
                best[o.eng] = d
        red.update(best.values())
        deps = red
        for d in deps:
            self.ops[d].signal = True
        self.ops.append(Op(eng, fn, deps, dma))
        return idx

    def barrier(self):
        pend = range(self.bar_start, len(self.ops))
        last = {}
        dmas = []
        for d in pend:
            o = self.ops[d]
            if o.dma:
                dmas.append(d)
            else:
                last[o.eng] = d
        for e in ENGS:
            deps = set(dmas)
            for e2, d in last.items():
                if e2 != e:
                    deps.add(d)
            for d in deps:
                self.ops[d].signal = True
            self.ops.append(Op(e, lambda eng: eng.nop(), deps, False))
        self.bar_start = len(self.ops)
        self.last_w.clear()
        self.readers.clear()
        self.live.clear()
        self.bufs.clear()

    def emit(self):
        nc = self.nc
        with ExitStack() as es:
            csem = {e: es.enter_context(nc.semaphore(f"c_{e}")) for e in ENGS}
            dsem = {
                e: [es.enter_context(nc.semaphore(f"d_{e}{i}")) for i in range(self.n_dma_sems)]
                for e in ("sp", "act", "pool")
            }
            ccount = {e: 0 for e in ENGS}
            dcount = {e: [0] * self.n_dma_sems for e in dsem}
            drr = {e: 0 for e in dsem}
            for o in self.ops:
                if o.dma:
                    i = drr[o.eng]
                    drr[o.eng] = (i + 1) % self.n_dma_sems
                    o.sem = dsem[o.eng][i]
                    o.prewait = dcount[o.eng][i]
                    dcount[o.eng][i] += 16
                    o.val = dcount[o.eng][i]
                elif o.signal:
                    ccount[o.eng] += 1
                    o.sem = csem[o.eng]
                    o.val = ccount[o.eng]
            ops = self.ops
            if os.environ.get("KDEBUG"):
                print("PROG ops", len(ops), "compute signals", ccount, "dma counts", dcount)
            final_waits = [
                (dsem[e][i], dcount[e][i]) for e in dsem for i in range(self.n_dma_sems) if dcount[e][i] > 0
            ]

            def run_engine(ename, eng):
                seen = {}
                semobj = {}
                for o in ops:
                    if o.eng != ename:
                        continue
                    need = {}
                    for d in o.deps:
                        od = ops[d]
                        k = id(od.sem)
                        semobj[k] = od.sem
                        if need.get(k, 0) < od.val:
                            need[k] = od.val
                    if o.dma and o.prewait > 0:
                        k = id(o.sem)
                        semobj[k] = o.sem
                        if need.get(k, 0) < o.prewait:
                            need[k] = o.prewait
                    for k, v in need.items():
                        if seen.get(k, 0) >= v:
                            continue
                        seen[k] = v
                        eng.wait_ge(semobj[k], v)
                    inst = o.fn(eng)
                    if o.dma:
                        inst.then_inc(o.sem, 16)
                    elif o.signal:
                        inst.then_inc(o.sem, 1)
                if ename == "sp":
                    for s, v in final_waits:
                        eng.wait_ge(s, v)

            with nc.Block() as block:

                @block.sync
                def _(e):
                    run_engine("sp", e)

                @block.tensor
                def _(e):
                    run_engine("pe", e)

                @block.scalar
                def _(e):
                    run_engine("act", e)

                @block.vector
                def _(e):
                    run_engine("dve", e)

                @block.gpsimd
                def _(e):
                    run_engine("pool", e)


class Pipe:
    def __init__(self):
        self.active = []

    def tick(self, gen=None):
        if gen is not None:
            self.active.append(gen)
        for g in list(self.active):
            try:
                next(g)
            except StopIteration:
                self.active.remove(g)

    def drain(self):
        while self.active:
            self.tick()


CF_ID, CF_UP, CF_US, CF_ONES, CF_RM, CF_RS = 0, 128, 256, 384, 512, 528
NCF = 528 + 16
CB_ID, CB_UP, CB_LP, CB_US, CB_LS, CB_E, CB_BD = 0, 128, 256, 384, 512, 640, 768
NCB = 768 + 2048


def _consts():
    t = np.arange(128)
    blk = t // LS
    same = (blk[:, None] == blk[None, :]).astype(np.float32)
    U_p = (t[:, None] <= t[None, :]).astype(np.float32)
    L_p = (t[:, None] > t[None, :]).astype(np.float32)
    U_s = U_p * same
    L_s = L_p * same
    cf = np.zeros((128, NCF), np.float32)
    cf[:, CF_ID:CF_ID + 128] = np.eye(128)
    cf[:, CF_UP:CF_UP + 128] = U_p
    cf[:, CF_US:CF_US + 128] = U_s
    cf[:, CF_ONES:CF_ONES + 128] = same
    rm = (blk[:, None] == np.arange(16)[None, :]).astype(np.float32)
    cf[:, CF_RM:CF_RM + 16] = rm
    rs = np.zeros((128, 16), np.float32)
    for b in range(16):
        rs[8 * b, b] = 1.0
    cf[:, CF_RS:CF_RS + 16] = rs
    cb = np.zeros((128, NCB), np.float32)
    cb[:, CB_ID:CB_ID + 128] = np.eye(128)
    cb[:, CB_UP:CB_UP + 128] = U_p
    cb[:, CB_LP:CB_LP + 128] = L_p
    cb[:, CB_US:CB_US + 128] = U_s
    cb[:, CB_LS:CB_LS + 128] = L_s
    E = np.zeros((128, 128), np.float32)
    for s in range(8):
        E[s, :] = (t % 8 == s)
    cb[:, CB_E:CB_E + 128] = E
    bd = np.zeros((128, 16, 128), np.float32)
    for b in range(16):
        bd[:, b, 8 * b:8 * b + 8] = 1.0
    cb[:, CB_BD:CB_BD + 2048] = bd.reshape(128, 2048)
    return cf, cb


def build_nc():
    nc = bass.Bass("TRN2", target_bir_lowering=False)

    def din(name, shape):
        return nc.dram_tensor(name, list(shape), F32, kind="ExternalInput").ap()

    def dout(name, shape):
        return nc.dram_tensor(name, list(shape), F32, kind="ExternalOutput").ap()

    xp = din("xp", [SEQ, D])
    xsm = din("xsm", [128, D])
    mem = din("mem", [256, D])
    ck = din("ck", [16, 256, 512])
    cv = din("cv", [16, 256, 512])
    sst = din("sst", [16, 2048, 128])
    scv = din("scv", [48, 4096])
    pstage = din("pstage", [8, 4096])
    w_in = din("w_in", [D, IN_DIM])
    dtb = din("dt_bias", [1, 32])
    alog = din("a_log", [1, 32])
    dsk = din("d_skip", [1, 32])
    lng = din("ln_v_g", [1, D])
    lnb = din("ln_v_b", [1, D])
    fng = din("final_norm_g", [1, D])
    wsp = din("w_spatial", [8, 128, 128])
    bsp = din("b_spatial", [1, 1024])
    wkv = din("w_mem_kv", [D, D])
    wpa = din("w_proj_a", [D, D])
    wpb = din("w_proj_b", [2048, D])
    wpx = din("w_proj_x", [512, D])
    wout = din("w_out", [D, D])
    cfd = din("cf", [128, NCF])
    cbd = din("cb", [128, NCB])

    wscr = nc.dram_tensor("wscr", [24, 128, 8192], BF16, kind="Internal").ap()
    kvscr = nc.dram_tensor("kvscr", [2, 16, 128, 1024], BF16, kind="Internal").ap()
    yp = dout("yp", [SEQ, D])
    ys = dout("ys", [128, D])
    mk = dout("mk", [256, 512])
    mv = dout("mv", [256, 512])
    hp = dout("hp", [2048, 128])
    cpo = dout("cp", [3, 4096])
    hs = dout("hs", [16, 2048, 128])
    cso = dout("cs", [48, 4096])
    vs = dout("vs", [128, D])

    with ExitStack() as es:
        def sb(name, shape, dt):
            return es.enter_context(nc.sbuf_tensor(name, list(shape), dt))

        psb = [es.enter_context(nc.psum_tensor(f"ps{k}", [128, 512], F32)) for k in range(8)]
        P = Prog(nc, n_dma_sems=16)

        class Banks:
            def __init__(self):
                self.pinned = set()
                self.i = 0

            def get(self, pin=False):
                for _ in range(16):
                    k = self.i
                    self.i = (self.i + 1) % 8
                    if k not in self.pinned:
                        if pin:
                            self.pinned.add(k)
                        return k
                raise RuntimeError("no psum bank")

            def unpin(self, k):
                self.pinned.discard(k)

        BK = Banks()

        def PK(k):
            return ("ps", k)

        def MM(out, lhsT, rhs, start, stop, rd, wr):
            P.add("pe", lambda e: e.matmul(out, lhsT, rhs, start=start, stop=stop), rd, wr)

        def TR(out, in_, ident, rd, wr):
            P.add("pe", lambda e: e.transpose(out, in_, ident), rd, wr)

        def ACT(out, in_, func, rd, wr, **kw):
            P.add("act", lambda e: e.activation(out, in_, func, **kw), rd, wr)

        def TT(out, a, b, op, rd, wr):
            P.add("dve", lambda e: e.tensor_tensor(out, a, b, op), rd, wr)

        def TS(out, a, s1, s2, op0, op1, rd, wr):
            P.add("dve", lambda e: e.tensor_scalar(out, a, s1, s2, op0, op1), rd, wr)

        def TS1(out, a, s, op, rd, wr):
            P.add("dve", lambda e: e.tensor_single_scalar(out, a, s, op), rd, wr)

        def STT(out, in0, scalar, in1, op0, op1, rd, wr):
            P.add("dve", lambda e: e.scalar_tensor_tensor(out, in0, scalar, in1, op0, op1), rd, wr)

        def CP(out, in_, rd, wr):
            P.add("dve", lambda e: e.tensor_copy(out, in_), rd, wr)

        def PTT(out, a, b, op, rd, wr):
            P.add("pool", lambda e: e.tensor_tensor(out, a, b, op), rd, wr)

        def PCP(out, in_, rd, wr):
            P.add("pool", lambda e: e.tensor_copy(out, in_), rd, wr)

        def ACP(out, in_, rd, wr):
            if "acpident" in KVAR:
                P.add("act", lambda e: e.activation(out, in_, AF.Identity), rd, wr)
            else:
                P.add("act", lambda e: e.copy(out, in_), rd, wr)

        def DMA(out, in_, rd, wr, q="sp"):
            P.add(q, lambda e: e.dma_start(out=out, in_=in_), rd, wr, dma=True)

        def MSET(ap, val, wr):
            P.add("dve", lambda e: e.memset(ap, val), (), wr)

        cfs = sb("cfs", [128, NCF], F32)
        cbs = sb("cbs", [128, NCB], BF16)
        ones_f = sb("ones_f", [128, 128], F32)
        ones_b = sb("ones_b", [128, 128], BF16)
        pfm = sb("pfm", [128, 32, 8], F32)
        fg_bc = sb("fg_bc", [128, D], F32)
        lng_bc = sb("lng_bc", [128, D], F32)
        lnb_bc = sb("lnb_bc", [128, D], F32)
        bsp_bc = sb("bsp_bc", [128, 8, 128], F32)
        dtb_bc = sb("dtb_bc", [128, 32], F32)
        a_bc = sb("a_bc", [128, 32], F32)
        dsk_bc = sb("dsk_bc", [128, 32], F32)
        WTp = sb("WTp", [128, 8, 128], BF16)
        WTs = sb("WTs", [128, 8, 128], BF16)
        xhalo = sb("xhalo", [128, 32, 3], F32)
        stats = sb("stats", [128, 256], F32)
        mT = sb("mT", [128, 8, 256], F32)
        mTb = sb("mTb", [128, 8, 256], BF16)
        hnT = sb("hnT", [128, 8, 256], BF16)
        xt = sb("xt", [128, 2, D], F32)
        wbuf = [sb(f"wb{j}", [128, 8192], BF16) for j in range(2)]
        junk = sb("junk", [128, D], BF16)
        xn = sb("xn", [128, D], BF16)
        ARENA = 59648
        arena = sb("arena", [128, ARENA], BF16)

        ident_f = cfs[:, CF_ID:CF_ID + 128]
        ident_b = cbs[:, CB_ID:CB_ID + 128]
        rowmask = cfs[:, CF_RM:CF_RM + 16]

        st_i = [0]

        def stat(n=1):
            if st_i[0] + n > 256:
                st_i[0] = 0
            a = st_i[0]
            st_i[0] += n
            return stats[:, a:a + n], [("st", j) for j in range(a, a + n)]

        class Arena:
            def __init__(self, tag):
                self.off = 0
                self.base = 0
                self.tag = tag

            def take(self, name, shape, dt, reg=False):
                n = int(np.prod(shape))
                nb = n if dt == BF16 else 2 * n
                nb = (nb + 1) // 2 * 2
                if self.off + nb > ARENA:
                    raise RuntimeError(f"arena overflow at {name}: {self.off}+{nb}")
                if reg:
                    P.register(name, self.off, self.off + nb)
                ap = arena[:, self.off:self.off + nb]
                self.off += nb
                if dt == F32:
                    ap = ap.bitcast(F32)
                if len(shape) == 2:
                    v = ap.rearrange("p (a b) -> p a b", a=shape[0])
                elif len(shape) == 3:
                    v = ap.rearrange("p (a b c) -> p a b c", a=shape[0], b=shape[1])
                else:
                    v = ap
                return v

        ARset = Arena("setup")
        pst = ARset.take("pst", [4096], F32)
        wsf = ARset.take("wsf", [8, 128], F32)
        rep = ARset.take("rep", [8, 8], F32)
        DMA(cfs[:], cfd, (), ["cfs"])
        DMA(cbs[:, 0:1408], cbd[:, 0:1408], (), ["cbs"], q="pool")
        DMA(cbs[:, 1408:NCB], cbd[:, 1408:NCB], (), ["cbs"], q="pool")
        DMA(pst[0:8, :], pstage, (), ["pst"])
        DMA(fg_bc[:], fng.partition_broadcast(128), (), ["fg_bc"])
        DMA(lng_bc[:], lng.partition_broadcast(128), (), ["lng_bc"])
        DMA(lnb_bc[:], lnb.partition_broadcast(128), (), ["lnb_bc"])
        DMA(bsp_bc[:].rearrange("p g t -> p (g t)"), bsp.partition_broadcast(128), (), ["bsp_bc"])
        DMA(dtb_bc[:], dtb.partition_broadcast(128), (), ["dtb_bc"])
        DMA(a_bc[:], alog.partition_broadcast(128), (), ["a_bc"])
        DMA(dsk_bc[:], dsk.partition_broadcast(128), (), ["dsk_bc"])
        DMA(wsf, wsp.rearrange("g t s -> t g s"), (), ["wsf"])
        MSET(ones_f[:], 1.0, ["ones_f"])
        MSET(ones_b[:], 1.0, ["ones_b"])
        MSET(xhalo[:], 0.0, ["xhalo"])
        ACT(a_bc[:], a_bc[:], AF.Exp, ["a_bc"], ["a_bc"])
        TS1(a_bc[:], a_bc[:], -1.0, ALU.mult, ["a_bc"], ["a_bc"])
        k = BK.get()
        for c in range(32):
            TR(psb[k][:, c * 8:(c + 1) * 8], pst[0:8, c * 128:(c + 1) * 128], cfs[0:8, CF_ID:CF_ID + 8], ["pst", "cfs"], [PK(k)])
        CP(pfm[:].rearrange("p c k -> p (c k)"), psb[k][:, 0:256], [PK(k)], ["pfm"])
        gfm = pfm[:, 16:24, 5]
        mgfm = pfm[:, 24:32, 5]
        ssdg = pfm[:, 0:16, 5]
        for half in range(2):
            k = BK.get()
            for gg in range(4):
                g = half * 4 + gg
                TR(psb[k][:, gg * 128:(gg + 1) * 128], wsf[:, g, :], ident_f, ["wsf", "cfs"], [PK(k)])
            TT(WTp[:, half * 4:half * 4 + 4, :], psb[k][:].rearrange("p (g t) -> p g t", g=4),
               cfs[:, None, CF_UP:CF_UP + 128].to_broadcast([128, 4, 128]), ALU.mult, [PK(k), "cfs"], ["WTp"])
        k = BK.get()
        MM(psb[k][:, 0:64].rearrange("p (g t) -> p g t", g=8), cbs[0:8, CB_E:CB_E + 128], WTp[0:8, :, 0:8], True, True,
           ["cbs", "WTp"], [PK(k)])
        CP(rep, psb[k][:, 0:64].rearrange("p (g t) -> p g t", g=8), [PK(k)], ["rep"])
        for g in range(8):
            TT(WTs[:, g, :].rearrange("p (b t) -> p b t", b=16), rep[:, g, None, :].to_broadcast([128, 16, 8]),
               rowmask[:, :, None].to_broadcast([128, 16, 8]), ALU.mult, ["rep", "cfs"], ["WTs"])
        P.barrier()
        if STOP <= 1:
            P.emit()
            return nc

        wcnt = [0]

        cached = set()

        def load_w(src, kc, ncols, blk=None):
            j = wcnt[0] % 2
            wcnt[0] += 1
            n = kc * ncols
            flat = wbuf[j][:, 0:n]
            view = flat.rearrange("p (kc n) -> p kc n", kc=kc)
            if blk is None:
                DMA(view, src.rearrange("(kc p) n -> p kc n", p=128), (), [("wb", j)], q="pool")
            elif blk not in cached:
                DMA(view, src.rearrange("(kc p) n -> p kc n", p=128), (), [("wb", j)], q="pool")
                DMA(wscr[blk][:, 0:n], flat, [("wb", j)], [("scr", blk)], q="sp")
                cached.add(blk)
            else:
                DMA(flat, wscr[blk][:, 0:n], [("scr", blk)], [("wb", j)], q="sp")
            return view, ("wb", j)

        def rstd_from_ss(ss_ap, ss_keys, n, out_ap, out_keys):
            ACT(out_ap, ss_ap, AF.Ln, ss_keys, out_keys, scale=1.0 / n, bias=EPS)
            ACT(out_ap, out_ap, AF.Exp, out_keys, out_keys, scale=-0.5)

        def norm_transpose(src_tile, xbuf, xkey, gsc, gkeys, dstT, dkeys, q="sp"):
            DMA(xbuf, src_tile, (), [xkey], q=q)
            ss, ssk = stat()
            ACT(junk[:], xbuf, AF.Square, [xkey], ["junk"] + ssk, accum_out=ss)
            r, rk = stat()
            rstd_from_ss(ss, ssk, D, r, rk)
            ACT(xn[:], xbuf, AF.Identity, [xkey] + rk, ["xn"], scale=r)
            k = BK.get()
            pv = psb[k][:].bitcast(BF16)
            for kc in range(8):
                TR(pv[:, kc * 128:(kc + 1) * 128], xn[:, kc * 128:(kc + 1) * 128], ident_b, ["xn", "cbs"], [PK(k)])
            TT(dstT, pv.rearrange("p (kc t) -> p kc t", kc=8), gsc[:, :, None].to_broadcast([128, 8, 128]), ALU.mult,
               [PK(k)] + gkeys, dkeys)

        def fm_to_rows(src_fn, nrows, out_dram, tagkey, rkeys, rows):
            for piece in range(4):
                for q4 in range(2):
                    k = BK.get()
                    for cc in range(4):
                        c = piece * 8 + q4 * 4 + cc
                        TR(psb[k][0:nrows, cc * 128:(cc + 1) * 128], src_fn(c), ident_f, rkeys + ["cfs"], [PK(k)])
                    CP(rows[0:nrows, q4 * 512:(q4 + 1) * 512], psb[k][0:nrows, :], [PK(k)], [("rows", tagkey)])
                DMA(out_dram[:, piece * 1024:(piece + 1) * 1024], rows[0:nrows, :], [("rows", tagkey)], [("rows", tagkey)])

        def run_st(kind, tok0, nT, AR, hT, hTb, KT, Vt, AR_s):
            T = 128 * nT
            samp = kind == "s"
            xsrc = xsm if samp else xp
            ydst = ys if samp else yp
            Umask_b = cbs[:, CB_US:CB_US + 128] if samp else cbs[:, CB_UP:CB_UP + 128]
            Lmask_b = cbs[:, CB_LS:CB_LS + 128] if samp else cbs[:, CB_LP:CB_LP + 128]
            U_f = cfs[:, CF_US:CF_US + 128] if samp else cfs[:, CF_UP:CF_UP + 128]
            tot_f = cfs[:, CF_ONES:CF_ONES + 128] if samp else ones_f[:]
            WT = WTs if samp else WTp
            bspx = AR_s["bsp_s"] if samp else bsp_bc
            hnk = [("hnT", i) for i in range(nT)]

            def tile_rows(ap, i):
                return ap[tok0 + i * 128: tok0 + (i + 1) * 128, :]

            for i in range(nT):
                norm_transpose(tile_rows(xsrc, i), xt[:, i % 2, :], ("xt", i % 2), gfm, ["pfm"],
                               hnT[:, :, i * 128:(i + 1) * 128], [("hnT", i)])

            def fm_block(wv, wk, nch, c_base, consumer):
                for c in range(nch):
                    k = BK.get()
                    for kc in range(8):
                        MM(psb[k][:, 0:T], wv[:, kc, c * 128:(c + 1) * 128], hnT[:, kc, 0:T], kc == 0, kc == 7,
                           [wk] + hnk, [PK(k)])
                    consumer(c_base + c, psb[k][:, 0:T], k)

            def tm_block(wv, wk, ncols, col_base, consumer):
                for i in range(nT):
                    for s0 in range(0, ncols, 512):
                        n = min(512, ncols - s0)
                        k = BK.get()
                        for kc in range(8):
                            MM(psb[k][:, 0:n], hnT[:, kc, i * 128:(i + 1) * 128], wv[:, kc, s0:s0 + n], kc == 0, kc == 7,
                               [wk, ("hnT", i)], [PK(k)])
                        consumer(i, col_base + s0, n, psb[k][:, 0:n], k)

            AR.off = AR.base
            gate = AR.take("gate", [8, T], BF16, reg=True)
            F0 = AR.take("F0", [2, 1024], F32, reg=True)
            tmpA = AR.take("tmpA", [4, 128], F32, reg=True)
            mark = AR.off
            G0 = AR.take("G0", [8, T], BF16, reg=True)
            G1 = AR.take("G1", [8, T], BF16, reg=True)
            G2 = AR.take("vbf", [nT, 1024], BF16, reg=True)
            G3 = AR.take("G3", [8, T], BF16, reg=True)
            hxT = AR.take("hxT", [4, T], BF16, reg=True)
            PT = AR.take("PT", [4, T], BF16, reg=True)

            def merge(first, last, c, ps_ap, k):
                if first:
                    STT(mT[:, c, 0:T], gate[:, c, :], 1.0, ps_ap, ALU.add, ALU.mult, [PK(k), ("gate", c)], [("mT", c)])
                else:
                    tmpm = F0[:, c % 2, 0:T]
                    STT(tmpm, gate[:, c, :], 1.0, ps_ap, ALU.add, ALU.mult, [PK(k), ("gate", c)], [("F0", c % 2)])
                    if last:
                        TT(mTb[:, c, 0:T], mT[:, c, 0:T], tmpm, ALU.add, [("F0", c % 2), ("mT", c)], [("mTb", c)])
                    else:
                        TT(mT[:, c, 0:T], mT[:, c, 0:T], tmpm, ALU.add, [("F0", c % 2), ("mT", c)], [("mT", c)])

            def gate_consumer(c, ps_ap, k):
                ACT(gate[:, c % 8, :], ps_ap, AF.Tanh, [PK(k)], [("gate", c % 8)], scale=0.5)

            sched = []

            vbf = G2
            acc_v = {}

            def a4_tile(i, kk):
                for half in range(2):
                    kb = kk[half]
                    sl = slice(half * 4, half * 4 + 4)
                    cols = slice(i * 128, (i + 1) * 128)
                    TT(tmpA, psb[kb][:].rearrange("p (g t) -> p g t", g=4), bspx[:, sl, :], ALU.add, [PK(kb), "bsp_bc", "bsp_s"], ["tmpA"])
                    TT(tmpA, tmpA, G0[:, sl, cols], ALU.mult, ["tmpA"] + [("G0", c) for c in range(half * 4, half * 4 + 4)], ["tmpA"])
                    TT(G3[:, sl, cols], tmpA, G1[:, sl, cols], ALU.mult, ["tmpA"] + [("G1", c) for c in range(half * 4, half * 4 + 4)],
                       [("G3", i)])

            var_all, var_keys = stat(nT)
            mean_t = {}

            def v_cons(i, col, n, ps_ap, k):
                sub = col // 512
                fk = [("F0", i % 2)]
                sa, sak = stat()
                ACT(F0[:, i % 2, col:col + n], ps_ap, AF.Gelu, [PK(k)], fk + sak, accum_out=sa)
                acc_v[(i, sub)] = (sa, sak)
                if sub == 1:
                    vt = F0[:, i % 2, :]
                    (s0, s0k), (s1, s1k) = acc_v[(i, 0)], acc_v[(i, 1)]
                    sq, sqk = stat()
                    ACT(junk[:], vt, AF.Square, fk, ["junk"] + sqk, accum_out=sq)
                    mean, mk_ = stat()
                    TS(mean, s0, s1, 1.0 / D, ALU.add, ALU.mult, s0k + s1k, mk_)
                    msq, msk = stat()
                    TT(msq, mean, mean, ALU.mult, mk_, msk)
                    TS(var_all[:, i:i + 1], sq, 1.0 / D, msq, ALU.mult, ALU.subtract, sqk + msk, [var_keys[i]])
                    mean_t[i] = (mean, mk_)
                    if i == nT - 1:
                        ACT(var_all, var_all, AF.Ln, var_keys, var_keys, bias=EPS)
                        ACT(var_all, var_all, AF.Exp, var_keys, var_keys, scale=-0.5)
                        for ii in range(nT):
                            v_norm(ii)

            def v_norm(i):
                fk = [("F0", i % 2)]
                vt = F0[:, i % 2, :]
                mean, mk_ = mean_t[i]
                TS(vt, vt, mean, var_all[:, i:i + 1], ALU.subtract, ALU.mult, fk + mk_ + var_keys, fk)
                TT(vt, vt, lng_bc[:], ALU.mult, fk + ["lng_bc"], fk)
                if samp:
                    TT(vt, vt, lnb_bc[:], ALU.add, fk + ["lnb_bc"], fk)
                    DMA(vs, vt, fk, [])
                    ACP(vbf[:, i, :], vt, fk, [("vbf", i)])
                else:
                    TT(vbf[:, i, :], vt, lnb_bc[:], ALU.add, fk + ["lnb_bc"], [("vbf", i)])

            def spatial_tile(i):
                kk = [BK.get(), BK.get()]
                for g in range(8):
                    kb = kk[g // 4]
                    MM(psb[kb][:, (g % 4) * 128:(g % 4 + 1) * 128], vbf[:, i, g * 128:(g + 1) * 128], WT[:, g, :], True, True,
                       [("vbf", i), "WTp", "WTs"], [PK(kb)])
                a4_tile(i, kk)

            def u_cons(c, ps_ap, k):
                ACT(G0[:, c, :], ps_ap, AF.Gelu, [PK(k)], [("G0", c)])

            def ga_cons(c, ps_ap, k):
                ACT(G1[:, c, :], ps_ap, AF.Silu, [PK(k)], [("G1", c)])

            def proj_generic(hTsrc, hkeys, KC, first, last):
                def f(wv, wk, c_lo, c_hi, col0):
                    for c in range(c_lo, c_hi):
                        k = BK.get()
                        for kc in range(KC):
                            MM(psb[k][:, 0:T], wv[:, kc, (c * 128 - col0):(c * 128 - col0) + 128], hTsrc[:, kc, 0:T], kc == 0, kc == KC - 1,
                               [wk] + hkeys, [PK(k)])
                        merge(first, last, c, psb[k][:, 0:T], k)
                return f

            sched.append((w_in[:, C_U:C_U + 1024], 8, 1024, lambda wv, wk: fm_block(wv, wk, 8, 0, u_cons)))
            sched.append((w_in[:, C_V:C_V + 1024], 8, 1024, lambda wv, wk: tm_block(wv, wk, 1024, 0, v_cons)))
            sched.append((w_in[:, C_GA:C_GA + 1024], 8, 1024, lambda wv, wk: fm_block(wv, wk, 8, 0, ga_cons)))
            def gate_a_consumer(c, ps_ap, k):
                gate_consumer(c, ps_ap, k)
                if nT == 2 and c in (3, 7):
                    spatial_tile(c // 4)
                elif nT == 1 and c == 7:
                    spatial_tile(0)

            sched.append((w_in[:, C_MG:C_MG + 1024], 8, 1024, lambda wv, wk: fm_block(wv, wk, 8, 0, gate_a_consumer)))
            pa = proj_generic(G3, [("G3", i) for i in range(nT)], 8, True, False)

            def proj_a_block(wv, wk):
                pa(wv, wk, 0, 8, 0)

            sched.append((wpa, 8, 1024, proj_a_block))

            qT = G0[:, 0:4, :]
            gx = G0[:, 4:8, :]
            rden = F0[:, 0, 0:T]
            tmpx = F0[:, 1, 0:T]
            SC = 128.0 ** -0.5

            def attn_head(h):
                par = h % 2
                ks_ = []
                for mt in range(2):
                    k = BK.get()
                    MM(psb[k][:, 0:T], KT[:, h, mt * 128:(mt + 1) * 128], qT[:, h, :], True, True, ["KT", ("G0", h)], [PK(k)])
                    ACT(PT[:, par * 2 + mt, :], psb[k][:, 0:T], AF.Exp, [PK(k)], [("PT", par, mt)], scale=SC)
                yield
                ko = BK.get()
                for mt in range(2):
                    MM(psb[ko][:, 0:T], Vt[:, mt, h * 128:(h + 1) * 128], PT[:, par * 2 + mt, :], mt == 0, mt == 1,
                       ["Vt", ("PT", par, mt)], [PK(ko)])
                kd = BK.get()
                for mt in range(2):
                    MM(psb[kd][:, 0:T], ones_b[:], PT[:, par * 2 + mt, :], mt == 0, mt == 1, ["ones_b", ("PT", par, mt)], [PK(kd)])
                P.add("dve", lambda e, kd=kd: e.reciprocal(rden, psb[kd][:, 0:T]), [PK(kd)], [("F0", 0)])
                TT(tmpx, psb[ko][:, 0:T], rden, ALU.mult, [PK(ko), ("F0", 0)], [("F0", 1)])
                TT(hxT[:, h, :], tmpx, gx[:, h, :], ALU.mult, [("F0", 1), ("G0", 4 + h)], [("hxT", h)])

            attpipe = Pipe()

            def attn_prompt():
                pass

            def attn_sample():
                ko = BK.get(pin=True)
                kd = BK.get(pin=True)
                KTs, Kc, Vc, PTs, rd4 = AR_s["KTs"], AR_s["Kc"], AR_s["Vc"], AR_s["PTs"], AR_s["rd4"]
                qk = [("G0", c) for c in range(4)]

                def chain(b):
                    j = b % 2
                    DMA(Kc[:, j], kvscr[0, b].rearrange("p (mt n) -> p mt n", mt=2), [("kvs", 0, b)], [("Kc", j)])
                    DMA(Vc[:, j], kvscr[1, b].rearrange("p (mt n) -> p mt n", mt=2), [("kvs", 1, b)], [("Vc", j)])
                    k = BK.get(pin=True)
                    pv = psb[k][:].bitcast(BF16)
                    for h in range(4):
                        for mt in range(2):
                            TR(pv[:, h * 256 + mt * 128: h * 256 + (mt + 1) * 128], Kc[:, j, mt, h * 128:(h + 1) * 128], ident_b,
                               [("Kc", j), "cbs"], [PK(k)])
                    yield
                    CP(KTs[:, j], pv.rearrange("p (h m) -> p h m", h=4), [PK(k)], [("KTs", j)])
                    BK.unpin(k)
                    yield
                    k2 = BK.get()
                    for h in range(4):
                        for mt in range(2):
                            MM(psb[k2][:, (h * 2 + mt) * 8:(h * 2 + mt + 1) * 8], KTs[:, j, h, mt * 128:(mt + 1) * 128],
                               qT[:, h, b * 8:(b + 1) * 8], True, True, [("KTs", j)] + qk, [PK(k2)])
                    ACT(PTs[:, j], psb[k2][:, 0:64], AF.Exp, [PK(k2)], [("PTs", j)], scale=SC)
                    for h in range(4):
                        for mt in range(2):
                            MM(psb[ko][:, h * 128 + b * 8: h * 128 + (b + 1) * 8], Vc[:, j, mt, h * 128:(h + 1) * 128],
                               PTs[:, j, (h * 2 + mt) * 8:(h * 2 + mt + 1) * 8], mt == 0, mt == 1, [("Vc", j), ("PTs", j)], [PK(ko)])
                    for h in range(4):
                        for mt in range(2):
                            MM(psb[kd][:, h * 128 + b * 8: h * 128 + (b + 1) * 8], ones_b[:],
                               PTs[:, j, (h * 2 + mt) * 8:(h * 2 + mt + 1) * 8], mt == 0, mt == 1, ["ones_b", ("PTs", j)], [PK(kd)])

                pp = Pipe()
                for b in range(NSEQ_S):
                    pp.tick(chain(b))
                pp.drain()
                P.add("dve", lambda e: e.reciprocal(rd4, psb[kd][:]), [PK(kd)], ["rd4"])
                TT(rd4, psb[ko][:], rd4, ALU.mult, [PK(ko), "rd4"], ["rd4"])
                TT(hxT.rearrange("p h t -> p (h t)"), rd4, gx.rearrange("p h t -> p (h t)"), ALU.mult,
                   ["rd4"] + [("G0", 4 + h) for h in range(4)], [("hxT", h) for h in range(4)])
                BK.unpin(ko)
                BK.unpin(kd)

            attn = attn_sample if samp else attn_prompt

            def qgx_cons(c, ps_ap, k):
                if c < 4:
                    CP(qT[:, c, :], ps_ap, [PK(k)], [("G0", c)])
                else:
                    ACT(gx[:, c - 4, :], ps_ap, AF.Silu, [PK(k)], [("G0", c)])
                    if c == 7:
                        attn()

            sched.append((w_in[:, C_Q:C_Q + 1024], 8, 1024, lambda wv, wk: fm_block(wv, wk, 8, 0, qgx_cons)))
            def gate_x_consumer(c, ps_ap, k):
                gate_consumer(c, ps_ap, k)
                if not samp:
                    if c % 2 == 0:
                        attpipe.tick(attn_head(c // 2))
                    if c == 7:
                        attpipe.drain()

            sched.append((w_in[:, C_MG + 2048:C_MG + 3072], 8, 1024, lambda wv, wk: fm_block(wv, wk, 8, 0, gate_x_consumer)))
            px = proj_generic(hxT, [("hxT", h) for h in range(4)], 4, False, False)
            sched.append((wpx, 4, 1024, lambda wv, wk: px(wv, wk, 0, 8, 0)))

            AR.off = mark
            zs = AR.take("zs", [nT, 2048], BF16, reg=True)
            xs_tm = AR.take("xs_tm", [nT, 2048], BF16, reg=True)
            BT = AR.take("BT", [8, T], BF16, reg=True)
            CT = AR.take("CT", [8, T], BF16, reg=True)
            B_tm = AR.take("B_tm", [nT, 1024], BF16, reg=True)
            hbT = AR.take("hbT", [16, T], BF16, reg=True)
            dtp = AR.take("dtp", [nT, 6, 32], F32, reg=True)
            XR = max(3 + T, 16 * 11)
            NBC, NBS = 4, 3
            xraw = AR.take("xraw", [NBC, XR], F32, reg=True)
            acc = AR.take("acc", [NBC, T], F32, reg=True)
            xsT = AR.take("xsT", [NBC, T], BF16, reg=True)
            x_dt = AR.take("x_dt", [2048], BF16, reg=True)
            xsD = AR.take("xsD", [2048], BF16, reg=True)
            xdd = AR.take("xdd", [2048], BF16, reg=True)
            cbm = AR.take("cbm", [2, 8, 128], BF16, reg=True)
            Rb = AR.take("R", [NBS, 512], BF16, reg=True)
            Eb = AR.take("E", [NBS, 512], BF16, reg=True)
            MTb_ = AR.take("MT", [NBS, 512], BF16, reg=True)
            yt = AR.take("yt", [NBS, 256], F32, reg=True)
            yn = AR.take("yn", [NBS, 256], BF16, reg=True)
            acs = AR.take("acs", [64], F32, reg=True)
            run_st.maxoff = max(getattr(run_st, "maxoff", 0), AR.off)
            if os.environ.get("KDEBUG") and (samp or tok0 == 0):
                print("ST", kind, "arena base", AR.base, "end", AR.off, "of", ARENA)

            def dt_cons(i, col, n, ps_ap, k):
                dt_, adt, ea, cd, ds, tmp = (dtp[:, i, j, :] for j in range(6))
                dk = lambda j: [("dtp", i, j)]
                TT(tmp, ps_ap, dtb_bc[:], ALU.add, [PK(k), "dtb_bc"], dk(5))
                ACT(tmp, tmp, AF.Exp, dk(5), dk(5))
                ACT(dt_, tmp, AF.Ln, dk(5), dk(0), bias=1.0)
                TT(adt, dt_, a_bc[:], ALU.mult, dk(0) + ["a_bc"], dk(1))
                k2 = BK.get()
                MM(psb[k2][:, 0:32], U_f, adt, True, True, ["cfs"] + dk(1), [PK(k2)])
                MM(psb[k2][:, 32:64], tot_f, adt, True, True, ["cfs", "ones_f"] + dk(1), [PK(k2)])
                CP(acs, psb[k2][:, 0:64], [PK(k2)], ["acs"])
                ACT(ea, acs[:, 0:32], AF.Exp, ["acs"], dk(2))
                ACT(cd, acs[:, 32:64], AF.Exp, ["acs"], dk(3))
                TT(tmp, acs[:, 32:64], acs[:, 0:32], ALU.subtract, ["acs"], dk(5))
                ACT(ds, tmp, AF.Exp, dk(5), dk(4))

            def z_cons_block(blk):
                def f(i, col, n, ps_ap, k):
                    ACT(zs[:, i, blk * 1024 + col: blk * 1024 + col + n], ps_ap, AF.Silu, [PK(k)], [("zs", i, blk)])
                return f

            S_ = NSEQ_S if samp else 1
            Lw = LS if samp else T
            Lx = Lw + 3

            convpipe = Pipe()
            ssdpipe = Pipe()

            def conv_chain(c, ps_ap, k):
                j = c % NBC
                xr = xraw[:, j, 0:S_ * Lx].rearrange("p (s l) -> p s l", s=S_)
                xk = ("xraw", j)
                if samp:
                    kst = BK.get()
                    cl = c % 8
                    TR(psb[kst][:, 0:48], AR_s["sc_tm"][0:48, cl * 128:(cl + 1) * 128], cfs[0:48, CF_ID:CF_ID + 48], ["sc_tm", "cfs"], [PK(kst)])
                    CP(xr[:, :, 0:3], psb[kst][:, 0:48].rearrange("p (b k) -> p b k", b=16), [PK(kst)], [xk])
                    ACP(xr[:, :, 3:Lx], ps_ap.rearrange("p (b l) -> p b l", b=16), [PK(k)], [xk])
                    CP(AR_s["xlast"][:, c], xr[:, :, Lw:Lx], [xk], [("xlast", c)])
                else:
                    PCP(xr[:, 0, 0:3], xhalo[:, c, :], [("xhalo", c)], [xk])
                    ACP(xr[:, 0, 3:Lx], ps_ap, [PK(k)], [xk])
                    PCP(xhalo[:, c, :], xr[:, 0, Lw:Lx], [xk], [("xhalo", c)])
                av = acc[:, j, 0:S_ * Lw].rearrange("p (s l) -> p s l", s=S_)
                ak = ("acc", j)
                ACT(av, xr[:, :, 3:3 + Lw], AF.Identity, [xk, "pfm"], [ak], scale=pfm[:, c, 3:4], bias=pfm[:, c, 4:5])
                yield
                for kk in (2, 1, 0):
                    STT(av, xr[:, :, kk:kk + Lw], pfm[:, c, kk:kk + 1], av, ALU.mult, ALU.add, [xk, ak, "pfm"], [ak])
                yield
                avf = acc[:, j, 0:T]
                if c < 24:
                    if c < 16:
                        dst = xsT[:, j, :]
                        dkey = ("xsT", j)
                    else:
                        dst = BT[:, c - 16, :]
                        dkey = ("BT", c - 16)
                    ACT(dst, avf, AF.Silu, [ak], [dkey])
                    kt = BK.get(pin=True)
                    pv = psb[kt][:].bitcast(BF16)
                    for i in range(nT):
                        TR(pv[:, i * 128:(i + 1) * 128], dst[:, i * 128:(i + 1) * 128], ident_b, [dkey, "cbs"], [PK(kt)])
                    yield
                    if c < 16:
                        ACP(xs_tm[:, :, c * 128:(c + 1) * 128], pv[:, 0:T].rearrange("p (i t) -> p i t", i=nT), [PK(kt)], [("xs_tm", c)])
                    else:
                        g = c - 16
                        ACP(B_tm[:, :, g * 128:(g + 1) * 128], pv[:, 0:T].rearrange("p (i t) -> p i t", i=nT), [PK(kt)], [("B_tm", g)])
                    BK.unpin(kt)
                else:
                    ACT(CT[:, c - 24, :], avf, AF.Silu, [ak], [("CT", c - 24)])

            chain_no = [0]

            def ssd_pre(i):
                cols = slice(i * 128, (i + 1) * 128)
                for half in range(2):
                    kcb = BK.get()
                    for gg in range(4):
                        g = half * 4 + gg
                        MM(psb[kcb][:, gg * 128:(gg + 1) * 128], BT[:, g, cols], CT[:, g, cols], True, True, [("BT", g), ("CT", g)], [PK(kcb)])
                    TT(cbm[:, i % 2, half * 4:half * 4 + 4, :], psb[kcb][:].rearrange("p (g l) -> p g l", g=4),
                       Umask_b[:, None, :].to_broadcast([128, 4, 128]), ALU.mult, [PK(kcb), "cbs"], [("cbm", i % 2, half)])

            def ssd_chain(i, g):
                j = chain_no[0] % NBS
                chain_no[0] += 1
                dt_, adt, ea, cd, ds, tmp = (dtp[:, i, q, :] for q in range(6))
                dk = lambda q: [("dtp", i, q)]
                cols = slice(i * 128, (i + 1) * 128)
                hs_ = slice(g * 256, (g + 1) * 256)
                h4 = slice(4 * g, 4 * g + 4)
                xkg = [("xs_tm", 2 * g), ("xs_tm", 2 * g + 1)]
                xs3 = xs_tm[:, i, hs_].rearrange("p (h q) -> p h q", h=4)
                v3 = lambda ap: ap[:, hs_].rearrange("p (h q) -> p h q", h=4)
                if not samp:
                    PTT(v3(x_dt), xs3, dt_[:, h4, None].to_broadcast([128, 4, 64]), ALU.mult, xkg + dk(0), [("x_dt", g)])
                    PTT(v3(xdd), v3(x_dt), ds[:, h4, None].to_broadcast([128, 4, 64]), ALU.mult, [("x_dt", g)] + dk(4), [("xdd", g)])
                PTT(v3(xsD), xs3, dsk_bc[:, h4, None].to_broadcast([128, 4, 64]), ALU.mult, xkg + ["dsk_bc"], [("xsD", g)])
                TT(Rb[:, j, :].rearrange("p (h l) -> p h l", h=4), adt[:, h4, None].to_broadcast([128, 4, 128]),
                   Umask_b[:, None, :].to_broadcast([128, 4, 128]), ALU.mult, dk(1) + ["cbs"], [("R", j)])
                yield
                kD = BK.get()
                MM(psb[kD][:], Lmask_b, Rb[:, j, :], True, True, ["cbs", ("R", j)], [PK(kD)])
                ACT(Eb[:, j, :], psb[kD][:], AF.Exp, [PK(kD)], [("E", j)])
                yield
                TT(MTb_[:, j, :].rearrange("p (h l) -> p h l", h=4), Eb[:, j, :].rearrange("p (h l) -> p h l", h=4),
                   cbm[:, i % 2, g, None, :].to_broadcast([128, 4, 128]), ALU.mult, [("E", j), ("cbm", i % 2, g // 4)], [("MT", j)])
                yield
                ky = BK.get(pin=True)
                MM(psb[ky][:, 0:256], ident_b, xsD[:, hs_], True, False, ["cbs", ("xsD", g)], [PK(ky)])
                for hh in range(4):
                    MM(psb[ky][:, hh * 64:(hh + 1) * 64], MTb_[:, j, hh * 128:(hh + 1) * 128], x_dt[:, g * 256 + hh * 64: g * 256 + (hh + 1) * 64],
                       False, hh == 3, [("MT", j), ("x_dt", g)], [PK(ky)])
                if not samp:
                    MM(psb[ky][:, 256:512], CT[:, g, cols], hTb[:, hs_], True, True, [("CT", g), ("hTb", g)], [PK(ky)])
                yield
                yk = ("yt", j)
                if samp:
                    TT(yt[:, j, :], psb[ky][:, 0:256], AR_s["yoff"][:, hs_], ALU.add, [PK(ky), "yoff"], [yk])
                else:
                    y3 = yt[:, j, :].rearrange("p (h q) -> p h q", h=4)
                    TT(y3, psb[ky][:, 256:512].rearrange("p (h q) -> p h q", h=4), ea[:, h4, None].to_broadcast([128, 4, 64]),
                       ALU.mult, [PK(ky)] + dk(2), [yk])
                    TT(yt[:, j, :], yt[:, j, :], psb[ky][:, 0:256], ALU.add, [yk, PK(ky)], [yk])
                TT(yt[:, j, :], yt[:, j, :], zs[:, i, hs_], ALU.mult, [yk, ("zs", i, g // 4)], [yk])
                BK.unpin(ky)
                yield
                ss, ssk = stat()
                ACT(junk[:, 0:256], yt[:, j, :], AF.Square, [yk], ["junk"] + ssk, accum_out=ss)
                r, rk = stat()
                rstd_from_ss(ss, ssk, 256, r, rk)
                ACT(yn[:, j, :], yt[:, j, :], AF.Identity, [yk] + rk, [("yn", j)], scale=r)
                yield
                kt = BK.get(pin=True)
                pv = psb[kt][:].bitcast(BF16)
                for q in range(2):
                    TR(pv[:, q * 128:(q + 1) * 128], yn[:, j, q * 128:(q + 1) * 128], ident_b, [("yn", j), "cbs"], [PK(kt)])
                if not samp:
                    ks = BK.get(pin=True)
                    MM(psb[ks][:, 0:256], B_tm[:, i, g * 128:(g + 1) * 128], xdd[:, hs_], True, True, [("B_tm", g), ("xdd", g)], [PK(ks)])
                yield
                TT(hbT[:, 2 * g:2 * g + 2, cols], pv[:, 0:256].rearrange("p (q t) -> p q t", q=2),
                   ssdg[:, 2 * g:2 * g + 2, None].to_broadcast([128, 2, 128]), ALU.mult, [PK(kt), "pfm"], [("hbT", i)])
                if not samp:
                    h3 = hT[:, hs_].rearrange("p (h q) -> p h q", h=4)
                    TT(h3, h3, cd[:, h4, None].to_broadcast([128, 4, 64]), ALU.mult, [("hT", g)] + dk(3), [("hT", g)])
                    TT(hT[:, hs_], hT[:, hs_], psb[ks][:, 0:256], ALU.add, [("hT", g), PK(ks)], [("hT", g)])
                    ACP(hTb[:, hs_], hT[:, hs_], [("hT", g)], [("hTb", g)])
                    BK.unpin(ks)
                BK.unpin(kt)

            def sample_states():
                h0n, h0T, Bm, cdn, cdx, CTm, yoff = (AR_s[n] for n in ("h0n", "h0T", "Bm", "cdn", "cdx", "CTm", "yoff"))
                cd = dtp[:, 0, 3, :]
                ea = dtp[:, 0, 2, :]
                ds = dtp[:, 0, 4, :]
                dt_ = dtp[:, 0, 0, :]
                sel = cfs[:, CF_RS:CF_RS + 16]
                kc_ = BK.get(pin=True)
                for jj in range(16):
                    CP(cdx[:, jj % 2], cd[:, 2 * jj:2 * jj + 2, None].to_broadcast([128, 2, 64]), [("dtp", 0, 3)], [("cdx", jj % 2)])
                    MM(psb[kc_][:, jj * 16:(jj + 1) * 16], cdx[:, jj % 2].rearrange("p a b -> p (a b)"), sel, True, True,
                       [("cdx", jj % 2), "cfs"], [PK(kc_)])
                CP(cdn, psb[kc_][:, 0:256], [PK(kc_)], ["cdn"])
                BK.unpin(kc_)
                cdn3 = cdn.rearrange("p (j b) -> p j b", j=16)
                xs3 = xs_tm[:, 0, :].rearrange("p (h q) -> p h q", h=32)
                xk = [("xs_tm", c) for c in range(16)]
                TT(x_dt.rearrange("p (h q) -> p h q", h=32), xs3, dt_[:, :, None].to_broadcast([128, 32, 64]), ALU.mult, xk + [("dtp", 0, 0)], [("x_dt", g) for g in range(8)])
                TT(xdd.rearrange("p (h q) -> p h q", h=32), x_dt.rearrange("p (h q) -> p h q", h=32),
                   ds[:, :, None].to_broadcast([128, 32, 64]), ALU.mult, [("x_dt", g) for g in range(8)] + [("dtp", 0, 4)], [("xdd", g) for g in range(8)])
                kyo = [BK.get(pin=True) for _ in range(4)]

                def schain(b):
                    j = b % 2
                    hb_ = h0n[j]
                    hkq = lambda q4: ("h0n%d" % j, q4)
                    src3 = sst[b].rearrange("(j p) n -> p j n", p=128)
                    dst3 = hs[b].rearrange("(j p) n -> p j n", p=128)
                    for q4 in range(4):
                        DMA(hb_[:, q4 * 4:(q4 + 1) * 4, :], src3[:, q4 * 4:(q4 + 1) * 4, :], (), [hkq(q4)])
                        k = BK.get()
                        for cc in range(4):
                            jj = q4 * 4 + cc
                            TR(psb[k][:, cc * 128:(cc + 1) * 128], hb_[:, jj, :], ident_f, [hkq(q4), "cfs"], [PK(k)])
                        ACP(h0T[:, q4 * 512:(q4 + 1) * 512], psb[k][:], [PK(k)], [("h0T", q4)])
                    TT(CTm, CT[:, :, 0:128], cbs[:, None, CB_BD + b * 128: CB_BD + (b + 1) * 128].to_broadcast([128, 8, 128]), ALU.mult,
                       [("CT", g) for g in range(8)] + ["cbs"], ["CTm"])
                    TT(Bm[:, j], B_tm[:, 0, :], rowmask[:, b:b + 1].to_broadcast([128, 1024]), ALU.mult,
                       [("B_tm", g) for g in range(8)] + ["cfs"], [("Bm", j)])
                    yield
                    for g in range(8):
                        kq = kyo[g // 2]
                        MM(psb[kq][:, (g % 2) * 256:(g % 2 + 1) * 256], CTm[:, g, :], h0T[:, g * 256:(g + 1) * 256],
                           b == 0 and g % 2 == 0, b == NSEQ_S - 1 and g % 2 == 1, ["CTm", ("h0T", g // 2)], [PK(kq)])
                    yield
                    for q4 in range(4):
                        k = BK.get()
                        for cc in range(4):
                            jj = q4 * 4 + cc
                            g = jj // 2
                            MM(psb[k][:, cc * 128:(cc + 1) * 128], xdd[:, jj * 128:(jj + 1) * 128], Bm[:, j, g * 128:(g + 1) * 128], True, True,
                               [("xdd", g), ("Bm", j)], [PK(k)])
                        hv = hb_[:, q4 * 4:(q4 + 1) * 4, :]
                        (PTT if q4 % 2 == 0 else TT)(hv, hv, cdn3[:, q4 * 4:(q4 + 1) * 4, b:b + 1].to_broadcast([128, 4, 128]), ALU.mult,
                                                      [hkq(q4), "cdn"], [hkq(q4)])
                        TT(hv, hv, psb[k][:].rearrange("p (c n) -> p c n", c=4), ALU.add, [hkq(q4), PK(k)], [hkq(q4)])
                        DMA(dst3[:, q4 * 4:(q4 + 1) * 4, :], hv, [hkq(q4)], [hkq(q4)])

                sp_ = Pipe()
                for b in range(NSEQ_S):
                    sp_.tick(schain(b))
                sp_.drain()
                for q in range(4):
                    TT(yoff[:, q * 512:(q + 1) * 512].rearrange("p (h q) -> p h q", h=8), psb[kyo[q]][:].rearrange("p (h q) -> p h q", h=8),
                       ea[:, 8 * q:8 * q + 8, None].to_broadcast([128, 8, 64]), ALU.mult, [PK(kyo[q]), ("dtp", 0, 2)], ["yoff"])
                    BK.unpin(kyo[q])

            def c_last_cons(c, ps_ap, k):
                convpipe.tick(conv_chain(c, ps_ap, k))
                if c == 31:
                    convpipe.drain()
                    if samp:
                        sample_states()
                    for i in range(nT):
                        ssd_pre(i)
                        for g in range(8):
                            ssdpipe.tick(ssd_chain(i, g))
                    ssdpipe.drain()

            def first_b(wv, wk):
                tm_block(wv, wk, 32, 0, dt_cons)

            def xbc_block(blk):
                def f(wv, wk):
                    if samp:
                        DMA(AR_s["sc_tm"][0:48, :], scv[:, blk * 1024:(blk + 1) * 1024], (), ["sc_tm"])
                    fm_block(wv, wk, 8, blk * 8, c_last_cons)
                return f

            sched.append((w_in[:, C_DT:C_DT + 32], 8, 32, first_b))
            sched.append((w_in[:, C_Z:C_Z + 1024], 8, 1024, lambda wv, wk: tm_block(wv, wk, 1024, 0, z_cons_block(0))))
            sched.append((w_in[:, C_Z + 1024:C_Z + 2048], 8, 1024, lambda wv, wk: tm_block(wv, wk, 1024, 0, z_cons_block(1))))
            for blk in range(4):
                sched.append((w_in[:, C_XBC + blk * 1024:C_XBC + (blk + 1) * 1024], 8, 1024, xbc_block(blk)))
            def gate_b_consumer(c, ps_ap, k):
                gate_consumer(c, ps_ap, k)

            sched.append((w_in[:, C_MG + 1024:C_MG + 2048], 8, 1024, lambda wv, wk: fm_block(wv, wk, 8, 0, gate_b_consumer)))
            pb = proj_generic(hbT, [("hbT", i) for i in range(nT)], 16, False, True)
            sched.append((wpb[:, 0:512], 16, 512, lambda wv, wk: pb(wv, wk, 0, 4, 0)))
            sched.append((wpb[:, 512:1024], 16, 512, lambda wv, wk: pb(wv, wk, 4, 8, 512)))

            yres = F0

            def out_block(wv, wk):
                for i in range(nT):
                    j = i % 2
                    DMA(xt[:, j, :], tile_rows(xsrc, i), (), [("xt", j)])
                    for half in range(2):
                        k = BK.get()
                        for kc in range(8):
                            MM(psb[k][:], mTb[:, kc, i * 128:(i + 1) * 128], wv[:, kc, half * 512:(half + 1) * 512], kc == 0, kc == 7,
                               [wk] + [("mTb", c) for c in range(8)], [PK(k)])
                        STT(yres[:, j, half * 512:(half + 1) * 512], psb[k][:], 0.5, xt[:, j, half * 512:(half + 1) * 512], ALU.mult, ALU.add,
                            [PK(k), ("xt", j)], [("F0", j)])
                    ss, ssk = stat()
                    ACT(junk[:], yres[:, j, :], AF.Square, [("F0", j)], ["junk"] + ssk, accum_out=ss)
                    r, rk = stat()
                    rstd_from_ss(ss, ssk, D, r, rk)
                    ACT(yres[:, j, :], yres[:, j, :], AF.Identity, [("F0", j)] + rk, [("F0", j)], scale=r)
                    TT(yres[:, j, :], yres[:, j, :], fg_bc[:], ALU.mult, [("F0", j), "fg_bc"], [("F0", j)])
                    DMA(tile_rows(ydst, i), yres[:, j, :], [("F0", j)], [("F0", j)], q="pool")

            sched.append((wout, 8, 1024, out_block))

            if not cached:
                for bi, (src_, kc_, ncols_, _fn) in enumerate(sched):
                    n_ = kc_ * ncols_
                    DMA(wscr[bi][:, 0:n_].rearrange("p (kc n) -> p kc n", kc=kc_), src_.rearrange("(kc p) n -> p kc n", p=128),
                        (), [("scr", bi)], q="pool")
                    cached.add(bi)
                for b_ in range(NSEQ_S):
                    for t_, src_ in ((0, ck), (1, cv)):
                        DMA(kvscr[t_, b_].rearrange("p (mt n) -> p mt n", mt=2),
                            src_[b_].rearrange("(mt p) n -> p mt n", p=128), (), [("kvs", t_, b_)], q="pool")
            nxt = load_w(sched[0][0], sched[0][1], sched[0][2], 0)
            for si, (src, kc, ncols, fn) in enumerate(sched):
                cur = nxt
                if si + 1 < len(sched):
                    nxt = load_w(sched[si + 1][0], sched[si + 1][1], sched[si + 1][2], si + 1)
                fn(cur[0], cur[1])

        ARp = Arena("p")
        hT = ARp.take("hT", [2048], F32)
        hTb = ARp.take("hTb", [2048], BF16)
        KT = ARp.take("KT", [4, 256], BF16)
        Vt = ARp.take("Vt", [2, 512], BF16)
        ARp.base = ARp.off
        MSET(hT, 0.0, [("hT", g) for g in range(8)])
        MSET(hTb, 0.0, [("hTb", g) for g in range(8)])
        memT = ARp.take("memT", [8, 256], BF16, reg=True)
        kvf = ARp.take("kvf", [2, 512], F32, reg=True)
        for mt in range(2):
            norm_transpose(mem[mt * 128:(mt + 1) * 128, :], xt[:, mt, :], ("xt", mt), mgfm, ["pfm"],
                           memT[:, :, mt * 128:(mt + 1) * 128], [("memT", mt)])
        if SUB <= 1:
            P.barrier()
            P.emit()
            return nc
        wv, wk = load_w(wkv, 8, 1024)
        if SUB <= 2:
            P.barrier()
            P.emit()
            return nc
        for mt in range(2):
            for half in range(2):
                k = BK.get()
                for kc in range(8):
                    MM(psb[k][:], memT[:, kc, mt * 128:(mt + 1) * 128], wv[:, kc, half * 512:(half + 1) * 512], kc == 0, kc == 7,
                       [wk, ("memT", mt)], [PK(k)])
                if "nocp" not in KVAR:
                    CP(kvf[:, half, :], psb[k][:], [PK(k)], [("kvf", half)])
                if "nodma" not in KVAR:
                    DMA((mk if half == 0 else mv)[mt * 128:(mt + 1) * 128, :], kvf[:, half, :], [("kvf", half)], [("kvf", half)])
                if half == 1 and "noacp" not in KVAR:
                    ACP(Vt[:, mt, :], psb[k][:], [PK(k)] + ([("kvf", half)] if "acpdep" in KVAR else []), ["Vt"])
        if SUB <= 3:
            P.barrier()
            P.emit()
            return nc
        for h in range(4):
            k = BK.get()
            for kc in range(8):
                MM(psb[k][:, 0:256], wv[:, kc, h * 128:(h + 1) * 128], memT[:, kc, :], kc == 0, kc == 7,
                   [wk, ("memT", 0), ("memT", 1)], [PK(k)])
            CP(KT[:, h, :], psb[k][:, 0:256], [PK(k)], ["KT"])
        if STOP <= 2:
            P.emit()
            return nc

        NT_P = 2
        for st in range(SEQ // (128 * NT_P)):
            run_st("p", st * 128 * NT_P, NT_P, ARp, hT, hTb, KT, Vt, None)
            if STOP <= 3:
                P.emit()
                return nc
        P.barrier()
        ARp.off = ARp.base
        hpo = ARp.take("hpo", [16, 128], F32)
        rows_p = ARp.take("rows_p", [1024], F32)
        for q4 in range(4):
            k = BK.get()
            for cc in range(4):
                jj = q4 * 4 + cc
                TR(psb[k][:, cc * 128:(cc + 1) * 128], hT[:, jj * 128:(jj + 1) * 128], ident_f, [("hT", g) for g in range(8)] + ["cfs"], [PK(k)])
            CP(hpo[:, q4 * 4:(q4 + 1) * 4, :], psb[k][:].rearrange("p (c n) -> p c n", c=4), [PK(k)], [("hpo", q4)])
        DMA(hp.rearrange("(j p) n -> p j n", p=128), hpo, [("hpo", q4) for q4 in range(4)], [])
        fm_to_rows(lambda c: xhalo[:, c, :], 3, cpo, "cp", [("xhalo", c) for c in range(32)], rows_p)
        P.barrier()
        if STOP <= 4:
            P.emit()
            return nc

        ARs = Arena("s")
        AR_s = {}
        AR_s["sc_tm"] = ARs.take("sc_tm", [1024], F32)
        AR_s["xlast"] = ARs.take("xlast", [32, 16, 3], F32)
        AR_s["bsp_s"] = ARs.take("bsp_s", [8, 128], F32)
        AR_s["CTm"] = ARs.take("CTm", [8, 128], BF16)
        AR_s["h0T"] = ARs.take("h0T", [2048], BF16)
        h0n0 = ARs.take("h0n0", [16, 128], F32)
        AR_s["Bm"] = ARs.take("Bm", [2, 1024], BF16)
        AR_s["cdx"] = ARs.take("cdx", [2, 2, 64], F32)
        mark_s = ARs.off
        AR_s["KTs"] = ARs.take("KTs", [2, 4, 256], BF16, reg=True)
        AR_s["Kc"] = ARs.take("Kc", [2, 2, 512], BF16, reg=True)
        AR_s["Vc"] = ARs.take("Vc", [2, 2, 512], BF16, reg=True)
        AR_s["PTs"] = ARs.take("PTs", [2, 64], BF16, reg=True)
        AR_s["rd4"] = ARs.take("rd4", [512], F32, reg=True)
        off_att = ARs.off
        ARs.off = mark_s
        AR_s["yoff"] = ARs.take("yoff", [2048], F32, reg=True)
        AR_s["cdn"] = ARs.take("cdn", [256], F32, reg=True)
        h0n1 = ARs.take("h0n1", [16, 128], F32, reg=True)
        AR_s["h0n"] = [h0n0, h0n1]
        ARs.off = max(ARs.off, off_att)
        rows_s = AR_s["sc_tm"]
        ARs.base = ARs.off
        for g in range(8):
            CP(AR_s["bsp_s"][:, g, :].rearrange("p (b l) -> p b l", b=16), bsp_bc[:, g, None, 0:8].to_broadcast([128, 16, 8]),
               ["bsp_bc"], ["bsp_s"])
        run_st("s", 0, 1, ARs, None, None, None, None, AR_s)
        P.barrier()
        fm_to_rows(lambda c: AR_s["xlast"][:, c].rearrange("p b k -> p (b k)"), 48, cso, "cs", [("xlast", c) for c in range(32)], rows_s)

        P.emit()
    return nc


_NC_CACHE = {}


def kernel(x_prompt, x_sample, mem_prompt, cache_mem_k, cache_mem_v, state_ssm, state_conv, norm_g, w_in,
           conv_w, conv_b, dt_bias, a_log, d_skip, ssd_norm_g, ln_v_g, ln_v_b, w_spatial, b_spatial,
           mem_norm_g, w_mem_kv, w_proj_a, w_proj_b, w_proj_x, w_out, final_norm_g):
    f = lambda a: np.ascontiguousarray(np.asarray(a, dtype=np.float32))
    if "nc" not in _NC_CACHE:
        _NC_CACHE["nc"] = build_nc()
    nc = _NC_CACHE["nc"]
    cf, cb = _consts()
    pstage = np.zeros((8, 4096), np.float32)
    pstage[0:4] = f(conv_w)[0]
    pstage[4] = f(conv_b)[0]
    pstage[5, 0:2048] = f(ssd_norm_g)[0]
    pstage[5, 2048:3072] = f(norm_g)[0]
    pstage[5, 3072:4096] = f(mem_norm_g)[0]
    shared = dict(
        pstage=pstage, w_in=f(w_in)[0], dt_bias=f(dt_bias).reshape(1, 32), a_log=f(a_log).reshape(1, 32),
        d_skip=f(d_skip).reshape(1, 32), ln_v_g=f(ln_v_g).reshape(1, D), ln_v_b=f(ln_v_b).reshape(1, D),
        final_norm_g=f(final_norm_g).reshape(1, D), w_spatial=f(w_spatial)[0], b_spatial=f(b_spatial).reshape(1, 1024),
        w_mem_kv=f(w_mem_kv)[0], w_proj_a=f(w_proj_a)[0], w_proj_b=f(w_proj_b)[0], w_proj_x=f(w_proj_x)[0],
        w_out=f(w_out)[0], cf=cf, cb=cb,
    )
    xp_, xs_, mem_ = f(x_prompt), f(x_sample), f(mem_prompt)
    ck_, cv_, sst_, scv_ = f(cache_mem_k)[0], f(cache_mem_v)[0], f(state_ssm)[0], f(state_conv)[0]
    in_maps = []
    for c in range(NCORES):
        sl = slice(c * 16, (c + 1) * 16)
        m = dict(shared)
        m["xp"] = xp_[c]
        m["xsm"] = xs_[sl].reshape(128, D)
        m["mem"] = mem_[c]
        m["ck"] = ck_[sl].reshape(16, 256, 512)
        m["cv"] = cv_[sl].reshape(16, 256, 512)
        m["sst"] = sst_[sl].reshape(16, 2048, 128)
        m["scv"] = scv_[sl].reshape(48, 4096)
        in_maps.append(m)
    res = run_bass_kernel_spmd(nc, in_maps, core_ids=list(range(NCORES)))
    R = res.results
    y_prompt = np.stack([R[c]["yp"] for c in range(NCORES)]).reshape(8, SEQ, D)
    y_sample = np.concatenate([R[c]["ys"].reshape(16, 8, D) for c in range(NCORES)], 0)
    mk = np.stack([R[c]["mk"].reshape(256, 4, 128) for c in range(NCORES)])[None]
    mv = np.stack([R[c]["mv"].reshape(256, 4, 128) for c in range(NCORES)])[None]
    hpo = np.stack([R[c]["hp"].reshape(32, 64, 128) for c in range(NCORES)])[None]
    cpo = np.stack([R[c]["cp"] for c in range(NCORES)])[None]
    hso = np.concatenate([R[c]["hs"].reshape(16, 32, 64, 128) for c in range(NCORES)], 0)[None]
    cso = np.concatenate([R[c]["cs"].reshape(16, 3, 4096) for c in range(NCORES)], 0)[None]
    vso = np.concatenate([R[c]["vs"].reshape(16, 8, D) for c in range(NCORES)], 0)[None]
    return (y_prompt.astype(np.float32), y_sample.astype(np.float32), mk.astype(np.float32), mv.astype(np.float32),
            hpo.astype(np.float32), cpo.astype(np.float32), hso.astype(np.float32), cso.astype(np.float32),
            vso.astype(np.float32))
```

```python
import os
import numpy as np
from contextlib import ExitStack
import concourse.bass as bass
import concourse.mybir as mybir
from concourse.bass_utils import run_bass_kernel_spmd

F32 = mybir.dt.float32
BF16 = mybir.dt.bfloat16
AF = mybir.ActivationFunctionType
ALU = mybir.AluOpType

NCORES = 8
STOP = int(os.environ.get("KSTOP", "99"))
SUB = int(os.environ.get("KSUB", "99"))
KVAR = os.environ.get("KVAR", "")
D = 1024
IN_DIM = 13344
NH = 32
EPS = 1e-6
C_U, C_V, C_GA, C_Z, C_XBC, C_DT, C_Q, C_GX, C_MG = 0, 1024, 2048, 3072, 5120, 9216, 9248, 9760, 10272
SEQ = 2048
NSEQ_S = 16
LS = 8

ENGS = ("pe", "act", "dve", "pool", "sp")


class Op:
    __slots__ = ("eng", "fn", "deps", "dma", "signal", "sem", "val", "prewait")

    def __init__(self, eng, fn, deps, dma):
        self.eng = eng
        self.fn = fn
        self.deps = deps
        self.dma = dma
        self.signal = dma
        self.sem = None
        self.val = None
        self.prewait = None


class Prog:
    def __init__(self, nc, n_dma_sems=8):
        self.nc = nc
        self.ops = []
        self.last_w = {}
        self.readers = {}
        self.n_dma_sems = n_dma_sems
        self.bar_start = 0
        self.bufs = {}
        self.live = {}

    def register(self, name, start, end):
        old = self.bufs.get(name)
        if old is not None and old != (start, end) and name in self.live:
            raise RuntimeError(f"buffer {name} re-registered with a new range while live")
        self.bufs[name] = (start, end)

    def _touch(self, key, idx, eng, dma, deps):
        bname = key if isinstance(key, str) else key[0]
        rng = self.bufs.get(bname)
        if rng is None:
            return
        if bname not in self.live:
            s, e = rng
            for other in list(self.live):
                os_, oe = self.bufs[other]
                if os_ < e and s < oe:
                    acc = self.live.pop(other)
                    for d in list(acc["eng"].values()) + acc["dma"]:
                        o = self.ops[d]
                        if o.eng == eng and eng == "pe" and not o.dma and not dma:
                            continue
                        deps.add(d)
            self.live[bname] = {"eng": {}, "dma": []}
        a = self.live[bname]
        if dma:
            a["dma"].append(idx)
        else:
            a["eng"][eng] = idx

    def add(self, eng, fn, reads=(), writes=(), dma=False):
        idx = len(self.ops)
        deps = set()
        ps_r = [k for k in reads if isinstance(k, tuple) and k[0] == "ps"]
        if ps_r:
            reads = [k for k in reads if not (isinstance(k, tuple) and k[0] == "ps")]
            writes = list(writes) + ps_r

        def consider(d, raw):
            o = self.ops[d]
            if o.eng == eng and not o.dma and not dma:
                if eng == "pe":
                    return
            deps.add(d)

        for k in reads:
            w = self.last_w.get(k)
            if w is not None:
                consider(w, True)
        for k in writes:
            w = self.last_w.get(k)
            if w is not None:
                consider(w, False)
            for r in self.readers.get(k, ()):
                consider(r, False)
        for k in writes:
            self.last_w[k] = idx
            self.readers[k] = []
        for k in reads:
            self.readers.setdefault(k, []).append(idx)
        for k in list(reads) + list(writes):
            self._touch(k, idx, eng, dma, deps)
        deps.discard(idx)
        best = {}
        red = set()
        for d in deps:
            o = self.ops[d]
            if o.dma:
                red.add(d)
            elif best.get(o.eng, -1) < d:
                best[o.eng] = d
        red.update(best.values())
        deps = red
        for d in deps:
            self.ops[d].signal = True
        self.ops.append(Op(eng, fn, deps, dma))
        return idx

    def barrier(self):
        pend = range(self.bar_start, len(self.ops))
        last = {}
        dmas = []
        for d in pend:
            o = self.ops[d]
            if o.dma:
                dmas.append(d)
            else:
                last[o.eng] = d
        for e in ENGS:
            deps = set(dmas)
            for e2, d in last.items():
                if e2 != e:
                    deps.add(d)
            for d in deps:
                self.ops[d].signal = True
            self.ops.append(Op(e, lambda eng: eng.nop(), deps, False))
        self.bar_start = len(self.ops)
        self.last_w.clear()
        self.readers.clear()
        self.live.clear()
        self.bufs.clear()

    def emit(self):
        nc = self.nc
        with ExitStack() as es:
            csem = {e: es.enter_context(nc.semaphore(f"c_{e}")) for e in ENGS}
            dsem = {
                e: [es.enter_context(nc.semaphore(f"d_{e}{i}")) for i in range(self.n_dma_sems)]
                for e in ("sp", "act", "pool")
            }
            ccount = {e: 0 for e in ENGS}
            dcount = {e: [0] * self.n_dma_sems for e in dsem}
            drr = {e: 0 for e in dsem}
            for o in self.ops:
                if o.dma:
                    i = drr[o.eng]
                    drr[o.eng] = (i + 1) % self.n_dma_sems
                    o.sem = dsem[o.eng][i]
                    o.prewait = dcount[o.eng][i]
                    dcount[o.eng][i] += 16
                    o.val = dcount[o.eng][i]
                elif o.signal:
                    ccount[o.eng] += 1
                    o.sem = csem[o.eng]
                    o.val = ccount[o.eng]
            ops = self.ops
            if os.environ.get("KDEBUG"):
                print("PROG ops", len(ops), "compute signals", ccount, "dma counts", dcount)
            final_waits = [
                (dsem[e][i], dcount[e][i]) for e in dsem for i in range(self.n_dma_sems) if dcount[e][i] > 0
            ]

            def run_engine(ename, eng):
                seen = {}
                semobj = {}
                for o in ops:
                    if o.eng != ename:
                        continue
                    need = {}
                    for d in o.deps:
                        od = ops[d]
                        k = id(od.sem)
                        semobj[k] = od.sem
                        if need.get(k, 0) < od.val:
                            need[k] = od.val
                    if o.dma and o.prewait > 0:
                        k = id(o.sem)
                        semobj[k] = o.sem
                        if need.get(k, 0) < o.prewait:
                            need[k] = o.prewait
                    for k, v in need.items():
                        if seen.get(k, 0) >= v:
                            continue
                        seen[k] = v
                        eng.wait_ge(semobj[k], v)
                    inst = o.fn(eng)
                    if o.dma:
                        inst.then_inc(o.sem, 16)
                    elif o.signal:
                        inst.then_inc(o.sem, 1)
                if ename == "sp":
                    for s, v in final_waits:
                        eng.wait_ge(s, v)

            with nc.Block() as block:

                @block.sync
                def _(e):
                    run_engine("sp", e)

                @block.tensor
                def _(e):
                    run_engine("pe", e)

                @block.scalar
                def _(e):
                    run_engine("act", e)

                @block.vector
                def _(e):
                    run_engine("dve", e)

                @block.gpsimd
                def _(e):
                    run_engine("pool", e)


class Pipe:
    def __init__(self):
        self.active = []

    def tick(self, gen=None):
        if gen is not None:
            self.active.append(gen)
        for g in list(self.active):
            try:
                next(g)
            except StopIteration:
                self.active.remove(g)

    def drain(self):
        while self.active:
            self.tick()


CF_ID, CF_UP, CF_US, CF_ONES, CF_RM, CF_RS = 0, 128, 256, 384, 512, 528
NCF = 528 + 16
CB_ID, CB_UP, CB_LP, CB_US, CB_LS, CB_E, CB_BD = 0, 128, 256, 384, 512, 640, 768
NCB = 768 + 2048


def _consts():
    t = np.arange(128)
    blk = t // LS
    same = (blk[:, None] == blk[None, :]).astype(np.float32)
    U_p = (t[:, None] <= t[None, :]).astype(np.float32)
    L_p = (t[:, None] > t[None, :]).astype(np.float32)
    U_s = U_p * same
    L_s = L_p * same
    cf = np.zeros((128, NCF), np.float32)
    cf[:, CF_ID:CF_ID + 128] = np.eye(128)
    cf[:, CF_UP:CF_UP + 128] = U_p
    cf[:, CF_US:CF_US + 128] = U_s
    cf[:, CF_ONES:CF_ONES + 128] = same
    rm = (blk[:, None] == np.arange(16)[None, :]).astype(np.float32)
    cf[:, CF_RM:CF_RM + 16] = rm
    rs = np.zeros((128, 16), np.float32)
    for b in range(16):
        rs[8 * b, b] = 1.0
    cf[:, CF_RS:CF_RS + 16] = rs
    cb = np.zeros((128, NCB), np.float32)
    cb[:, CB_ID:CB_ID + 128] = np.eye(128)
    cb[:, CB_UP:CB_UP + 128] = U_p
    cb[:, CB_LP:CB_LP + 128] = L_p
    cb[:, CB_US:CB_US + 128] = U_s
    cb[:, CB_LS:CB_LS + 128] = L_s
    E = np.zeros((128, 128), np.float32)
    for s in range(8):
        E[s, :] = (t % 8 == s)
    cb[:, CB_E:CB_E + 128] = E
    bd = np.zeros((128, 16, 128), np.float32)
    for b in range(16):
        bd[:, b, 8 * b:8 * b + 8] = 1.0
    cb[:, CB_BD:CB_BD + 2048] = bd.reshape(128, 2048)
    return cf, cb


def build_nc():
    nc = bass.Bass("TRN2", target_bir_lowering=False)

    def din(name, shape):
        return nc.dram_tensor(name, list(shape), F32, kind="ExternalInput").ap()

    def dout(name, shape):
        return nc.dram_tensor(name, list(shape), F32, kind="ExternalOutput").ap()

    xp = din("xp", [SEQ, D])
    xsm = din("xsm", [128, D])
    mem = din("mem", [256, D])
    ck = din("ck", [16, 256, 512])
    cv = din("cv", [16, 256, 512])
    sst = din("sst", [16, 2048, 128])
    scv = din("scv", [48, 4096])
    pstage = din("pstage", [8, 4096])
    w_in = din("w_in", [D, IN_DIM])
    dtb = din("dt_bias", [1, 32])
    alog = din("a_log", [1, 32])
    dsk = din("d_skip", [1, 32])
    lng = din("ln_v_g", [1, D])
    lnb = din("ln_v_b", [1, D])
    fng = din("final_norm_g", [1, D])
    wsp = din("w_spatial", [8, 128, 128])
    bsp = din("b_spatial", [1, 1024])
    wkv = din("w_mem_kv", [D, D])
    wpa = din("w_proj_a", [D, D])
    wpb = din("w_proj_b", [2048, D])
    wpx = din("w_proj_x", [512, D])
    wout = din("w_out", [D, D])
    cfd = din("cf", [128, NCF])
    cbd = din("cb", [128, NCB])

    wscr = nc.dram_tensor("wscr", [24, 128, 8192], BF16, kind="Internal").ap()
    kvscr = nc.dram_tensor("kvscr", [2, 16, 128, 1024], BF16, kind="Internal").ap()
    yp = dout("yp", [SEQ, D])
    ys = dout("ys", [128, D])
    mk = dout("mk", [256, 512])
    mv = dout("mv", [256, 512])
    hp = dout("hp", [2048, 128])
    cpo = dout("cp", [3, 4096])
    hs = dout("hs", [16, 2048, 128])
    cso = dout("cs", [48, 4096])
    vs = dout("vs", [128, D])

    with ExitStack() as es:
        def sb(name, shape, dt):
            return es.enter_context(nc.sbuf_tensor(name, list(shape), dt))

        psb = [es.enter_context(nc.psum_tensor(f"ps{k}", [128, 512], F32)) for k in range(8)]
        P = Prog(nc, n_dma_sems=16)

        class Banks:
            def __init__(self):
                self.pinned = set()
                self.i = 0

            def get(self, pin=False):
                for _ in range(16):
                    k = self.i
                    self.i = (self.i + 1) % 8
                    if k not in self.pinned:
                        if pin:
                            self.pinned.add(k)
                        return k
                raise RuntimeError("no psum bank")

            def unpin(self, k):
                self.pinned.discard(k)

        BK = Banks()

        def PK(k):
            return ("ps", k)

        def MM(out, lhsT, rhs, start, stop, rd, wr):
            P.add("pe", lambda e: e.matmul(out, lhsT, rhs, start=start, stop=stop), rd, wr)

        def TR(out, in_, ident, rd, wr):
            P.add("pe", lambda e: e.transpose(out, in_, ident), rd, wr)

        def ACT(out, in_, func, rd, wr, **kw):
            P.add("act", lambda e: e.activation(out, in_, func, **kw), rd, wr)

        def TT(out, a, b, op, rd, wr):
            P.add("dve", lambda e: e.tensor_tensor(out, a, b, op), rd, wr)

        def TS(out, a, s1, s2, op0, op1, rd, wr):
            P.add("dve", lambda e: e.tensor_scalar(out, a, s1, s2, op0, op1), rd, wr)

        def TS1(out, a, s, op, rd, wr):
            P.add("dve", lambda e: e.tensor_single_scalar(out, a, s, op), rd, wr)

        def STT(out, in0, scalar, in1, op0, op1, rd, wr):
            P.add("dve", lambda e: e.scalar_tensor_tensor(out, in0, scalar, in1, op0, op1), rd, wr)

        def CP(out, in_, rd, wr):
            P.add("dve", lambda e: e.tensor_copy(out, in_), rd, wr)

        def PTT(out, a, b, op, rd, wr):
            P.add("pool", lambda e: e.tensor_tensor(out, a, b, op), rd, wr)

        def PCP(out, in_, rd, wr):
            P.add("pool", lambda e: e.tensor_copy(out, in_), rd, wr)

        def ACP(out, in_, rd, wr):
            if "acpident" in KVAR:
                P.add("act", lambda e: e.activation(out, in_, AF.Identity), rd, wr)
            else:
                P.add("act", lambda e: e.copy(out, in_), rd, wr)

        def DMA(out, in_, rd, wr, q="sp"):
            P.add(q, lambda e: e.dma_start(out=out, in_=in_), rd, wr, dma=True)

        def MSET(ap, val, wr):
            P.add("dve", lambda e: e.memset(ap, val), (), wr)

        cfs = sb("cfs", [128, NCF], F32)
        cbs = sb("cbs", [128, NCB], BF16)
        ones_f = sb("ones_f", [128, 128], F32)
        ones_b = sb("ones_b", [128, 128], BF16)
        pfm = sb("pfm", [128, 32, 8], F32)
        fg_bc = sb("fg_bc", [128, D], F32)
        lng_bc = sb("lng_bc", [128, D], F32)
        lnb_bc = sb("lnb_bc", [128, D], F32)
        bsp_bc = sb("bsp_bc", [128, 8, 128], F32)
        dtb_bc = sb("dtb_bc", [128, 32], F32)
        a_bc = sb("a_bc", [128, 32], F32)
        dsk_bc = sb("dsk_bc", [128, 32], F32)
        WTp = sb("WTp", [128, 8, 128], BF16)
        WTs = sb("WTs", [128, 8, 128], BF16)
        xhalo = sb("xhalo", [128, 32, 3], F32)
        stats = sb("stats", [128, 256], F32)
        mT = sb("mT", [128, 8, 256], F32)
        mTb = sb("mTb", [128, 8, 256], BF16)
        hnT = sb("hnT", [128, 8, 256], BF16)
        xt = sb("xt", [128, 2, D], F32)
        wbuf = [sb(f"wb{j}", [128, 8192], BF16) for j in range(2)]
        junk = sb("junk", [128, D], BF16)
        xn = sb("xn", [128, D], BF16)
        ARENA = 59648
        arena = sb("arena", [128, ARENA], BF16)

        ident_f = cfs[:, CF_ID:CF_ID + 128]
        ident_b = cbs[:, CB_ID:CB_ID + 128]
        rowmask = cfs[:, CF_RM:CF_RM + 16]

        st_i = [0]

        def stat(n=1):
            if st_i[0] + n > 256:
                st_i[0] = 0
            a = st_i[0]
            st_i[0] += n
            return stats[:, a:a + n], [("st", j) for j in range(a, a + n)]

        class Arena:
            def __init__(self, tag):
                self.off = 0
                self.base = 0
                self.tag = tag

            def take(self, name, shape, dt, reg=False):
                n = int(np.prod(shape))
                nb = n if dt == BF16 else 2 * n
                nb = (nb + 1) // 2 * 2
                if self.off + nb > ARENA:
                    raise RuntimeError(f"arena overflow at {name}: {self.off}+{nb}")
                if reg:
                    P.register(name, self.off, self.off + nb)
                ap = arena[:, self.off:self.off + nb]
                self.off += nb
                if dt == F32:
                    ap = ap.bitcast(F32)
                if len(shape) == 2:
                    v = ap.rearrange("p (a b) -> p a b", a=shape[0])
                elif len(shape) == 3:
                    v = ap.rearrange("p (a b c) -> p a b c", a=shape[0], b=shape[1])
                else:
                    v = ap
                return v

        ARset = Arena("setup")
        pst = ARset.take("pst", [4096], F32)
        wsf = ARset.take("wsf", [8, 128], F32)
        rep = ARset.take("rep", [8, 8], F32)
        DMA(cfs[:], cfd, (), ["cfs"])
        DMA(cbs[:, 0:1408], cbd[:, 0:1408], (), ["cbs"], q="pool")
        DMA(cbs[:, 1408:NCB], cbd[:, 1408:NCB], (), ["cbs"], q="pool")
        DMA(pst[0:8, :], pstage, (), ["pst"])
        DMA(fg_bc[:], fng.partition_broadcast(128), (), ["fg_bc"])
        DMA(lng_bc[:], lng.partition_broadcast(128), (), ["lng_bc"])
        DMA(lnb_bc[:], lnb.partition_broadcast(128), (), ["lnb_bc"])
        DMA(bsp_bc[:].rearrange("p g t -> p (g t)"), bsp.partition_broadcast(128), (), ["bsp_bc"])
        DMA(dtb_bc[:], dtb.partition_broadcast(128), (), ["dtb_bc"])
        DMA(a_bc[:], alog.partition_broadcast(128), (), ["a_bc"])
        DMA(dsk_bc[:], dsk.partition_broadcast(128), (), ["dsk_bc"])
        DMA(wsf, wsp.rearrange("g t s -> t g s"), (), ["wsf"])
        MSET(ones_f[:], 1.0, ["ones_f"])
        MSET(ones_b[:], 1.0, ["ones_b"])
        MSET(xhalo[:], 0.0, ["xhalo"])
        ACT(a_bc[:], a_bc[:], AF.Exp, ["a_bc"], ["a_bc"])
        TS1(a_bc[:], a_bc[:], -1.0, ALU.mult, ["a_bc"], ["a_bc"])
        k = BK.get()
        for c in range(32):
            TR(psb[k][:, c * 8:(c + 1) * 8], pst[0:8, c * 128:(c + 1) * 128], cfs[0:8, CF_ID:CF_ID + 8], ["pst", "cfs"], [PK(k)])
        CP(pfm[:].rearrange("p c k -> p (c k)"), psb[k][:, 0:256], [PK(k)], ["pfm"])
        gfm = pfm[:, 16:24, 5]
        mgfm = pfm[:, 24:32, 5]
        ssdg = pfm[:, 0:16, 5]
        for half in range(2):
            k = BK.get()
            for gg in range(4):
                g = half * 4 + gg
                TR(psb[k][:, gg * 128:(gg + 1) * 128], wsf[:, g, :], ident_f, ["wsf", "cfs"], [PK(k)])
            TT(WTp[:, half * 4:half * 4 + 4, :], psb[k][:].rearrange("p (g t) -> p g t", g=4),
               cfs[:, None, CF_UP:CF_UP + 128].to_broadcast([128, 4, 128]), ALU.mult, [PK(k), "cfs"], ["WTp"])
        k = BK.get()
        MM(psb[k][:, 0:64].rearrange("p (g t) -> p g t", g=8), cbs[0:8, CB_E:CB_E + 128], WTp[0:8, :, 0:8], True, True,
           ["cbs", "WTp"], [PK(k)])
        CP(rep, psb[k][:, 0:64].rearrange("p (g t) -> p g t", g=8), [PK(k)], ["rep"])
        for g in range(8):
            TT(WTs[:, g, :].rearrange("p (b t) -> p b t", b=16), rep[:, g, None, :].to_broadcast([128, 16, 8]),
               rowmask[:, :, None].to_broadcast([128, 16, 8]), ALU.mult, ["rep", "cfs"], ["WTs"])
        P.barrier()
        if STOP <= 1:
            P.emit()
            return nc

        wcnt = [0]

        cached = set()

        def load_w(src, kc, ncols, blk=None):
            j = wcnt[0] % 2
            wcnt[0] += 1
            n = kc * ncols
            flat = wbuf[j][:, 0:n]
            view = flat.rearrange("p (kc n) -> p kc n", kc=kc)
            if blk is None:
                DMA(view, src.rearrange("(kc p) n -> p kc n", p=128), (), [("wb", j)], q="pool")
            elif blk not in cached:
                DMA(view, src.rearrange("(kc p) n -> p kc n", p=128), (), [("wb", j)], q="pool")
                DMA(wscr[blk][:, 0:n], flat, [("wb", j)], [("scr", blk)], q="sp")
                cached.add(blk)
            else:
                DMA(flat, wscr[blk][:, 0:n], [("scr", blk)], [("wb", j)], q="sp")
            return view, ("wb", j)

        def rstd_from_ss(ss_ap, ss_keys, n, out_ap, out_keys):
            ACT(out_ap, ss_ap, AF.Ln, ss_keys, out_keys, scale=1.0 / n, bias=EPS)
            ACT(out_ap, out_ap, AF.Exp, out_keys, out_keys, scale=-0.5)

        def norm_transpose(src_tile, xbuf, xkey, gsc, gkeys, dstT, dkeys, q="sp"):
            DMA(xbuf, src_tile, (), [xkey], q=q)
            ss, ssk = stat()
            ACT(junk[:], xbuf, AF.Square, [xkey], ["junk"] + ssk, accum_out=ss)
            r, rk = stat()
            rstd_from_ss(ss, ssk, D, r, rk)
            ACT(xn[:], xbuf, AF.Identity, [xkey] + rk, ["xn"], scale=r)
            k = BK.get()
            pv = psb[k][:].bitcast(BF16)
            for kc in range(8):
                TR(pv[:, kc * 128:(kc + 1) * 128], xn[:, kc * 128:(kc + 1) * 128], ident_b, ["xn", "cbs"], [PK(k)])
            TT(dstT, pv.rearrange("p (kc t) -> p kc t", kc=8), gsc[:, :, None].to_broadcast([128, 8, 128]), ALU.mult,
               [PK(k)] + gkeys, dkeys)

        def fm_to_rows(src_fn, nrows, out_dram, tagkey, rkeys, rows):
            for piece in range(4):
                for q4 in range(2):
                    k = BK.get()
                    for cc in range(4):
                        c = piece * 8 + q4 * 4 + cc
                        TR(psb[k][0:nrows, cc * 128:(cc + 1) * 128], src_fn(c), ident_f, rkeys + ["cfs"], [PK(k)])
                    CP(rows[0:nrows, q4 * 512:(q4 + 1) * 512], psb[k][0:nrows, :], [PK(k)], [("rows", tagkey)])
                DMA(out_dram[:, piece * 1024:(piece + 1) * 1024], rows[0:nrows, :], [("rows", tagkey)], [("rows", tagkey)])

        def run_st(kind, tok0, nT, AR, hT, hTb, KT, Vt, AR_s):
            T = 128 * nT
            samp = kind == "s"
            xsrc = xsm if samp else xp
            ydst = ys if samp else yp
            Umask_b = cbs[:, CB_US:CB_US + 128] if samp else cbs[:, CB_UP:CB_UP + 128]
            Lmask_b = cbs[:, CB_LS:CB_LS + 128] if samp else cbs[:, CB_LP:CB_LP + 128]
            U_f = cfs[:, CF_US:CF_US + 128] if samp else cfs[:, CF_UP:CF_UP + 128]
            tot_f = cfs[:, CF_ONES:CF_ONES + 128] if samp else ones_f[:]
            WT = WTs if samp else WTp
            bspx = AR_s["bsp_s"] if samp else bsp_bc
            hnk = [("hnT", i) for i in range(nT)]

            def tile_rows(ap, i):
                return ap[tok0 + i * 128: tok0 + (i + 1) * 128, :]

            for i in range(nT):
                norm_transpose(tile_rows(xsrc, i), xt[:, i % 2, :], ("xt", i % 2), gfm, ["pfm"],
                               hnT[:, :, i * 128:(i + 1) * 128], [("hnT", i)])

            def fm_block(wv, wk, nch, c_base, consumer):
                for c in range(nch):
                    k = BK.get()
                    for kc in range(8):
                        MM(psb[k][:, 0:T], wv[:, kc, c * 128:(c + 1) * 128], hnT[:, kc, 0:T], kc == 0, kc == 7,
                           [wk] + hnk, [PK(k)])
                    consumer(c_base + c, psb[k][:, 0:T], k)

            def tm_block(wv, wk, ncols, col_base, consumer):
                for i in range(nT):
                    for s0 in range(0, ncols, 512):
                        n = min(512, ncols - s0)
                        k = BK.get()
                        for kc in range(8):
                            MM(psb[k][:, 0:n], hnT[:, kc, i * 128:(i + 1) * 128], wv[:, kc, s0:s0 + n], kc == 0, kc == 7,
                               [wk, ("hnT", i)], [PK(k)])
                        consumer(i, col_base + s0, n, psb[k][:, 0:n], k)

            AR.off = AR.base
            gate = AR.take("gate", [8, T], BF16, reg=True)
            F0 = AR.take("F0", [2, 1024], F32, reg=True)
            tmpA = AR.take("tmpA", [4, 128], F32, reg=True)
            mark = AR.off
            G0 = AR.take("G0", [8, T], BF16, reg=True)
            G1 = AR.take("G1", [8, T], BF16, reg=True)
            G2 = AR.take("vbf", [nT, 1024], BF16, reg=True)
            G3 = AR.take("G3", [8, T], BF16, reg=True)
            hxT = AR.take("hxT", [4, T], BF16, reg=True)
            PT = AR.take("PT", [4, T], BF16, reg=True)

            def merge(first, last, c, ps_ap, k):
                if first:
                    STT(mT[:, c, 0:T], gate[:, c, :], 1.0, ps_ap, ALU.add, ALU.mult, [PK(k), ("gate", c)], [("mT", c)])
                else:
                    tmpm = F0[:, c % 2, 0:T]
                    STT(tmpm, gate[:, c, :], 1.0, ps_ap, ALU.add, ALU.mult, [PK(k), ("gate", c)], [("F0", c % 2)])
                    if last:
                        TT(mTb[:, c, 0:T], mT[:, c, 0:T], tmpm, ALU.add, [("F0", c % 2), ("mT", c)], [("mTb", c)])
                    else:
                        TT(mT[:, c, 0:T], mT[:, c, 0:T], tmpm, ALU.add, [("F0", c % 2), ("mT", c)], [("mT", c)])

            def gate_consumer(c, ps_ap, k):
                ACT(gate[:, c % 8, :], ps_ap, AF.Tanh, [PK(k)], [("gate", c % 8)], scale=0.5)

            sched = []

            vbf = G2
            acc_v = {}

            def a4_tile(i, kk):
                for half in range(2):
                    kb = kk[half]
                    sl = slice(half * 4, half * 4 + 4)
                    cols = slice(i * 128, (i + 1) * 128)
                    TT(tmpA, psb[kb][:].rearrange("p (g t) -> p g t", g=4), bspx[:, sl, :], ALU.add, [PK(kb), "bsp_bc", "bsp_s"], ["tmpA"])
                    TT(tmpA, tmpA, G0[:, sl, cols], ALU.mult, ["tmpA"] + [("G0", c) for c in range(half * 4, half * 4 + 4)], ["tmpA"])
                    TT(G3[:, sl, cols], tmpA, G1[:, sl, cols], ALU.mult, ["tmpA"] + [("G1", c) for c in range(half * 4, half * 4 + 4)],
                       [("G3", i)])

            var_all, var_keys = stat(nT)
            mean_t = {}

            def v_cons(i, col, n, ps_ap, k):
                sub = col // 512
                fk = [("F0", i % 2)]
                sa, sak = stat()
                ACT(F0[:, i % 2, col:col + n], ps_ap, AF.Gelu, [PK(k)], fk + sak, accum_out=sa)
                acc_v[(i, sub)] = (sa, sak)
                if sub == 1:
                    vt = F0[:, i % 2, :]
                    (s0, s0k), (s1, s1k) = acc_v[(i, 0)], acc_v[(i, 1)]
                    sq, sqk = stat()
                    ACT(junk[:], vt, AF.Square, fk, ["junk"] + sqk, accum_out=sq)
                    mean, mk_ = stat()
                    TS(mean, s0, s1, 1.0 / D, ALU.add, ALU.mult, s0k + s1k, mk_)
                    msq, msk = stat()
                    TT(msq, mean, mean, ALU.mult, mk_, msk)
                    TS(var_all[:, i:i + 1], sq, 1.0 / D, msq, ALU.mult, ALU.subtract, sqk + msk, [var_keys[i]])
                    mean_t[i] = (mean, mk_)
                    if i == nT - 1:
                        ACT(var_all, var_all, AF.Ln, var_keys, var_keys, bias=EPS)
                        ACT(var_all, var_all, AF.Exp, var_keys, var_keys, scale=-0.5)
                        for ii in range(nT):
                            v_norm(ii)

            def v_norm(i):
                fk = [("F0", i % 2)]
                vt = F0[:, i % 2, :]
                mean, mk_ = mean_t[i]
                TS(vt, vt, mean, var_all[:, i:i + 1], ALU.subtract, ALU.mult, fk + mk_ + var_keys, fk)
                TT(vt, vt, lng_bc[:], ALU.mult, fk + ["lng_bc"], fk)
                if samp:
                    TT(vt, vt, lnb_bc[:], ALU.add, fk + ["lnb_bc"], fk)
                    DMA(vs, vt, fk, [])
                    ACP(vbf[:, i, :], vt, fk, [("vbf", i)])
                else:
                    TT(vbf[:, i, :], vt, lnb_bc[:], ALU.add, fk + ["lnb_bc"], [("vbf", i)])

            def spatial_tile(i):
                kk = [BK.get(), BK.get()]
                for g in range(8):
                    kb = kk[g // 4]
                    MM(psb[kb][:, (g % 4) * 128:(g % 4 + 1) * 128], vbf[:, i, g * 128:(g + 1) * 128], WT[:, g, :], True, True,
                       [("vbf", i), "WTp", "WTs"], [PK(kb)])
                a4_tile(i, kk)

            def u_cons(c, ps_ap, k):
                ACT(G0[:, c, :], ps_ap, AF.Gelu, [PK(k)], [("G0", c)])

            def ga_cons(c, ps_ap, k):
                ACT(G1[:, c, :], ps_ap, AF.Silu, [PK(k)], [("G1", c)])

            def proj_generic(hTsrc, hkeys, KC, first, last):
                def f(wv, wk, c_lo, c_hi, col0):
                    for c in range(c_lo, c_hi):
                        k = BK.get()
                        for kc in range(KC):
                            MM(psb[k][:, 0:T], wv[:, kc, (c * 128 - col0):(c * 128 - col0) + 128], hTsrc[:, kc, 0:T], kc == 0, kc == KC - 1,
                               [wk] + hkeys, [PK(k)])
                        merge(first, last, c, psb[k][:, 0:T], k)
                return f

            sched.append((w_in[:, C_U:C_U + 1024], 8, 1024, lambda wv, wk: fm_block(wv, wk, 8, 0, u_cons)))
            sched.append((w_in[:, C_V:C_V + 1024], 8, 1024, lambda wv, wk: tm_block(wv, wk, 1024, 0, v_cons)))
            sched.append((w_in[:, C_GA:C_GA + 1024], 8, 1024, lambda wv, wk: fm_block(wv, wk, 8, 0, ga_cons)))
            def gate_a_consumer(c, ps_ap, k):
                gate_consumer(c, ps_ap, k)
                if nT == 2 and c in (3, 7):
                    spatial_tile(c // 4)
                elif nT == 1 and c == 7:
                    spatial_tile(0)

            sched.append((w_in[:, C_MG:C_MG + 1024], 8, 1024, lambda wv, wk: fm_block(wv, wk, 8, 0, gate_a_consumer)))
            pa = proj_generic(G3, [("G3", i) for i in range(nT)], 8, True, False)

            def proj_a_block(wv, wk):
                pa(wv, wk, 0, 8, 0)

            sched.append((wpa, 8, 1024, proj_a_block))

            qT = G0[:, 0:4, :]
            gx = G0[:, 4:8, :]
            rden = F0[:, 0, 0:T]
            tmpx = F0[:, 1, 0:T]
            SC = 128.0 ** -0.5

            def attn_head(h):
                par = h % 2
                ks_ = []
                for mt in range(2):
                    k = BK.get()
                    MM(psb[k][:, 0:T], KT[:, h, mt * 128:(mt + 1) * 128], qT[:, h, :], True, True, ["KT", ("G0", h)], [PK(k)])
                    ACT(PT[:, par * 2 + mt, :], psb[k][:, 0:T], AF.Exp, [PK(k)], [("PT", par, mt)], scale=SC)
                yield
                ko = BK.get()
                for mt in range(2):
                    MM(psb[ko][:, 0:T], Vt[:, mt, h * 128:(h + 1) * 128], PT[:, par * 2 + mt, :], mt == 0, mt == 1,
                       ["Vt", ("PT", par, mt)], [PK(ko)])
                kd = BK.get()
                for mt in range(2):
                    MM(psb[kd][:, 0:T], ones_b[:], PT[:, par * 2 + mt, :], mt == 0, mt == 1, ["ones_b", ("PT", par, mt)], [PK(kd)])
                P.add("dve", lambda e, kd=kd: e.reciprocal(rden, psb[kd][:, 0:T]), [PK(kd)], [("F0", 0)])
                TT(tmpx, psb[ko][:, 0:T], rden, ALU.mult, [PK(ko), ("F0", 0)], [("F0", 1)])
                TT(hxT[:, h, :], tmpx, gx[:, h, :], ALU.mult, [("F0", 1), ("G0", 4 + h)], [("hxT", h)])

            attpipe = Pipe()

            def attn_prompt():
                pass

            def attn_sample():
                ko = BK.get(pin=True)
                kd = BK.get(pin=True)
                KTs, Kc, Vc, PTs, rd4 = AR_s["KTs"], AR_s["Kc"], AR_s["Vc"], AR_s["PTs"], AR_s["rd4"]
                qk = [("G0", c) for c in range(4)]

                def chain(b):
                    j = b % 2
                    DMA(Kc[:, j], kvscr[0, b].rearrange("p (mt n) -> p mt n", mt=2), [("kvs", 0, b)], [("Kc", j)])
                    DMA(Vc[:, j], kvscr[1, b].rearrange("p (mt n) -> p mt n", mt=2), [("kvs", 1, b)], [("Vc", j)])
                    k = BK.get(pin=True)
                    pv = psb[k][:].bitcast(BF16)
                    for h in range(4):
                        for mt in range(2):
                            TR(pv[:, h * 256 + mt * 128: h * 256 + (mt + 1) * 128], Kc[:, j, mt, h * 128:(h + 1) * 128], ident_b,
                               [("Kc", j), "cbs"], [PK(k)])
                    yield
                    CP(KTs[:, j], pv.rearrange("p (h m) -> p h m", h=4), [PK(k)], [("KTs", j)])
                    BK.unpin(k)
                    yield
                    k2 = BK.get()
                    for h in range(4):
                        for mt in range(2):
                            MM(psb[k2][:, (h * 2 + mt) * 8:(h * 2 + mt + 1) * 8], KTs[:, j, h, mt * 128:(mt + 1) * 128],
                               qT[:, h, b * 8:(b + 1) * 8], True, True, [("KTs", j)] + qk, [PK(k2)])
                    ACT(PTs[:, j], psb[k2][:, 0:64], AF.Exp, [PK(k2)], [("PTs", j)], scale=SC)
                    for h in range(4):
                        for mt in range(2):
                            MM(psb[ko][:, h * 128 + b * 8: h * 128 + (b + 1) * 8], Vc[:, j, mt, h * 128:(h + 1) * 128],
                               PTs[:, j, (h * 2 + mt) * 8:(h * 2 + mt + 1) * 8], mt == 0, mt == 1, [("Vc", j), ("PTs", j)], [PK(ko)])
                    for h in range(4):
                        for mt in range(2):
                            MM(psb[kd][:, h * 128 + b * 8: h * 128 + (b + 1) * 8], ones_b[:],
                               PTs[:, j, (h * 2 + mt) * 8:(h * 2 + mt + 1) * 8], mt == 0, mt == 1, ["ones_b", ("PTs", j)], [PK(kd)])

                pp = Pipe()
                for b in range(NSEQ_S):
                    pp.tick(chain(b))
                pp.drain()
                P.add("dve", lambda e: e.reciprocal(rd4, psb[kd][:]), [PK(kd)], ["rd4"])
                TT(rd4, psb[ko][:], rd4, ALU.mult, [PK(ko), "rd4"], ["rd4"])
                TT(hxT.rearrange("p h t -> p (h t)"), rd4, gx.rearrange("p h t -> p (h t)"), ALU.mult,
                   ["rd4"] + [("G0", 4 + h) for h in range(4)], [("hxT", h) for h in range(4)])
                BK.unpin(ko)
                BK.unpin(kd)

            attn = attn_sample if samp else attn_prompt

            def qgx_cons(c, ps_ap, k):
                if c < 4:
                    CP(qT[:, c, :], ps_ap, [PK(k)], [("G0", c)])
                else:
                    ACT(gx[:, c - 4, :], ps_ap, AF.Silu, [PK(k)], [("G0", c)])
                    if c == 7:
                        attn()

            sched.append((w_in[:, C_Q:C_Q + 1024], 8, 1024, lambda wv, wk: fm_block(wv, wk, 8, 0, qgx_cons)))
            def gate_x_consumer(c, ps_ap, k):
                gate_consumer(c, ps_ap, k)
                if not samp:
                    if c % 2 == 0:
                        attpipe.tick(attn_head(c // 2))
                    if c == 7:
                        attpipe.drain()

            sched.append((w_in[:, C_MG + 2048:C_MG + 3072], 8, 1024, lambda wv, wk: fm_block(wv, wk, 8, 0, gate_x_consumer)))
            px = proj_generic(hxT, [("hxT", h) for h in range(4)], 4, False, False)
            sched.append((wpx, 4, 1024, lambda wv, wk: px(wv, wk, 0, 8, 0)))

            AR.off = mark
            zs = AR.take("zs", [nT, 2048], BF16, reg=True)
            xs_tm = AR.take("xs_tm", [nT, 2048], BF16, reg=True)
            BT = AR.take("BT", [8, T], BF16, reg=True)
            CT = AR.take("CT", [8, T], BF16, reg=True)
            B_tm = AR.take("B_tm", [nT, 1024], BF16, reg=True)
            hbT = AR.take("hbT", [16, T], BF16, reg=True)
            dtp = AR.take("dtp", [nT, 6, 32], F32, reg=True)
            XR = max(3 + T, 16 * 11)
            NBC, NBS = 4, 3
            xraw = AR.take("xraw", [NBC, XR], F32, reg=True)
            acc = AR.take("acc", [NBC, T], F32, reg=True)
            xsT = AR.take("xsT", [NBC, T], BF16, reg=True)
            x_dt = AR.take("x_dt", [2048], BF16, reg=True)
            xsD = AR.take("xsD", [2048], BF16, reg=True)
            xdd = AR.take("xdd", [2048], BF16, reg=True)
            cbm = AR.take("cbm", [2, 8, 128], BF16, reg=True)
            Rb = AR.take("R", [NBS, 512], BF16, reg=True)
            Eb = AR.take("E", [NBS, 512], BF16, reg=True)
            MTb_ = AR.take("MT", [NBS, 512], BF16, reg=True)
            yt = AR.take("yt", [NBS, 256], F32, reg=True)
            yn = AR.take("yn", [NBS, 256], BF16, reg=True)
            acs = AR.take("acs", [64], F32, reg=True)
            run_st.maxoff = max(getattr(run_st, "maxoff", 0), AR.off)
            if os.environ.get("KDEBUG") and (samp or tok0 == 0):
                print("ST", kind, "arena base", AR.base, "end", AR.off, "of", ARENA)

            def dt_cons(i, col, n, ps_ap, k):
                dt_, adt, ea, cd, ds, tmp = (dtp[:, i, j, :] for j in range(6))
                dk = lambda j: [("dtp", i, j)]
                TT(tmp, ps_ap, dtb_bc[:], ALU.add, [PK(k), "dtb_bc"], dk(5))
                ACT(tmp, tmp, AF.Exp, dk(5), dk(5))
                ACT(dt_, tmp, AF.Ln, dk(5), dk(0), bias=1.0)
                TT(adt, dt_, a_bc[:], ALU.mult, dk(0) + ["a_bc"], dk(1))
                k2 = BK.get()
                MM(psb[k2][:, 0:32], U_f, adt, True, True, ["cfs"] + dk(1), [PK(k2)])
                MM(psb[k2][:, 32:64], tot_f, adt, True, True, ["cfs", "ones_f"] + dk(1), [PK(k2)])
                CP(acs, psb[k2][:, 0:64], [PK(k2)], ["acs"])
                ACT(ea, acs[:, 0:32], AF.Exp, ["acs"], dk(2))
                ACT(cd, acs[:, 32:64], AF.Exp, ["acs"], dk(3))
                TT(tmp, acs[:, 32:64], acs[:, 0:32], ALU.subtract, ["acs"], dk(5))
                ACT(ds, tmp, AF.Exp, dk(5), dk(4))

            def z_cons_block(blk):
                def f(i, col, n, ps_ap, k):
                    ACT(zs[:, i, blk * 1024 + col: blk * 1024 + col + n], ps_ap, AF.Silu, [PK(k)], [("zs", i, blk)])
                return f

            S_ = NSEQ_S if samp else 1
            Lw = LS if samp else T
            Lx = Lw + 3

            convpipe = Pipe()
            ssdpipe = Pipe()

            def conv_chain(c, ps_ap, k):
                j = c % NBC
                xr = xraw[:, j, 0:S_ * Lx].rearrange("p (s l) -> p s l", s=S_)
                xk = ("xraw", j)
                if samp:
                    kst = BK.get()
                    cl = c % 8
                    TR(psb[kst][:, 0:48], AR_s["sc_tm"][0:48, cl * 128:(cl + 1) * 128], cfs[0:48, CF_ID:CF_ID + 48], ["sc_tm", "cfs"], [PK(kst)])
                    CP(xr[:, :, 0:3], psb[kst][:, 0:48].rearrange("p (b k) -> p b k", b=16), [PK(kst)], [xk])
                    ACP(xr[:, :, 3:Lx], ps_ap.rearrange("p (b l) -> p b l", b=16), [PK(k)], [xk])
                    CP(AR_s["xlast"][:, c], xr[:, :, Lw:Lx], [xk], [("xlast", c)])
                else:
                    PCP(xr[:, 0, 0:3], xhalo[:, c, :], [("xhalo", c)], [xk])
                    ACP(xr[:, 0, 3:Lx], ps_ap, [PK(k)], [xk])
                    PCP(xhalo[:, c, :], xr[:, 0, Lw:Lx], [xk], [("xhalo", c)])
                av = acc[:, j, 0:S_ * Lw].rearrange("p (s l) -> p s l", s=S_)
                ak = ("acc", j)
                ACT(av, xr[:, :, 3:3 + Lw], AF.Identity, [xk, "pfm"], [ak], scale=pfm[:, c, 3:4], bias=pfm[:, c, 4:5])
                yield
                for kk in (2, 1, 0):
                    STT(av, xr[:, :, kk:kk + Lw], pfm[:, c, kk:kk + 1], av, ALU.mult, ALU.add, [xk, ak, "pfm"], [ak])
                yield
                avf = acc[:, j, 0:T]
                if c < 24:
                    if c < 16:
                        dst = xsT[:, j, :]
                        dkey = ("xsT", j)
                    else:
                        dst = BT[:, c - 16, :]
                        dkey = ("BT", c - 16)
                    ACT(dst, avf, AF.Silu, [ak], [dkey])
                    kt = BK.get(pin=True)
                    pv = psb[kt][:].bitcast(BF16)
                    for i in range(nT):
                        TR(pv[:, i * 128:(i + 1) * 128], dst[:, i * 128:(i + 1) * 128], ident_b, [dkey, "cbs"], [PK(kt)])
                    yield
                    if c < 16:
                        ACP(xs_tm[:, :, c * 128:(c + 1) * 128], pv[:, 0:T].rearrange("p (i t) -> p i t", i=nT), [PK(kt)], [("xs_tm", c)])
                    else:
                        g = c - 16
                        ACP(B_tm[:, :, g * 128:(g + 1) * 128], pv[:, 0:T].rearrange("p (i t) -> p i t", i=nT), [PK(kt)], [("B_tm", g)])
                    BK.unpin(kt)
                else:
                    ACT(CT[:, c - 24, :], avf, AF.Silu, [ak], [("CT", c - 24)])

            chain_no = [0]

            def ssd_pre(i):
                cols = slice(i * 128, (i + 1) * 128)
                for half in range(2):
                    kcb = BK.get()
                    for gg in range(4):
                        g = half * 4 + gg
                        MM(psb[kcb][:, gg * 128:(gg + 1) * 128], BT[:, g, cols], CT[:, g, cols], True, True, [("BT", g), ("CT", g)], [PK(kcb)])
                    TT(cbm[:, i % 2, half * 4:half * 4 + 4, :], psb[kcb][:].rearrange("p (g l) -> p g l", g=4),
                       Umask_b[:, None, :].to_broadcast([128, 4, 128]), ALU.mult, [PK(kcb), "cbs"], [("cbm", i % 2, half)])

            def ssd_chain(i, g):
                j = chain_no[0] % NBS
                chain_no[0] += 1
                dt_, adt, ea, cd, ds, tmp = (dtp[:, i, q, :] for q in range(6))
                dk = lambda q: [("dtp", i, q)]
                cols = slice(i * 128, (i + 1) * 128)
                hs_ = slice(g * 256, (g + 1) * 256)
                h4 = slice(4 * g, 4 * g + 4)
                xkg = [("xs_tm", 2 * g), ("xs_tm", 2 * g + 1)]
                xs3 = xs_tm[:, i, hs_].rearrange("p (h q) -> p h q", h=4)
                v3 = lambda ap: ap[:, hs_].rearrange("p (h q) -> p h q", h=4)
                if not samp:
                    PTT(v3(x_dt), xs3, dt_[:, h4, None].to_broadcast([128, 4, 64]), ALU.mult, xkg + dk(0), [("x_dt", g)])
                    PTT(v3(xdd), v3(x_dt), ds[:, h4, None].to_broadcast([128, 4, 64]), ALU.mult, [("x_dt", g)] + dk(4), [("xdd", g)])
                PTT(v3(xsD), xs3, dsk_bc[:, h4, None].to_broadcast([128, 4, 64]), ALU.mult, xkg + ["dsk_bc"], [("xsD", g)])
                TT(Rb[:, j, :].rearrange("p (h l) -> p h l", h=4), adt[:, h4, None].to_broadcast([128, 4, 128]),
                   Umask_b[:, None, :].to_broadcast([128, 4, 128]), ALU.mult, dk(1) + ["cbs"], [("R", j)])
                yield
                kD = BK.get()
                MM(psb[kD][:], Lmask_b, Rb[:, j, :], True, True, ["cbs", ("R", j)], [PK(kD)])
                ACT(Eb[:, j, :], psb[kD][:], AF.Exp, [PK(kD)], [("E", j)])
                yield
                TT(MTb_[:, j, :].rearrange("p (h l) -> p h l", h=4), Eb[:, j, :].rearrange("p (h l) -> p h l", h=4),
                   cbm[:, i % 2, g, None, :].to_broadcast([128, 4, 128]), ALU.mult, [("E", j), ("cbm", i % 2, g // 4)], [("MT", j)])
                yield
                ky = BK.get(pin=True)
                MM(psb[ky][:, 0:256], ident_b, xsD[:, hs_], True, False, ["cbs", ("xsD", g)], [PK(ky)])
                for hh in range(4):
                    MM(psb[ky][:, hh * 64:(hh + 1) * 64], MTb_[:, j, hh * 128:(hh + 1) * 128], x_dt[:, g * 256 + hh * 64: g * 256 + (hh + 1) * 64],
                       False, hh == 3, [("MT", j), ("x_dt", g)], [PK(ky)])
                if not samp:
                    MM(psb[ky][:, 256:512], CT[:, g, cols], hTb[:, hs_], True, True, [("CT", g), ("hTb", g)], [PK(ky)])
                yield
                yk = ("yt", j)
                if samp:
                    TT(yt[:, j, :], psb[ky][:, 0:256], AR_s["yoff"][:, hs_], ALU.add, [PK(ky), "yoff"], [yk])
                else:
                    y3 = yt[:, j, :].rearrange("p (h q) -> p h q", h=4)
                    TT(y3, psb[ky][:, 256:512].rearrange("p (h q) -> p h q", h=4), ea[:, h4, None].to_broadcast([128, 4, 64]),
                       ALU.mult, [PK(ky)] + dk(2), [yk])
                    TT(yt[:, j, :], yt[:, j, :], psb[ky][:, 0:256], ALU.add, [yk, PK(ky)], [yk])
                TT(yt[:, j, :], yt[:, j, :], zs[:, i, hs_], ALU.mult, [yk, ("zs", i, g // 4)], [yk])
                BK.unpin(ky)
                yield
                ss, ssk = stat()
                ACT(junk[:, 0:256], yt[:, j, :], AF.Square, [yk], ["junk"] + ssk, accum_out=ss)
                r, rk = stat()
                rstd_from_ss(ss, ssk, 256, r, rk)
                ACT(yn[:, j, :], yt[:, j, :], AF.Identity, [yk] + rk, [("yn", j)], scale=r)
                yield
                kt = BK.get(pin=True)
                pv = psb[kt][:].bitcast(BF16)
                for q in range(2):
                    TR(pv[:, q * 128:(q + 1) * 128], yn[:, j, q * 128:(q + 1) * 128], ident_b, [("yn", j), "cbs"], [PK(kt)])
                if not samp:
                    ks = BK.get(pin=True)
                    MM(psb[ks][:, 0:256], B_tm[:, i, g * 128:(g + 1) * 128], xdd[:, hs_], True, True, [("B_tm", g), ("xdd", g)], [PK(ks)])
                yield
                TT(hbT[:, 2 * g:2 * g + 2, cols], pv[:, 0:256].rearrange("p (q t) -> p q t", q=2),
                   ssdg[:, 2 * g:2 * g + 2, None].to_broadcast([128, 2, 128]), ALU.mult, [PK(kt), "pfm"], [("hbT", i)])
                if not samp:
                    h3 = hT[:, hs_].rearrange("p (h q) -> p h q", h=4)
                    TT(h3, h3, cd[:, h4, None].to_broadcast([128, 4, 64]), ALU.mult, [("hT", g)] + dk(3), [("hT", g)])
                    TT(hT[:, hs_], hT[:, hs_], psb[ks][:, 0:256], ALU.add, [("hT", g), PK(ks)], [("hT", g)])
                    ACP(hTb[:, hs_], hT[:, hs_], [("hT", g)], [("hTb", g)])
                    BK.unpin(ks)
                BK.unpin(kt)

            def sample_states():
                h0n, h0T, Bm, cdn, cdx, CTm, yoff = (AR_s[n] for n in ("h0n", "h0T", "Bm", "cdn", "cdx", "CTm", "yoff"))
                cd = dtp[:, 0, 3, :]
                ea = dtp[:, 0, 2, :]
                ds = dtp[:, 0, 4, :]
                dt_ = dtp[:, 0, 0, :]
                sel = cfs[:, CF_RS:CF_RS + 16]
                kc_ = BK.get(pin=True)
                for jj in range(16):
                    CP(cdx[:, jj % 2], cd[:, 2 * jj:2 * jj + 2, None].to_broadcast([128, 2, 64]), [("dtp", 0, 3)], [("cdx", jj % 2)])
                    MM(psb[kc_][:, jj * 16:(jj + 1) * 16], cdx[:, jj % 2].rearrange("p a b -> p (a b)"), sel, True, True,
                       [("cdx", jj % 2), "cfs"], [PK(kc_)])
                CP(cdn, psb[kc_][:, 0:256], [PK(kc_)], ["cdn"])
                BK.unpin(kc_)
                cdn3 = cdn.rearrange("p (j b) -> p j b", j=16)
                xs3 = xs_tm[:, 0, :].rearrange("p (h q) -> p h q", h=32)
                xk = [("xs_tm", c) for c in range(16)]
                TT(x_dt.rearrange("p (h q) -> p h q", h=32), xs3, dt_[:, :, None].to_broadcast([128, 32, 64]), ALU.mult, xk + [("dtp", 0, 0)], [("x_dt", g) for g in range(8)])
                TT(xdd.rearrange("p (h q) -> p h q", h=32), x_dt.rearrange("p (h q) -> p h q", h=32),
                   ds[:, :, None].to_broadcast([128, 32, 64]), ALU.mult, [("x_dt", g) for g in range(8)] + [("dtp", 0, 4)], [("xdd", g) for g in range(8)])
                kyo = [BK.get(pin=True) for _ in range(4)]

                def schain(b):
                    j = b % 2
                    hb_ = h0n[j]
                    hkq = lambda q4: ("h0n%d" % j, q4)
                    src3 = sst[b].rearrange("(j p) n -> p j n", p=128)
                    dst3 = hs[b].rearrange("(j p) n -> p j n", p=128)
                    for q4 in range(4):
                        DMA(hb_[:, q4 * 4:(q4 + 1) * 4, :], src3[:, q4 * 4:(q4 + 1) * 4, :], (), [hkq(q4)])
                        k = BK.get()
                        for cc in range(4):
                            jj = q4 * 4 + cc
                            TR(psb[k][:, cc * 128:(cc + 1) * 128], hb_[:, jj, :], ident_f, [hkq(q4), "cfs"], [PK(k)])
                        ACP(h0T[:, q4 * 512:(q4 + 1) * 512], psb[k][:], [PK(k)], [("h0T", q4)])
                    TT(CTm, CT[:, :, 0:128], cbs[:, None, CB_BD + b * 128: CB_BD + (b + 1) * 128].to_broadcast([128, 8, 128]), ALU.mult,
                       [("CT", g) for g in range(8)] + ["cbs"], ["CTm"])
                    TT(Bm[:, j], B_tm[:, 0, :], rowmask[:, b:b + 1].to_broadcast([128, 1024]), ALU.mult,
                       [("B_tm", g) for g in range(8)] + ["cfs"], [("Bm", j)])
                    yield
                    for g in range(8):
                        kq = kyo[g // 2]
                        MM(psb[kq][:, (g % 2) * 256:(g % 2 + 1) * 256], CTm[:, g, :], h0T[:, g * 256:(g + 1) * 256],
                           b == 0 and g % 2 == 0, b == NSEQ_S - 1 and g % 2 == 1, ["CTm", ("h0T", g // 2)], [PK(kq)])
                    yield
                    for q4 in range(4):
                        k = BK.get()
                        for cc in range(4):
                            jj = q4 * 4 + cc
                            g = jj // 2
                            MM(psb[k][:, cc * 128:(cc + 1) * 128], xdd[:, jj * 128:(jj + 1) * 128], Bm[:, j, g * 128:(g + 1) * 128], True, True,
                               [("xdd", g), ("Bm", j)], [PK(k)])
                        hv = hb_[:, q4 * 4:(q4 + 1) * 4, :]
                        (PTT if q4 % 2 == 0 else TT)(hv, hv, cdn3[:, q4 * 4:(q4 + 1) * 4, b:b + 1].to_broadcast([128, 4, 128]), ALU.mult,
                                                      [hkq(q4), "cdn"], [hkq(q4)])
                        TT(hv, hv, psb[k][:].rearrange("p (c n) -> p c n", c=4), ALU.add, [hkq(q4), PK(k)], [hkq(q4)])
                        DMA(dst3[:, q4 * 4:(q4 + 1) * 4, :], hv, [hkq(q4)], [hkq(q4)])

                sp_ = Pipe()
                for b in range(NSEQ_S):
                    sp_.tick(schain(b))
                sp_.drain()
                for q in range(4):
                    TT(yoff[:, q * 512:(q + 1) * 512].rearrange("p (h q) -> p h q", h=8), psb[kyo[q]][:].rearrange("p (h q) -> p h q", h=8),
                       ea[:, 8 * q:8 * q + 8, None].to_broadcast([128, 8, 64]), ALU.mult, [PK(kyo[q]), ("dtp", 0, 2)], ["yoff"])
                    BK.unpin(kyo[q])

            def c_last_cons(c, ps_ap, k):
                convpipe.tick(conv_chain(c, ps_ap, k))
                if c == 31:
                    convpipe.drain()
                    if samp:
                        sample_states()
                    for i in range(nT):
                        ssd_pre(i)
                        for g in range(8):
                            ssdpipe.tick(ssd_chain(i, g))
                    ssdpipe.drain()

            def first_b(wv, wk):
                tm_block(wv, wk, 32, 0, dt_cons)

            def xbc_block(blk):
                def f(wv, wk):
                    if samp:
                        DMA(AR_s["sc_tm"][0:48, :], scv[:, blk * 1024:(blk + 1) * 1024], (), ["sc_tm"])
                    fm_block(wv, wk, 8, blk * 8, c_last_cons)
                return f

            sched.append((w_in[:, C_DT:C_DT + 32], 8, 32, first_b))
            sched.append((w_in[:, C_Z:C_Z + 1024], 8, 1024, lambda wv, wk: tm_block(wv, wk, 1024, 0, z_cons_block(0))))
            sched.append((w_in[:, C_Z + 1024:C_Z + 2048], 8, 1024, lambda wv, wk: tm_block(wv, wk, 1024, 0, z_cons_block(1))))
            for blk in range(4):
                sched.append((w_in[:, C_XBC + blk * 1024:C_XBC + (blk + 1) * 1024], 8, 1024, xbc_block(blk)))
            def gate_b_consumer(c, ps_ap, k):
                gate_consumer(c, ps_ap, k)

            sched.append((w_in[:, C_MG + 1024:C_MG + 2048], 8, 1024, lambda wv, wk: fm_block(wv, wk, 8, 0, gate_b_consumer)))
            pb = proj_generic(hbT, [("hbT", i) for i in range(nT)], 16, False, True)
            sched.append((wpb[:, 0:512], 16, 512, lambda wv, wk: pb(wv, wk, 0, 4, 0)))
            sched.append((wpb[:, 512:1024], 16, 512, lambda wv, wk: pb(wv, wk, 4, 8, 512)))

            yres = F0

            def out_block(wv, wk):
                for i in range(nT):
                    j = i % 2
                    DMA(xt[:, j, :], tile_rows(xsrc, i), (), [("xt", j)])
                    for half in range(2):
                        k = BK.get()
                        for kc in range(8):
                            MM(psb[k][:], mTb[:, kc, i * 128:(i + 1) * 128], wv[:, kc, half * 512:(half + 1) * 512], kc == 0, kc == 7,
                               [wk] + [("mTb", c) for c in range(8)], [PK(k)])
                        STT(yres[:, j, half * 512:(half + 1) * 512], psb[k][:], 0.5, xt[:, j, half * 512:(half + 1) * 512], ALU.mult, ALU.add,
                            [PK(k), ("xt", j)], [("F0", j)])
                    ss, ssk = stat()
                    ACT(junk[:], yres[:, j, :], AF.Square, [("F0", j)], ["junk"] + ssk, accum_out=ss)
                    r, rk = stat()
                    rstd_from_ss(ss, ssk, D, r, rk)
                    ACT(yres[:, j, :], yres[:, j, :], AF.Identity, [("F0", j)] + rk, [("F0", j)], scale=r)
                    TT(yres[:, j, :], yres[:, j, :], fg_bc[:], ALU.mult, [("F0", j), "fg_bc"], [("F0", j)])
                    DMA(tile_rows(ydst, i), yres[:, j, :], [("F0", j)], [("F0", j)], q="pool")

            sched.append((wout, 8, 1024, out_block))

            if not cached:
                for bi, (src_, kc_, ncols_, _fn) in enumerate(sched):
                    n_ = kc_ * ncols_
                    DMA(wscr[bi][:, 0:n_].rearrange("p (kc n) -> p kc n", kc=kc_), src_.rearrange("(kc p) n -> p kc n", p=128),
                        (), [("scr", bi)], q="pool")
                    cached.add(bi)
            if (not samp) and tok0 == 3 * T:
                for b_ in range(NSEQ_S):
                    for t_, src_ in ((0, ck), (1, cv)):
                        DMA(kvscr[t_, b_].rearrange("p (mt n) -> p mt n", mt=2),
                            src_[b_].rearrange("(mt p) n -> p mt n", p=128), (), [("kvs", t_, b_)], q="pool")
            nxt = load_w(sched[0][0], sched[0][1], sched[0][2], 0)
            for si, (src, kc, ncols, fn) in enumerate(sched):
                cur = nxt
                if si + 1 < len(sched):
                    nxt = load_w(sched[si + 1][0], sched[si + 1][1], sched[si + 1][2], si + 1)
                fn(cur[0], cur[1])

        ARp = Arena("p")
        hT = ARp.take("hT", [2048], F32)
        hTb = ARp.take("hTb", [2048], BF16)
        KT = ARp.take("KT", [4, 256], BF16)
        Vt = ARp.take("Vt", [2, 512], BF16)
        ARp.base = ARp.off
        MSET(hT, 0.0, [("hT", g) for g in range(8)])
        MSET(hTb, 0.0, [("hTb", g) for g in range(8)])
        memT = ARp.take("memT", [8, 256], BF16, reg=True)
        kvf = ARp.take("kvf", [2, 512], F32, reg=True)
        for mt in range(2):
            norm_transpose(mem[mt * 128:(mt + 1) * 128, :], xt[:, mt, :], ("xt", mt), mgfm, ["pfm"],
                           memT[:, :, mt * 128:(mt + 1) * 128], [("memT", mt)])
        if SUB <= 1:
            P.barrier()
            P.emit()
            return nc
        wv, wk = load_w(wkv, 8, 1024)
        if SUB <= 2:
            P.barrier()
            P.emit()
            return nc
        for mt in range(2):
            for half in range(2):
                k = BK.get()
                for kc in range(8):
                    MM(psb[k][:], memT[:, kc, mt * 128:(mt + 1) * 128], wv[:, kc, half * 512:(half + 1) * 512], kc == 0, kc == 7,
                       [wk, ("memT", mt)], [PK(k)])
                if "nocp" not in KVAR:
                    CP(kvf[:, half, :], psb[k][:], [PK(k)], [("kvf", half)])
                if "nodma" not in KVAR:
                    DMA((mk if half == 0 else mv)[mt * 128:(mt + 1) * 128, :], kvf[:, half, :], [("kvf", half)], [("kvf", half)])
                if half == 1 and "noacp" not in KVAR:
                    ACP(Vt[:, mt, :], psb[k][:], [PK(k)] + ([("kvf", half)] if "acpdep" in KVAR else []), ["Vt"])
        if SUB <= 3:
            P.barrier()
            P.emit()
            return nc
        for h in range(4):
            k = BK.get()
            for kc in range(8):
                MM(psb[k][:, 0:256], wv[:, kc, h * 128:(h + 1) * 128], memT[:, kc, :], kc == 0, kc == 7,
                   [wk, ("memT", 0), ("memT", 1)], [PK(k)])
            CP(KT[:, h, :], psb[k][:, 0:256], [PK(k)], ["KT"])
        if STOP <= 2:
            P.emit()
            return nc

        NT_P = 2
        for st in range(SEQ // (128 * NT_P)):
            run_st("p", st * 128 * NT_P, NT_P, ARp, hT, hTb, KT, Vt, None)
            if STOP <= 3:
                P.emit()
                return nc
        P.barrier()
        ARp.off = ARp.base
        hpo = ARp.take("hpo", [16, 128], F32)
        rows_p = ARp.take("rows_p", [1024], F32)
        for q4 in range(4):
            k = BK.get()
            for cc in range(4):
                jj = q4 * 4 + cc
                TR(psb[k][:, cc * 128:(cc + 1) * 128], hT[:, jj * 128:(jj + 1) * 128], ident_f, [("hT", g) for g in range(8)] + ["cfs"], [PK(k)])
            CP(hpo[:, q4 * 4:(q4 + 1) * 4, :], psb[k][:].rearrange("p (c n) -> p c n", c=4), [PK(k)], [("hpo", q4)])
        DMA(hp.rearrange("(j p) n -> p j n", p=128), hpo, [("hpo", q4) for q4 in range(4)], [])
        fm_to_rows(lambda c: xhalo[:, c, :], 3, cpo, "cp", [("xhalo", c) for c in range(32)], rows_p)
        P.barrier()
        if STOP <= 4:
            P.emit()
            return nc

        ARs = Arena("s")
        AR_s = {}
        AR_s["sc_tm"] = ARs.take("sc_tm", [1024], F32)
        AR_s["xlast"] = ARs.take("xlast", [32, 16, 3], F32)
        AR_s["bsp_s"] = ARs.take("bsp_s", [8, 128], F32)
        AR_s["CTm"] = ARs.take("CTm", [8, 128], BF16)
        AR_s["h0T"] = ARs.take("h0T", [2048], BF16)
        h0n0 = ARs.take("h0n0", [16, 128], F32)
        AR_s["Bm"] = ARs.take("Bm", [2, 1024], BF16)
        AR_s["cdx"] = ARs.take("cdx", [2, 2, 64], F32)
        mark_s = ARs.off
        AR_s["KTs"] = ARs.take("KTs", [2, 4, 256], BF16, reg=True)
        AR_s["Kc"] = ARs.take("Kc", [2, 2, 512], BF16, reg=True)
        AR_s["Vc"] = ARs.take("Vc", [2, 2, 512], BF16, reg=True)
        AR_s["PTs"] = ARs.take("PTs", [2, 64], BF16, reg=True)
        AR_s["rd4"] = ARs.take("rd4", [512], F32, reg=True)
        off_att = ARs.off
        ARs.off = mark_s
        AR_s["yoff"] = ARs.take("yoff", [2048], F32, reg=True)
        AR_s["cdn"] = ARs.take("cdn", [256], F32, reg=True)
        h0n1 = ARs.take("h0n1", [16, 128], F32, reg=True)
        AR_s["h0n"] = [h0n0, h0n1]
        ARs.off = max(ARs.off, off_att)
        rows_s = AR_s["sc_tm"]
        ARs.base = ARs.off
        for g in range(8):
            CP(AR_s["bsp_s"][:, g, :].rearrange("p (b l) -> p b l", b=16), bsp_bc[:, g, None, 0:8].to_broadcast([128, 16, 8]),
               ["bsp_bc"], ["bsp_s"])
        run_st("s", 0, 1, ARs, None, None, None, None, AR_s)
        P.barrier()
        fm_to_rows(lambda c: AR_s["xlast"][:, c].rearrange("p b k -> p (b k)"), 48, cso, "cs", [("xlast", c) for c in range(32)], rows_s)

        P.emit()
    return nc


_NC_CACHE = {}


def kernel(x_prompt, x_sample, mem_prompt, cache_mem_k, cache_mem_v, state_ssm, state_conv, norm_g, w_in,
           conv_w, conv_b, dt_bias, a_log, d_skip, ssd_norm_g, ln_v_g, ln_v_b, w_spatial, b_spatial,
           mem_norm_g, w_mem_kv, w_proj_a, w_proj_b, w_proj_x, w_out, final_norm_g):
    f = lambda a: np.ascontiguousarray(np.asarray(a, dtype=np.float32))
    if "nc" not in _NC_CACHE:
        _NC_CACHE["nc"] = build_nc()
    nc = _NC_CACHE["nc"]
    cf, cb = _consts()
    pstage = np.zeros((8, 4096), np.float32)
    pstage[0:4] = f(conv_w)[0]
    pstage[4] = f(conv_b)[0]
    pstage[5, 0:2048] = f(ssd_norm_g)[0]
    pstage[5, 2048:3072] = f(norm_g)[0]
    pstage[5, 3072:4096] = f(mem_norm_g)[0]
    shared = dict(
        pstage=pstage, w_in=f(w_in)[0], dt_bias=f(dt_bias).reshape(1, 32), a_log=f(a_log).reshape(1, 32),
        d_skip=f(d_skip).reshape(1, 32), ln_v_g=f(ln_v_g).reshape(1, D), ln_v_b=f(ln_v_b).reshape(1, D),
        final_norm_g=f(final_norm_g).reshape(1, D), w_spatial=f(w_spatial)[0], b_spatial=f(b_spatial).reshape(1, 1024),
        w_mem_kv=f(w_mem_kv)[0], w_proj_a=f(w_proj_a)[0], w_proj_b=f(w_proj_b)[0], w_proj_x=f(w_proj_x)[0],
        w_out=f(w_out)[0], cf=cf, cb=cb,
    )
    xp_, xs_, mem_ = f(x_prompt), f(x_sample), f(mem_prompt)
    ck_, cv_, sst_, scv_ = f(cache_mem_k)[0], f(cache_mem_v)[0], f(state_ssm)[0], f(state_conv)[0]
    in_maps = []
    for c in range(NCORES):
        sl = slice(c * 16, (c + 1) * 16)
        m = dict(shared)
        m["xp"] = xp_[c]
        m["xsm"] = xs_[sl].reshape(128, D)
        m["mem"] = mem_[c]
        m["ck"] = ck_[sl].reshape(16, 256, 512)
        m["cv"] = cv_[sl].reshape(16, 256, 512)
        m["sst"] = sst_[sl].reshape(16, 2048, 128)
        m["scv"] = scv_[sl].reshape(48, 4096)
        in_maps.append(m)
    res = run_bass_kernel_spmd(nc, in_maps, core_ids=list(range(NCORES)))
    R = res.results
    y_prompt = np.stack([R[c]["yp"] for c in range(NCORES)]).reshape(8, SEQ, D)
    y_sample = np.concatenate([R[c]["ys"].reshape(16, 8, D) for c in range(NCORES)], 0)
    mk = np.stack([R[c]["mk"].reshape(256, 4, 128) for c in range(NCORES)])[None]
    mv = np.stack([R[c]["mv"].reshape(256, 4, 128) for c in range(NCORES)])[None]
    hpo = np.stack([R[c]["hp"].reshape(32, 64, 128) for c in range(NCORES)])[None]
    cpo = np.stack([R[c]["cp"] for c in range(NCORES)])[None]
    hso = np.concatenate([R[c]["hs"].reshape(16, 32, 64, 128) for c in range(NCORES)], 0)[None]
    cso = np.concatenate([R[c]["cs"].reshape(16, 3, 4096) for c in range(NCORES)], 0)[None]
    vso = np.concatenate([R[c]["vs"].reshape(16, 8, D) for c in range(NCORES)], 0)[None]
    return (y_prompt.astype(np.float32), y_sample.astype(np.float32), mk.astype(np.float32), mv.astype(np.float32),
            hpo.astype(np.float32), cpo.astype(np.float32), hso.astype(np.float32), cso.astype(np.float32),
            vso.astype(np.float32))
```

```python
import os
import numpy as np
from contextlib import ExitStack
import concourse.bass as bass
import concourse.mybir as mybir
from concourse.bass_utils import run_bass_kernel_spmd

F32 = mybir.dt.float32
BF16 = mybir.dt.bfloat16
AF = mybir.ActivationFunctionType
ALU = mybir.AluOpType

NCORES = 8
STOP = int(os.environ.get("KSTOP", "99"))
SUB = int(os.environ.get("KSUB", "99"))
KVAR = os.environ.get("KVAR", "")
D = 1024
IN_DIM = 13344
NH = 32
EPS = 1e-6
C_U, C_V, C_GA, C_Z, C_XBC, C_DT, C_Q, C_GX, C_MG = 0, 1024, 2048, 3072, 5120, 9216, 9248, 9760, 10272
SEQ = 2048
NSEQ_S = 16
LS = 8

ENGS = ("pe", "act", "dve", "pool", "sp")


class Op:
    __slots__ = ("eng", "fn", "deps", "dma", "signal", "sem", "val", "prewait")

    def __init__(self, eng, fn, deps, dma):
        self.eng = eng
        self.fn = fn
        self.deps = deps
        self.dma = dma
        self.signal = dma
        self.sem = None
        self.val = None
        self.prewait = None


class Prog:
    def __init__(self, nc, n_dma_sems=8):
        self.nc = nc
        self.ops = []
        self.last_w = {}
        self.readers = {}
        self.n_dma_sems = n_dma_sems
        self.bar_start = 0
        self.bufs = {}
        self.live = {}

    def register(self, name, start, end):
        old = self.bufs.get(name)
        if old is not None and old != (start, end) and name in self.live:
            raise RuntimeError(f"buffer {name} re-registered with a new range while live")
        self.bufs[name] = (start, end)

    def _touch(self, key, idx, eng, dma, deps):
        bname = key if isinstance(key, str) else key[0]
        rng = self.bufs.get(bname)
        if rng is None:
            return
        if bname not in self.live:
            s, e = rng
            for other in list(self.live):
                os_, oe = self.bufs[other]
                if os_ < e and s < oe:
                    acc = self.live.pop(other)
                    for d in list(acc["eng"].values()) + acc["dma"]:
                        o = self.ops[d]
                        if o.eng == eng and eng == "pe" and not o.dma and not dma:
                            continue
                        deps.add(d)
            self.live[bname] = {"eng": {}, "dma": []}
        a = self.live[bname]
        if dma:
            a["dma"].append(idx)
        else:
            a["eng"][eng] = idx

    def add(self, eng, fn, reads=(), writes=(), dma=False):
        idx = len(self.ops)
        deps = set()
        ps_r = [k for k in reads if isinstance(k, tuple) and k[0] == "ps"]
        if ps_r:
            reads = [k for k in reads if not (isinstance(k, tuple) and k[0] == "ps")]
            writes = list(writes) + ps_r

        def consider(d, raw):
            o = self.ops[d]
            if o.eng == eng and not o.dma and not dma:
                if eng == "pe":
                    return
            deps.add(d)

        for k in reads:
            w = self.last_w.get(k)
            if w is not None:
                consider(w, True)
        for k in writes:
            w = self.last_w.get(k)
            if w is not None:
                consider(w, False)
            for r in self.readers.get(k, ()):
                consider(r, False)
        for k in writes:
            self.last_w[k] = idx
            self.readers[k] = []
        for k in reads:
            self.readers.setdefault(k, []).append(idx)
        for k in list(reads) + list(writes):
            self._touch(k, idx, eng, dma, deps)
        deps.discard(idx)
        best = {}
        red = set()
        for d in deps:
            o = self.ops[d]
            if o.dma:
                red.add(d)
            elif best.get(o.eng, -1) < d:
                best[o.eng] = d
        red.update(best.values())
        deps = red
        for d in deps:
            self.ops[d].signal = True
        self.ops.append(Op(eng, fn, deps, dma))
        return idx

    def barrier(self):
        pend = range(self.bar_start, len(self.ops))
        last = {}
        dmas = []
        for d in pend:
            o = self.ops[d]
            if o.dma:
                dmas.append(d)
            else:
                last[o.eng] = d
        for e in ENGS:
            deps = set(dmas)
            for e2, d in last.items():
                if e2 != e:
                    deps.add(d)
            for d in deps:
                self.ops[d].signal = True
            self.ops.append(Op(e, lambda eng: eng.nop(), deps, False))
        self.bar_start = len(self.ops)
        self.last_w.clear()
        self.readers.clear()
        self.live.clear()
        self.bufs.clear()

    def emit(self):
        nc = self.nc
        with ExitStack() as es:
            csem = {e: es.enter_context(nc.semaphore(f"c_{e}")) for e in ENGS}
            dsem = {
                e: [es.enter_context(nc.semaphore(f"d_{e}{i}")) for i in range(self.n_dma_sems)]
                for e in ("sp", "act", "pool")
            }
            ccount = {e: 0 for e in ENGS}
            dcount = {e: [0] * self.n_dma_sems for e in dsem}
            drr = {e: 0 for e in dsem}
            for o in self.ops:
                if o.dma:
                    i = drr[o.eng]
                    drr[o.eng] = (i + 1) % self.n_dma_sems
                    o.sem = dsem[o.eng][i]
                    o.prewait = dcount[o.eng][i]
                    dcount[o.eng][i] += 16
                    o.val = dcount[o.eng][i]
                elif o.signal:
                    ccount[o.eng] += 1
                    o.sem = csem[o.eng]
                    o.val = ccount[o.eng]
            ops = self.ops
            if os.environ.get("KDEBUG"):
                print("PROG ops", len(ops), "compute signals", ccount, "dma counts", dcount)
            final_waits = [
                (dsem[e][i], dcount[e][i]) for e in dsem for i in range(self.n_dma_sems) if dcount[e][i] > 0
            ]

            def run_engine(ename, eng):
                seen = {}
                semobj = {}
                for o in ops:
                    if o.eng != ename:
                        continue
                    need = {}
                    for d in o.deps:
                        od = ops[d]
                        k = id(od.sem)
                        semobj[k] = od.sem
                        if need.get(k, 0) < od.val:
                            need[k] = od.val
                    if o.dma and o.prewait > 0:
                        k = id(o.sem)
                        semobj[k] = o.sem
                        if need.get(k, 0) < o.prewait:
                            need[k] = o.prewait
                    for k, v in need.items():
                        if seen.get(k, 0) >= v:
                            continue
                        seen[k] = v
                        eng.wait_ge(semobj[k], v)
                    inst = o.fn(eng)
                    if o.dma:
                        inst.then_inc(o.sem, 16)
                    elif o.signal:
                        inst.then_inc(o.sem, 1)
                if ename == "sp":
                    for s, v in final_waits:
                        eng.wait_ge(s, v)

            with nc.Block() as block:

                @block.sync
                def _(e):
                    run_engine("sp", e)

                @block.tensor
                def _(e):
                    run_engine("pe", e)

                @block.scalar
                def _(e):
                    run_engine("act", e)

                @block.vector
                def _(e):
                    run_engine("dve", e)

                @block.gpsimd
                def _(e):
                    run_engine("pool", e)


class Pipe:
    def __init__(self):
        self.active = []

    def tick(self, gen=None):
        if gen is not None:
            self.active.append(gen)
        for g in list(self.active):
            try:
                next(g)
            except StopIteration:
                self.active.remove(g)

    def drain(self):
        while self.active:
            self.tick()


CF_ID, CF_UP, CF_US, CF_ONES, CF_RM, CF_RS = 0, 128, 256, 384, 512, 528
NCF = 528 + 16
CB_ID, CB_UP, CB_LP, CB_US, CB_LS, CB_E, CB_BD = 0, 128, 256, 384, 512, 640, 768
NCB = 768 + 2048


def _consts():
    t = np.arange(128)
    blk = t // LS
    same = (blk[:, None] == blk[None, :]).astype(np.float32)
    U_p = (t[:, None] <= t[None, :]).astype(np.float32)
    L_p = (t[:, None] > t[None, :]).astype(np.float32)
    U_s = U_p * same
    L_s = L_p * same
    cf = np.zeros((128, NCF), np.float32)
    cf[:, CF_ID:CF_ID + 128] = np.eye(128)
    cf[:, CF_UP:CF_UP + 128] = U_p
    cf[:, CF_US:CF_US + 128] = U_s
    cf[:, CF_ONES:CF_ONES + 128] = same
    rm = (blk[:, None] == np.arange(16)[None, :]).astype(np.float32)
    cf[:, CF_RM:CF_RM + 16] = rm
    rs = np.zeros((128, 16), np.float32)
    for b in range(16):
        rs[8 * b, b] = 1.0
    cf[:, CF_RS:CF_RS + 16] = rs
    cb = np.zeros((128, NCB), np.float32)
    cb[:, CB_ID:CB_ID + 128] = np.eye(128)
    cb[:, CB_UP:CB_UP + 128] = U_p
    cb[:, CB_LP:CB_LP + 128] = L_p
    cb[:, CB_US:CB_US + 128] = U_s
    cb[:, CB_LS:CB_LS + 128] = L_s
    E = np.zeros((128, 128), np.float32)
    for s in range(8):
        E[s, :] = (t % 8 == s)
    cb[:, CB_E:CB_E + 128] = E
    bd = np.zeros((128, 16, 128), np.float32)
    for b in range(16):
        bd[:, b, 8 * b:8 * b + 8] = 1.0
    cb[:, CB_BD:CB_BD + 2048] = bd.reshape(128, 2048)
    return cf, cb


def build_nc():
    nc = bass.Bass("TRN2", target_bir_lowering=False)

    def din(name, shape):
        return nc.dram_tensor(name, list(shape), F32, kind="ExternalInput").ap()

    def dout(name, shape):
        return nc.dram_tensor(name, list(shape), F32, kind="ExternalOutput").ap()

    xp = din("xp", [SEQ, D])
    xsm = din("xsm", [128, D])
    mem = din("mem", [256, D])
    ck = din("ck", [16, 256, 512])
    cv = din("cv", [16, 256, 512])
    sst = din("sst", [16, 2048, 128])
    scv = din("scv", [48, 4096])
    pstage = din("pstage", [8, 4096])
    w_in = din("w_in", [D, IN_DIM])
    dtb = din("dt_bias", [1, 32])
    alog = din("a_log", [1, 32])
    dsk = din("d_skip", [1, 32])
    lng = din("ln_v_g", [1, D])
    lnb = din("ln_v_b", [1, D])
    fng = din("final_norm_g", [1, D])
    wsp = din("w_spatial", [8, 128, 128])
    bsp = din("b_spatial", [1, 1024])
    wkv = din("w_mem_kv", [D, D])
    wpa = din("w_proj_a", [D, D])
    wpb = din("w_proj_b", [2048, D])
    wpx = din("w_proj_x", [512, D])
    wout = din("w_out", [D, D])
    cfd = din("cf", [128, NCF])
    cbd = din("cb", [128, NCB])

    wscr = nc.dram_tensor("wscr", [24, 128, 8192], BF16, kind="Internal").ap()
    yp = dout("yp", [SEQ, D])
    ys = dout("ys", [128, D])
    mk = dout("mk", [256, 512])
    mv = dout("mv", [256, 512])
    hp = dout("hp", [2048, 128])
    cpo = dout("cp", [3, 4096])
    hs = dout("hs", [16, 2048, 128])
    cso = dout("cs", [48, 4096])
    vs = dout("vs", [128, D])

    with ExitStack() as es:
        def sb(name, shape, dt):
            return es.enter_context(nc.sbuf_tensor(name, list(shape), dt))

        psb = [es.enter_context(nc.psum_tensor(f"ps{k}", [128, 512], F32)) for k in range(8)]
        P = Prog(nc, n_dma_sems=16)

        class Banks:
            def __init__(self):
                self.pinned = set()
                self.i = 0

            def get(self, pin=False):
                for _ in range(16):
                    k = self.i
                    self.i = (self.i + 1) % 8
                    if k not in self.pinned:
                        if pin:
                            self.pinned.add(k)
                        return k
                raise RuntimeError("no psum bank")

            def unpin(self, k):
                self.pinned.discard(k)

        BK = Banks()

        def PK(k):
            return ("ps", k)

        def MM(out, lhsT, rhs, start, stop, rd, wr):
            P.add("pe", lambda e: e.matmul(out, lhsT, rhs, start=start, stop=stop), rd, wr)

        def TR(out, in_, ident, rd, wr):
            P.add("pe", lambda e: e.transpose(out, in_, ident), rd, wr)

        def ACT(out, in_, func, rd, wr, **kw):
            P.add("act", lambda e: e.activation(out, in_, func, **kw), rd, wr)

        def TT(out, a, b, op, rd, wr):
            P.add("dve", lambda e: e.tensor_tensor(out, a, b, op), rd, wr)

        def TS(out, a, s1, s2, op0, op1, rd, wr):
            P.add("dve", lambda e: e.tensor_scalar(out, a, s1, s2, op0, op1), rd, wr)

        def TS1(out, a, s, op, rd, wr):
            P.add("dve", lambda e: e.tensor_single_scalar(out, a, s, op), rd, wr)

        def STT(out, in0, scalar, in1, op0, op1, rd, wr):
            P.add("dve", lambda e: e.scalar_tensor_tensor(out, in0, scalar, in1, op0, op1), rd, wr)

        def CP(out, in_, rd, wr):
            P.add("dve", lambda e: e.tensor_copy(out, in_), rd, wr)

        def PTT(out, a, b, op, rd, wr):
            P.add("pool", lambda e: e.tensor_tensor(out, a, b, op), rd, wr)

        def PCP(out, in_, rd, wr):
            P.add("pool", lambda e: e.tensor_copy(out, in_), rd, wr)

        def ACP(out, in_, rd, wr):
            if "acpident" in KVAR:
                P.add("act", lambda e: e.activation(out, in_, AF.Identity), rd, wr)
            else:
                P.add("act", lambda e: e.copy(out, in_), rd, wr)

        def DMA(out, in_, rd, wr, q="sp"):
            P.add(q, lambda e: e.dma_start(out=out, in_=in_), rd, wr, dma=True)

        def MSET(ap, val, wr):
            P.add("dve", lambda e: e.memset(ap, val), (), wr)

        cfs = sb("cfs", [128, NCF], F32)
        cbs = sb("cbs", [128, NCB], BF16)
        ones_f = sb("ones_f", [128, 128], F32)
        ones_b = sb("ones_b", [128, 128], BF16)
        pfm = sb("pfm", [128, 32, 8], F32)
        fg_bc = sb("fg_bc", [128, D], F32)
        lng_bc = sb("lng_bc", [128, D], F32)
        lnb_bc = sb("lnb_bc", [128, D], F32)
        bsp_bc = sb("bsp_bc", [128, 8, 128], F32)
        dtb_bc = sb("dtb_bc", [128, 32], F32)
        a_bc = sb("a_bc", [128, 32], F32)
        dsk_bc = sb("dsk_bc", [128, 32], F32)
        WTp = sb("WTp", [128, 8, 128], BF16)
        WTs = sb("WTs", [128, 8, 128], BF16)
        xhalo = sb("xhalo", [128, 32, 3], F32)
        stats = sb("stats", [128, 256], F32)
        mT = sb("mT", [128, 8, 256], F32)
        mTb = sb("mTb", [128, 8, 256], BF16)
        hnT = sb("hnT", [128, 8, 256], BF16)
        xt = sb("xt", [128, 2, D], F32)
        wbuf = [sb(f"wb{j}", [128, 8192], BF16) for j in range(2)]
        junk = sb("junk", [128, D], BF16)
        xn = sb("xn", [128, D], BF16)
        ARENA = 59648
        arena = sb("arena", [128, ARENA], BF16)

        ident_f = cfs[:, CF_ID:CF_ID + 128]
        ident_b = cbs[:, CB_ID:CB_ID + 128]
        rowmask = cfs[:, CF_RM:CF_RM + 16]

        st_i = [0]

        def stat(n=1):
            if st_i[0] + n > 256:
                st_i[0] = 0
            a = st_i[0]
            st_i[0] += n
            return stats[:, a:a + n], [("st", j) for j in range(a, a + n)]

        class Arena:
            def __init__(self, tag):
                self.off = 0
                self.base = 0
                self.tag = tag

            def take(self, name, shape, dt, reg=False):
                n = int(np.prod(shape))
                nb = n if dt == BF16 else 2 * n
                nb = (nb + 1) // 2 * 2
                if self.off + nb > ARENA:
                    raise RuntimeError(f"arena overflow at {name}: {self.off}+{nb}")
                if reg:
                    P.register(name, self.off, self.off + nb)
                ap = arena[:, self.off:self.off + nb]
                self.off += nb
                if dt == F32:
                    ap = ap.bitcast(F32)
                if len(shape) == 2:
                    v = ap.rearrange("p (a b) -> p a b", a=shape[0])
                elif len(shape) == 3:
                    v = ap.rearrange("p (a b c) -> p a b c", a=shape[0], b=shape[1])
                else:
                    v = ap
                return v

        ARset = Arena("setup")
        pst = ARset.take("pst", [4096], F32)
        wsf = ARset.take("wsf", [8, 128], F32)
        rep = ARset.take("rep", [8, 8], F32)
        DMA(cfs[:], cfd, (), ["cfs"])
        DMA(cbs[:, 0:1408], cbd[:, 0:1408], (), ["cbs"], q="pool")
        DMA(cbs[:, 1408:NCB], cbd[:, 1408:NCB], (), ["cbs"], q="pool")
        DMA(pst[0:8, :], pstage, (), ["pst"])
        DMA(fg_bc[:], fng.partition_broadcast(128), (), ["fg_bc"])
        DMA(lng_bc[:], lng.partition_broadcast(128), (), ["lng_bc"])
        DMA(lnb_bc[:], lnb.partition_broadcast(128), (), ["lnb_bc"])
        DMA(bsp_bc[:].rearrange("p g t -> p (g t)"), bsp.partition_broadcast(128), (), ["bsp_bc"])
        DMA(dtb_bc[:], dtb.partition_broadcast(128), (), ["dtb_bc"])
        DMA(a_bc[:], alog.partition_broadcast(128), (), ["a_bc"])
        DMA(dsk_bc[:], dsk.partition_broadcast(128), (), ["dsk_bc"])
        DMA(wsf, wsp.rearrange("g t s -> t g s"), (), ["wsf"])
        MSET(ones_f[:], 1.0, ["ones_f"])
        MSET(ones_b[:], 1.0, ["ones_b"])
        MSET(xhalo[:], 0.0, ["xhalo"])
        ACT(a_bc[:], a_bc[:], AF.Exp, ["a_bc"], ["a_bc"])
        TS1(a_bc[:], a_bc[:], -1.0, ALU.mult, ["a_bc"], ["a_bc"])
        k = BK.get()
        for c in range(32):
            TR(psb[k][:, c * 8:(c + 1) * 8], pst[0:8, c * 128:(c + 1) * 128], cfs[0:8, CF_ID:CF_ID + 8], ["pst", "cfs"], [PK(k)])
        CP(pfm[:].rearrange("p c k -> p (c k)"), psb[k][:, 0:256], [PK(k)], ["pfm"])
        gfm = pfm[:, 16:24, 5]
        mgfm = pfm[:, 24:32, 5]
        ssdg = pfm[:, 0:16, 5]
        for half in range(2):
            k = BK.get()
            for gg in range(4):
                g = half * 4 + gg
                TR(psb[k][:, gg * 128:(gg + 1) * 128], wsf[:, g, :], ident_f, ["wsf", "cfs"], [PK(k)])
            TT(WTp[:, half * 4:half * 4 + 4, :], psb[k][:].rearrange("p (g t) -> p g t", g=4),
               cfs[:, None, CF_UP:CF_UP + 128].to_broadcast([128, 4, 128]), ALU.mult, [PK(k), "cfs"], ["WTp"])
        k = BK.get()
        MM(psb[k][:, 0:64].rearrange("p (g t) -> p g t", g=8), cbs[0:8, CB_E:CB_E + 128], WTp[0:8, :, 0:8], True, True,
           ["cbs", "WTp"], [PK(k)])
        CP(rep, psb[k][:, 0:64].rearrange("p (g t) -> p g t", g=8), [PK(k)], ["rep"])
        for g in range(8):
            TT(WTs[:, g, :].rearrange("p (b t) -> p b t", b=16), rep[:, g, None, :].to_broadcast([128, 16, 8]),
               rowmask[:, :, None].to_broadcast([128, 16, 8]), ALU.mult, ["rep", "cfs"], ["WTs"])
        P.barrier()
        if STOP <= 1:
            P.emit()
            return nc

        wcnt = [0]

        cached = set()

        def load_w(src, kc, ncols, blk=None):
            j = wcnt[0] % 2
            wcnt[0] += 1
            n = kc * ncols
            flat = wbuf[j][:, 0:n]
            view = flat.rearrange("p (kc n) -> p kc n", kc=kc)
            if blk is None:
                DMA(view, src.rearrange("(kc p) n -> p kc n", p=128), (), [("wb", j)], q="pool")
            elif blk not in cached:
                DMA(view, src.rearrange("(kc p) n -> p kc n", p=128), (), [("wb", j)], q="pool")
                DMA(wscr[blk][:, 0:n], flat, [("wb", j)], [("scr", blk)], q="sp")
                cached.add(blk)
            else:
                DMA(flat, wscr[blk][:, 0:n], [("scr", blk)], [("wb", j)], q="sp")
            return view, ("wb", j)

        def rstd_from_ss(ss_ap, ss_keys, n, out_ap, out_keys):
            ACT(out_ap, ss_ap, AF.Ln, ss_keys, out_keys, scale=1.0 / n, bias=EPS)
            ACT(out_ap, out_ap, AF.Exp, out_keys, out_keys, scale=-0.5)

        def norm_transpose(src_tile, xbuf, xkey, gsc, gkeys, dstT, dkeys, q="sp"):
            DMA(xbuf, src_tile, (), [xkey], q=q)
            ss, ssk = stat()
            ACT(junk[:], xbuf, AF.Square, [xkey], ["junk"] + ssk, accum_out=ss)
            r, rk = stat()
            rstd_from_ss(ss, ssk, D, r, rk)
            ACT(xn[:], xbuf, AF.Identity, [xkey] + rk, ["xn"], scale=r)
            k = BK.get()
            pv = psb[k][:].bitcast(BF16)
            for kc in range(8):
                TR(pv[:, kc * 128:(kc + 1) * 128], xn[:, kc * 128:(kc + 1) * 128], ident_b, ["xn", "cbs"], [PK(k)])
            TT(dstT, pv.rearrange("p (kc t) -> p kc t", kc=8), gsc[:, :, None].to_broadcast([128, 8, 128]), ALU.mult,
               [PK(k)] + gkeys, dkeys)

        def fm_to_rows(src_fn, nrows, out_dram, tagkey, rkeys, rows):
            for piece in range(4):
                for q4 in range(2):
                    k = BK.get()
                    for cc in range(4):
                        c = piece * 8 + q4 * 4 + cc
                        TR(psb[k][0:nrows, cc * 128:(cc + 1) * 128], src_fn(c), ident_f, rkeys + ["cfs"], [PK(k)])
                    CP(rows[0:nrows, q4 * 512:(q4 + 1) * 512], psb[k][0:nrows, :], [PK(k)], [("rows", tagkey)])
                DMA(out_dram[:, piece * 1024:(piece + 1) * 1024], rows[0:nrows, :], [("rows", tagkey)], [("rows", tagkey)])

        def run_st(kind, tok0, nT, AR, hT, hTb, KT, Vt, AR_s):
            T = 128 * nT
            samp = kind == "s"
            xsrc = xsm if samp else xp
            ydst = ys if samp else yp
            Umask_b = cbs[:, CB_US:CB_US + 128] if samp else cbs[:, CB_UP:CB_UP + 128]
            Lmask_b = cbs[:, CB_LS:CB_LS + 128] if samp else cbs[:, CB_LP:CB_LP + 128]
            U_f = cfs[:, CF_US:CF_US + 128] if samp else cfs[:, CF_UP:CF_UP + 128]
            tot_f = cfs[:, CF_ONES:CF_ONES + 128] if samp else ones_f[:]
            WT = WTs if samp else WTp
            bspx = AR_s["bsp_s"] if samp else bsp_bc
            hnk = [("hnT", i) for i in range(nT)]

            def tile_rows(ap, i):
                return ap[tok0 + i * 128: tok0 + (i + 1) * 128, :]

            for i in range(nT):
                norm_transpose(tile_rows(xsrc, i), xt[:, i % 2, :], ("xt", i % 2), gfm, ["pfm"],
                               hnT[:, :, i * 128:(i + 1) * 128], [("hnT", i)])

            def fm_block(wv, wk, nch, c_base, consumer):
                for c in range(nch):
                    k = BK.get()
                    for kc in range(8):
                        MM(psb[k][:, 0:T], wv[:, kc, c * 128:(c + 1) * 128], hnT[:, kc, 0:T], kc == 0, kc == 7,
                           [wk] + hnk, [PK(k)])
                    consumer(c_base + c, psb[k][:, 0:T], k)

            def tm_block(wv, wk, ncols, col_base, consumer):
                for i in range(nT):
                    for s0 in range(0, ncols, 512):
                        n = min(512, ncols - s0)
                        k = BK.get()
                        for kc in range(8):
                            MM(psb[k][:, 0:n], hnT[:, kc, i * 128:(i + 1) * 128], wv[:, kc, s0:s0 + n], kc == 0, kc == 7,
                               [wk, ("hnT", i)], [PK(k)])
                        consumer(i, col_base + s0, n, psb[k][:, 0:n], k)

            AR.off = AR.base
            gate = AR.take("gate", [8, T], BF16, reg=True)
            F0 = AR.take("F0", [2, 1024], F32, reg=True)
            tmpA = AR.take("tmpA", [4, 128], F32, reg=True)
            mark = AR.off
            G0 = AR.take("G0", [8, T], BF16, reg=True)
            G1 = AR.take("G1", [8, T], BF16, reg=True)
            G2 = AR.take("vbf", [nT, 1024], BF16, reg=True)
            G3 = AR.take("G3", [8, T], BF16, reg=True)
            hxT = AR.take("hxT", [4, T], BF16, reg=True)
            PT = AR.take("PT", [4, T], BF16, reg=True)

            def merge(first, last, c, ps_ap, k):
                if first:
                    STT(mT[:, c, 0:T], gate[:, c, :], 1.0, ps_ap, ALU.add, ALU.mult, [PK(k), ("gate", c)], [("mT", c)])
                else:
                    tmpm = F0[:, c % 2, 0:T]
                    STT(tmpm, gate[:, c, :], 1.0, ps_ap, ALU.add, ALU.mult, [PK(k), ("gate", c)], [("F0", c % 2)])
                    if last:
                        TT(mTb[:, c, 0:T], mT[:, c, 0:T], tmpm, ALU.add, [("F0", c % 2), ("mT", c)], [("mTb", c)])
                    else:
                        TT(mT[:, c, 0:T], mT[:, c, 0:T], tmpm, ALU.add, [("F0", c % 2), ("mT", c)], [("mT", c)])

            def gate_consumer(c, ps_ap, k):
                ACT(gate[:, c % 8, :], ps_ap, AF.Tanh, [PK(k)], [("gate", c % 8)], scale=0.5)

            sched = []

            vbf = G2
            acc_v = {}

            def a4_tile(i, kk):
                for half in range(2):
                    kb = kk[half]
                    sl = slice(half * 4, half * 4 + 4)
                    cols = slice(i * 128, (i + 1) * 128)
                    TT(tmpA, psb[kb][:].rearrange("p (g t) -> p g t", g=4), bspx[:, sl, :], ALU.add, [PK(kb), "bsp_bc", "bsp_s"], ["tmpA"])
                    TT(tmpA, tmpA, G0[:, sl, cols], ALU.mult, ["tmpA"] + [("G0", c) for c in range(half * 4, half * 4 + 4)], ["tmpA"])
                    TT(G3[:, sl, cols], tmpA, G1[:, sl, cols], ALU.mult, ["tmpA"] + [("G1", c) for c in range(half * 4, half * 4 + 4)],
                       [("G3", i)])

            var_all, var_keys = stat(nT)
            mean_t = {}

            def v_cons(i, col, n, ps_ap, k):
                sub = col // 512
                fk = [("F0", i % 2)]
                sa, sak = stat()
                ACT(F0[:, i % 2, col:col + n], ps_ap, AF.Gelu, [PK(k)], fk + sak, accum_out=sa)
                acc_v[(i, sub)] = (sa, sak)
                if sub == 1:
                    vt = F0[:, i % 2, :]
                    (s0, s0k), (s1, s1k) = acc_v[(i, 0)], acc_v[(i, 1)]
                    sq, sqk = stat()
                    ACT(junk[:], vt, AF.Square, fk, ["junk"] + sqk, accum_out=sq)
                    mean, mk_ = stat()
                    TS(mean, s0, s1, 1.0 / D, ALU.add, ALU.mult, s0k + s1k, mk_)
                    msq, msk = stat()
                    TT(msq, mean, mean, ALU.mult, mk_, msk)
                    TS(var_all[:, i:i + 1], sq, 1.0 / D, msq, ALU.mult, ALU.subtract, sqk + msk, [var_keys[i]])
                    mean_t[i] = (mean, mk_)
                    if i == nT - 1:
                        ACT(var_all, var_all, AF.Ln, var_keys, var_keys, bias=EPS)
                        ACT(var_all, var_all, AF.Exp, var_keys, var_keys, scale=-0.5)
                        for ii in range(nT):
                            v_norm(ii)

            def v_norm(i):
                fk = [("F0", i % 2)]
                vt = F0[:, i % 2, :]
                mean, mk_ = mean_t[i]
                TS(vt, vt, mean, var_all[:, i:i + 1], ALU.subtract, ALU.mult, fk + mk_ + var_keys, fk)
                TT(vt, vt, lng_bc[:], ALU.mult, fk + ["lng_bc"], fk)
                if samp:
                    TT(vt, vt, lnb_bc[:], ALU.add, fk + ["lnb_bc"], fk)
                    DMA(vs, vt, fk, [])
                    ACP(vbf[:, i, :], vt, fk, [("vbf", i)])
                else:
                    TT(vbf[:, i, :], vt, lnb_bc[:], ALU.add, fk + ["lnb_bc"], [("vbf", i)])

            def spatial_tile(i):
                kk = [BK.get(), BK.get()]
                for g in range(8):
                    kb = kk[g // 4]
                    MM(psb[kb][:, (g % 4) * 128:(g % 4 + 1) * 128], vbf[:, i, g * 128:(g + 1) * 128], WT[:, g, :], True, True,
                       [("vbf", i), "WTp", "WTs"], [PK(kb)])
                a4_tile(i, kk)

            def u_cons(c, ps_ap, k):
                ACT(G0[:, c, :], ps_ap, AF.Gelu, [PK(k)], [("G0", c)])

            def ga_cons(c, ps_ap, k):
                ACT(G1[:, c, :], ps_ap, AF.Silu, [PK(k)], [("G1", c)])

            def proj_generic(hTsrc, hkeys, KC, first, last):
                def f(wv, wk, c_lo, c_hi, col0):
                    for c in range(c_lo, c_hi):
                        k = BK.get()
                        for kc in range(KC):
                            MM(psb[k][:, 0:T], wv[:, kc, (c * 128 - col0):(c * 128 - col0) + 128], hTsrc[:, kc, 0:T], kc == 0, kc == KC - 1,
                               [wk] + hkeys, [PK(k)])
                        merge(first, last, c, psb[k][:, 0:T], k)
                return f

            sched.append((w_in[:, C_U:C_U + 1024], 8, 1024, lambda wv, wk: fm_block(wv, wk, 8, 0, u_cons)))
            sched.append((w_in[:, C_V:C_V + 1024], 8, 1024, lambda wv, wk: tm_block(wv, wk, 1024, 0, v_cons)))
            sched.append((w_in[:, C_GA:C_GA + 1024], 8, 1024, lambda wv, wk: fm_block(wv, wk, 8, 0, ga_cons)))
            def gate_a_consumer(c, ps_ap, k):
                gate_consumer(c, ps_ap, k)
                if nT == 2 and c in (3, 7):
                    spatial_tile(c // 4)
                elif nT == 1 and c == 7:
                    spatial_tile(0)

            sched.append((w_in[:, C_MG:C_MG + 1024], 8, 1024, lambda wv, wk: fm_block(wv, wk, 8, 0, gate_a_consumer)))
            pa = proj_generic(G3, [("G3", i) for i in range(nT)], 8, True, False)

            def proj_a_block(wv, wk):
                pa(wv, wk, 0, 8, 0)

            sched.append((wpa, 8, 1024, proj_a_block))

            qT = G0[:, 0:4, :]
            gx = G0[:, 4:8, :]
            rden = F0[:, 0, 0:T]
            tmpx = F0[:, 1, 0:T]
            SC = 128.0 ** -0.5

            def attn_head(h):
                par = h % 2
                ks_ = []
                for mt in range(2):
                    k = BK.get()
                    MM(psb[k][:, 0:T], KT[:, h, mt * 128:(mt + 1) * 128], qT[:, h, :], True, True, ["KT", ("G0", h)], [PK(k)])
                    ACT(PT[:, par * 2 + mt, :], psb[k][:, 0:T], AF.Exp, [PK(k)], [("PT", par, mt)], scale=SC)
                yield
                ko = BK.get()
                for mt in range(2):
                    MM(psb[ko][:, 0:T], Vt[:, mt, h * 128:(h + 1) * 128], PT[:, par * 2 + mt, :], mt == 0, mt == 1,
                       ["Vt", ("PT", par, mt)], [PK(ko)])
                kd = BK.get()
                for mt in range(2):
                    MM(psb[kd][:, 0:T], ones_b[:], PT[:, par * 2 + mt, :], mt == 0, mt == 1, ["ones_b", ("PT", par, mt)], [PK(kd)])
                P.add("dve", lambda e, kd=kd: e.reciprocal(rden, psb[kd][:, 0:T]), [PK(kd)], [("F0", 0)])
                TT(tmpx, psb[ko][:, 0:T], rden, ALU.mult, [PK(ko), ("F0", 0)], [("F0", 1)])
                TT(hxT[:, h, :], tmpx, gx[:, h, :], ALU.mult, [("F0", 1), ("G0", 4 + h)], [("hxT", h)])

            attpipe = Pipe()

            def attn_prompt():
                pass

            def attn_sample():
                ko = BK.get(pin=True)
                kd = BK.get(pin=True)
                KTs, Kc, Vc, PTs, rd4 = AR_s["KTs"], AR_s["Kc"], AR_s["Vc"], AR_s["PTs"], AR_s["rd4"]
                qk = [("G0", c) for c in range(4)]

                def chain(b):
                    j = b % 2
                    DMA(Kc[:, j], ck[b].rearrange("(mt p) n -> p mt n", p=128), (), [("Kc", j)], q="pool")
                    DMA(Vc[:, j], cv[b].rearrange("(mt p) n -> p mt n", p=128), (), [("Vc", j)], q="pool")
                    k = BK.get(pin=True)
                    pv = psb[k][:].bitcast(BF16)
                    for h in range(4):
                        for mt in range(2):
                            TR(pv[:, h * 256 + mt * 128: h * 256 + (mt + 1) * 128], Kc[:, j, mt, h * 128:(h + 1) * 128], ident_b,
                               [("Kc", j), "cbs"], [PK(k)])
                    yield
                    CP(KTs[:, j], pv.rearrange("p (h m) -> p h m", h=4), [PK(k)], [("KTs", j)])
                    BK.unpin(k)
                    yield
                    k2 = BK.get()
                    for h in range(4):
                        for mt in range(2):
                            MM(psb[k2][:, (h * 2 + mt) * 8:(h * 2 + mt + 1) * 8], KTs[:, j, h, mt * 128:(mt + 1) * 128],
                               qT[:, h, b * 8:(b + 1) * 8], True, True, [("KTs", j)] + qk, [PK(k2)])
                    ACT(PTs[:, j], psb[k2][:, 0:64], AF.Exp, [PK(k2)], [("PTs", j)], scale=SC)
                    for h in range(4):
                        for mt in range(2):
                            MM(psb[ko][:, h * 128 + b * 8: h * 128 + (b + 1) * 8], Vc[:, j, mt, h * 128:(h + 1) * 128],
                               PTs[:, j, (h * 2 + mt) * 8:(h * 2 + mt + 1) * 8], mt == 0, mt == 1, [("Vc", j), ("PTs", j)], [PK(ko)])
                    for h in range(4):
                        for mt in range(2):
                            MM(psb[kd][:, h * 128 + b * 8: h * 128 + (b + 1) * 8], ones_b[:],
                               PTs[:, j, (h * 2 + mt) * 8:(h * 2 + mt + 1) * 8], mt == 0, mt == 1, ["ones_b", ("PTs", j)], [PK(kd)])

                pp = Pipe()
                for b in range(NSEQ_S):
                    pp.tick(chain(b))
                pp.drain()
                P.add("dve", lambda e: e.reciprocal(rd4, psb[kd][:]), [PK(kd)], ["rd4"])
                TT(rd4, psb[ko][:], rd4, ALU.mult, [PK(ko), "rd4"], ["rd4"])
                TT(hxT.rearrange("p h t -> p (h t)"), rd4, gx.rearrange("p h t -> p (h t)"), ALU.mult,
                   ["rd4"] + [("G0", 4 + h) for h in range(4)], [("hxT", h) for h in range(4)])
                BK.unpin(ko)
                BK.unpin(kd)

            attn = attn_sample if samp else attn_prompt

            def qgx_cons(c, ps_ap, k):
                if c < 4:
                    CP(qT[:, c, :], ps_ap, [PK(k)], [("G0", c)])
                else:
                    ACT(gx[:, c - 4, :], ps_ap, AF.Silu, [PK(k)], [("G0", c)])
                    if c == 7:
                        attn()

            sched.append((w_in[:, C_Q:C_Q + 1024], 8, 1024, lambda wv, wk: fm_block(wv, wk, 8, 0, qgx_cons)))
            def gate_x_consumer(c, ps_ap, k):
                gate_consumer(c, ps_ap, k)
                if not samp:
                    if c % 2 == 0:
                        attpipe.tick(attn_head(c // 2))
                    if c == 7:
                        attpipe.drain()

            sched.append((w_in[:, C_MG + 2048:C_MG + 3072], 8, 1024, lambda wv, wk: fm_block(wv, wk, 8, 0, gate_x_consumer)))
            px = proj_generic(hxT, [("hxT", h) for h in range(4)], 4, False, False)
            sched.append((wpx, 4, 1024, lambda wv, wk: px(wv, wk, 0, 8, 0)))

            AR.off = mark
            zs = AR.take("zs", [nT, 2048], BF16, reg=True)
            xs_tm = AR.take("xs_tm", [nT, 2048], BF16, reg=True)
            BT = AR.take("BT", [8, T], BF16, reg=True)
            CT = AR.take("CT", [8, T], BF16, reg=True)
            B_tm = AR.take("B_tm", [nT, 1024], BF16, reg=True)
            hbT = AR.take("hbT", [16, T], BF16, reg=True)
            dtp = AR.take("dtp", [nT, 6, 32], F32, reg=True)
            XR = max(3 + T, 16 * 11)
            NBC, NBS = 4, 3
            xraw = AR.take("xraw", [NBC, XR], F32, reg=True)
            acc = AR.take("acc", [NBC, T], F32, reg=True)
            xsT = AR.take("xsT", [NBC, T], BF16, reg=True)
            x_dt = AR.take("x_dt", [2048], BF16, reg=True)
            xsD = AR.take("xsD", [2048], BF16, reg=True)
            xdd = AR.take("xdd", [2048], BF16, reg=True)
            cbm = AR.take("cbm", [2, 8, 128], BF16, reg=True)
            Rb = AR.take("R", [NBS, 512], BF16, reg=True)
            Eb = AR.take("E", [NBS, 512], BF16, reg=True)
            MTb_ = AR.take("MT", [NBS, 512], BF16, reg=True)
            yt = AR.take("yt", [NBS, 256], F32, reg=True)
            yn = AR.take("yn", [NBS, 256], BF16, reg=True)
            acs = AR.take("acs", [64], F32, reg=True)
            run_st.maxoff = max(getattr(run_st, "maxoff", 0), AR.off)
            if os.environ.get("KDEBUG") and (samp or tok0 == 0):
                print("ST", kind, "arena base", AR.base, "end", AR.off, "of", ARENA)

            def dt_cons(i, col, n, ps_ap, k):
                dt_, adt, ea, cd, ds, tmp = (dtp[:, i, j, :] for j in range(6))
                dk = lambda j: [("dtp", i, j)]
                TT(tmp, ps_ap, dtb_bc[:], ALU.add, [PK(k), "dtb_bc"], dk(5))
                ACT(tmp, tmp, AF.Exp, dk(5), dk(5))
                ACT(dt_, tmp, AF.Ln, dk(5), dk(0), bias=1.0)
                TT(adt, dt_, a_bc[:], ALU.mult, dk(0) + ["a_bc"], dk(1))
                k2 = BK.get()
                MM(psb[k2][:, 0:32], U_f, adt, True, True, ["cfs"] + dk(1), [PK(k2)])
                MM(psb[k2][:, 32:64], tot_f, adt, True, True, ["cfs", "ones_f"] + dk(1), [PK(k2)])
                CP(acs, psb[k2][:, 0:64], [PK(k2)], ["acs"])
                ACT(ea, acs[:, 0:32], AF.Exp, ["acs"], dk(2))
                ACT(cd, acs[:, 32:64], AF.Exp, ["acs"], dk(3))
                TT(tmp, acs[:, 32:64], acs[:, 0:32], ALU.subtract, ["acs"], dk(5))
                ACT(ds, tmp, AF.Exp, dk(5), dk(4))

            def z_cons_block(blk):
                def f(i, col, n, ps_ap, k):
                    ACT(zs[:, i, blk * 1024 + col: blk * 1024 + col + n], ps_ap, AF.Silu, [PK(k)], [("zs", i, blk)])
                return f

            S_ = NSEQ_S if samp else 1
            Lw = LS if samp else T
            Lx = Lw + 3

            convpipe = Pipe()
            ssdpipe = Pipe()

            def conv_chain(c, ps_ap, k):
                j = c % NBC
                xr = xraw[:, j, 0:S_ * Lx].rearrange("p (s l) -> p s l", s=S_)
                xk = ("xraw", j)
                if samp:
                    kst = BK.get()
                    cl = c % 8
                    TR(psb[kst][:, 0:48], AR_s["sc_tm"][0:48, cl * 128:(cl + 1) * 128], cfs[0:48, CF_ID:CF_ID + 48], ["sc_tm", "cfs"], [PK(kst)])
                    CP(xr[:, :, 0:3], psb[kst][:, 0:48].rearrange("p (b k) -> p b k", b=16), [PK(kst)], [xk])
                    ACP(xr[:, :, 3:Lx], ps_ap.rearrange("p (b l) -> p b l", b=16), [PK(k)], [xk])
                    CP(AR_s["xlast"][:, c], xr[:, :, Lw:Lx], [xk], [("xlast", c)])
                else:
                    PCP(xr[:, 0, 0:3], xhalo[:, c, :], [("xhalo", c)], [xk])
                    ACP(xr[:, 0, 3:Lx], ps_ap, [PK(k)], [xk])
                    PCP(xhalo[:, c, :], xr[:, 0, Lw:Lx], [xk], [("xhalo", c)])
                av = acc[:, j, 0:S_ * Lw].rearrange("p (s l) -> p s l", s=S_)
                ak = ("acc", j)
                ACT(av, ps_ap.rearrange("p (s l) -> p s l", s=S_), AF.Identity, [PK(k), "pfm"], [ak],
                    scale=pfm[:, c, 3:4], bias=pfm[:, c, 4:5])
                yield
                for kk in (2, 1, 0):
                    STT(av, xr[:, :, kk:kk + Lw], pfm[:, c, kk:kk + 1], av, ALU.mult, ALU.add, [xk, ak, "pfm"], [ak])
                yield
                avf = acc[:, j, 0:T]
                if c < 24:
                    if c < 16:
                        dst = xsT[:, j, :]
                        dkey = ("xsT", j)
                    else:
                        dst = BT[:, c - 16, :]
                        dkey = ("BT", c - 16)
                    ACT(dst, avf, AF.Silu, [ak], [dkey])
                    kt = BK.get(pin=True)
                    pv = psb[kt][:].bitcast(BF16)
                    for i in range(nT):
                        TR(pv[:, i * 128:(i + 1) * 128], dst[:, i * 128:(i + 1) * 128], ident_b, [dkey, "cbs"], [PK(kt)])
                    yield
                    if c < 16:
                        ACP(xs_tm[:, :, c * 128:(c + 1) * 128], pv[:, 0:T].rearrange("p (i t) -> p i t", i=nT), [PK(kt)], [("xs_tm", c)])
                    else:
                        g = c - 16
                        ACP(B_tm[:, :, g * 128:(g + 1) * 128], pv[:, 0:T].rearrange("p (i t) -> p i t", i=nT), [PK(kt)], [("B_tm", g)])
                    BK.unpin(kt)
                else:
                    ACT(CT[:, c - 24, :], avf, AF.Silu, [ak], [("CT", c - 24)])

            chain_no = [0]

            def ssd_pre(i):
                cols = slice(i * 128, (i + 1) * 128)
                for half in range(2):
                    kcb = BK.get()
                    for gg in range(4):
                        g = half * 4 + gg
                        MM(psb[kcb][:, gg * 128:(gg + 1) * 128], BT[:, g, cols], CT[:, g, cols], True, True, [("BT", g), ("CT", g)], [PK(kcb)])
                    TT(cbm[:, i % 2, half * 4:half * 4 + 4, :], psb[kcb][:].rearrange("p (g l) -> p g l", g=4),
                       Umask_b[:, None, :].to_broadcast([128, 4, 128]), ALU.mult, [PK(kcb), "cbs"], [("cbm", i % 2, half)])

            def ssd_chain(i, g):
                j = chain_no[0] % NBS
                chain_no[0] += 1
                dt_, adt, ea, cd, ds, tmp = (dtp[:, i, q, :] for q in range(6))
                dk = lambda q: [("dtp", i, q)]
                cols = slice(i * 128, (i + 1) * 128)
                hs_ = slice(g * 256, (g + 1) * 256)
                h4 = slice(4 * g, 4 * g + 4)
                xkg = [("xs_tm", 2 * g), ("xs_tm", 2 * g + 1)]
                xs3 = xs_tm[:, i, hs_].rearrange("p (h q) -> p h q", h=4)
                v3 = lambda ap: ap[:, hs_].rearrange("p (h q) -> p h q", h=4)
                if not samp:
                    PTT(v3(x_dt), xs3, dt_[:, h4, None].to_broadcast([128, 4, 64]), ALU.mult, xkg + dk(0), [("x_dt", g)])
                    PTT(v3(xdd), v3(x_dt), ds[:, h4, None].to_broadcast([128, 4, 64]), ALU.mult, [("x_dt", g)] + dk(4), [("xdd", g)])
                PTT(v3(xsD), xs3, dsk_bc[:, h4, None].to_broadcast([128, 4, 64]), ALU.mult, xkg + ["dsk_bc"], [("xsD", g)])
                TT(Rb[:, j, :].rearrange("p (h l) -> p h l", h=4), adt[:, h4, None].to_broadcast([128, 4, 128]),
                   Umask_b[:, None, :].to_broadcast([128, 4, 128]), ALU.mult, dk(1) + ["cbs"], [("R", j)])
                yield
                kD = BK.get()
                MM(psb[kD][:], Lmask_b, Rb[:, j, :], True, True, ["cbs", ("R", j)], [PK(kD)])
                ACT(Eb[:, j, :], psb[kD][:], AF.Exp, [PK(kD)], [("E", j)])
                yield
                TT(MTb_[:, j, :].rearrange("p (h l) -> p h l", h=4), Eb[:, j, :].rearrange("p (h l) -> p h l", h=4),
                   cbm[:, i % 2, g, None, :].to_broadcast([128, 4, 128]), ALU.mult, [("E", j), ("cbm", i % 2, g // 4)], [("MT", j)])
                yield
                ky = BK.get(pin=True)
                MM(psb[ky][:, 0:256], ident_b, xsD[:, hs_], True, False, ["cbs", ("xsD", g)], [PK(ky)])
                for hh in range(4):
                    MM(psb[ky][:, hh * 64:(hh + 1) * 64], MTb_[:, j, hh * 128:(hh + 1) * 128], x_dt[:, g * 256 + hh * 64: g * 256 + (hh + 1) * 64],
                       False, hh == 3, [("MT", j), ("x_dt", g)], [PK(ky)])
                if not samp:
                    MM(psb[ky][:, 256:512], CT[:, g, cols], hTb[:, hs_], True, True, [("CT", g), ("hTb", g)], [PK(ky)])
                yield
                yk = ("yt", j)
                if samp:
                    TT(yt[:, j, :], psb[ky][:, 0:256], AR_s["yoff"][:, hs_], ALU.add, [PK(ky), "yoff"], [yk])
                else:
                    y3 = yt[:, j, :].rearrange("p (h q) -> p h q", h=4)
                    TT(y3, psb[ky][:, 256:512].rearrange("p (h q) -> p h q", h=4), ea[:, h4, None].to_broadcast([128, 4, 64]),
                       ALU.mult, [PK(ky)] + dk(2), [yk])
                    TT(yt[:, j, :], yt[:, j, :], psb[ky][:, 0:256], ALU.add, [yk, PK(ky)], [yk])
                TT(yt[:, j, :], yt[:, j, :], zs[:, i, hs_], ALU.mult, [yk, ("zs", i, g // 4)], [yk])
                BK.unpin(ky)
                yield
                ss, ssk = stat()
                ACT(junk[:, 0:256], yt[:, j, :], AF.Square, [yk], ["junk"] + ssk, accum_out=ss)
                r, rk = stat()
                rstd_from_ss(ss, ssk, 256, r, rk)
                ACT(yn[:, j, :], yt[:, j, :], AF.Identity, [yk] + rk, [("yn", j)], scale=r)
                yield
                kt = BK.get(pin=True)
                pv = psb[kt][:].bitcast(BF16)
                for q in range(2):
                    TR(pv[:, q * 128:(q + 1) * 128], yn[:, j, q * 128:(q + 1) * 128], ident_b, [("yn", j), "cbs"], [PK(kt)])
                if not samp:
                    ks = BK.get(pin=True)
                    MM(psb[ks][:, 0:256], B_tm[:, i, g * 128:(g + 1) * 128], xdd[:, hs_], True, True, [("B_tm", g), ("xdd", g)], [PK(ks)])
                yield
                TT(hbT[:, 2 * g:2 * g + 2, cols], pv[:, 0:256].rearrange("p (q t) -> p q t", q=2),
                   ssdg[:, 2 * g:2 * g + 2, None].to_broadcast([128, 2, 128]), ALU.mult, [PK(kt), "pfm"], [("hbT", i)])
                if not samp:
                    h3 = hT[:, hs_].rearrange("p (h q) -> p h q", h=4)
                    TT(h3, h3, cd[:, h4, None].to_broadcast([128, 4, 64]), ALU.mult, [("hT", g)] + dk(3), [("hT", g)])
                    TT(hT[:, hs_], hT[:, hs_], psb[ks][:, 0:256], ALU.add, [("hT", g), PK(ks)], [("hT", g)])
                    ACP(hTb[:, hs_], hT[:, hs_], [("hT", g)], [("hTb", g)])
                    BK.unpin(ks)
                BK.unpin(kt)

            def sample_states():
                h0n, h0T, Bm, cdn, cdx, CTm, yoff = (AR_s[n] for n in ("h0n", "h0T", "Bm", "cdn", "cdx", "CTm", "yoff"))
                cd = dtp[:, 0, 3, :]
                ea = dtp[:, 0, 2, :]
                ds = dtp[:, 0, 4, :]
                dt_ = dtp[:, 0, 0, :]
                sel = cfs[:, CF_RS:CF_RS + 16]
                kc_ = BK.get(pin=True)
                for jj in range(16):
                    CP(cdx[:, jj % 2], cd[:, 2 * jj:2 * jj + 2, None].to_broadcast([128, 2, 64]), [("dtp", 0, 3)], [("cdx", jj % 2)])
                    MM(psb[kc_][:, jj * 16:(jj + 1) * 16], cdx[:, jj % 2].rearrange("p a b -> p (a b)"), sel, True, True,
                       [("cdx", jj % 2), "cfs"], [PK(kc_)])
                CP(cdn, psb[kc_][:, 0:256], [PK(kc_)], ["cdn"])
                BK.unpin(kc_)
                cdn3 = cdn.rearrange("p (j b) -> p j b", j=16)
                xs3 = xs_tm[:, 0, :].rearrange("p (h q) -> p h q", h=32)
                xk = [("xs_tm", c) for c in range(16)]
                TT(x_dt.rearrange("p (h q) -> p h q", h=32), xs3, dt_[:, :, None].to_broadcast([128, 32, 64]), ALU.mult, xk + [("dtp", 0, 0)], [("x_dt", g) for g in range(8)])
                TT(xdd.rearrange("p (h q) -> p h q", h=32), x_dt.rearrange("p (h q) -> p h q", h=32),
                   ds[:, :, None].to_broadcast([128, 32, 64]), ALU.mult, [("x_dt", g) for g in range(8)] + [("dtp", 0, 4)], [("xdd", g) for g in range(8)])
                kyo = [BK.get(pin=True) for _ in range(4)]

                def schain(b):
                    j = b % 2
                    hb_ = h0n[j]
                    hkq = lambda q4: ("h0n%d" % j, q4)
                    src3 = sst[b].rearrange("(j p) n -> p j n", p=128)
                    dst3 = hs[b].rearrange("(j p) n -> p j n", p=128)
                    for q4 in range(4):
                        DMA(hb_[:, q4 * 4:(q4 + 1) * 4, :], src3[:, q4 * 4:(q4 + 1) * 4, :], (), [hkq(q4)])
                        k = BK.get()
                        for cc in range(4):
                            jj = q4 * 4 + cc
                            TR(psb[k][:, cc * 128:(cc + 1) * 128], hb_[:, jj, :], ident_f, [hkq(q4), "cfs"], [PK(k)])
                        ACP(h0T[:, q4 * 512:(q4 + 1) * 512], psb[k][:], [PK(k)], [("h0T", q4)])
                    TT(CTm, CT[:, :, 0:128], cbs[:, None, CB_BD + b * 128: CB_BD + (b + 1) * 128].to_broadcast([128, 8, 128]), ALU.mult,
                       [("CT", g) for g in range(8)] + ["cbs"], ["CTm"])
                    TT(Bm[:, j], B_tm[:, 0, :], rowmask[:, b:b + 1].to_broadcast([128, 1024]), ALU.mult,
                       [("B_tm", g) for g in range(8)] + ["cfs"], [("Bm", j)])
                    yield
                    for g in range(8):
                        kq = kyo[g // 2]
                        MM(psb[kq][:, (g % 2) * 256:(g % 2 + 1) * 256], CTm[:, g, :], h0T[:, g * 256:(g + 1) * 256],
                           b == 0 and g % 2 == 0, b == NSEQ_S - 1 and g % 2 == 1, ["CTm", ("h0T", g // 2)], [PK(kq)])
                    yield
                    for q4 in range(4):
                        k = BK.get()
                        for cc in range(4):
                            jj = q4 * 4 + cc
                            g = jj // 2
                            MM(psb[k][:, cc * 128:(cc + 1) * 128], xdd[:, jj * 128:(jj + 1) * 128], Bm[:, j, g * 128:(g + 1) * 128], True, True,
                               [("xdd", g), ("Bm", j)], [PK(k)])
                        hv = hb_[:, q4 * 4:(q4 + 1) * 4, :]
                        (PTT if q4 % 2 == 0 else TT)(hv, hv, cdn3[:, q4 * 4:(q4 + 1) * 4, b:b + 1].to_broadcast([128, 4, 128]), ALU.mult,
                                                      [hkq(q4), "cdn"], [hkq(q4)])
                        TT(hv, hv, psb[k][:].rearrange("p (c n) -> p c n", c=4), ALU.add, [hkq(q4), PK(k)], [hkq(q4)])
                        DMA(dst3[:, q4 * 4:(q4 + 1) * 4, :], hv, [hkq(q4)], [hkq(q4)])

                sp_ = Pipe()
                for b in range(NSEQ_S):
                    sp_.tick(schain(b))
                sp_.drain()
                for q in range(4):
                    TT(yoff[:, q * 512:(q + 1) * 512].rearrange("p (h q) -> p h q", h=8), psb[kyo[q]][:].rearrange("p (h q) -> p h q", h=8),
                       ea[:, 8 * q:8 * q + 8, None].to_broadcast([128, 8, 64]), ALU.mult, [PK(kyo[q]), ("dtp", 0, 2)], ["yoff"])
                    BK.unpin(kyo[q])

            def c_last_cons(c, ps_ap, k):
                convpipe.tick(conv_chain(c, ps_ap, k))
                if c == 31:
                    convpipe.drain()
                    if samp:
                        sample_states()
                    for i in range(nT):
                        ssd_pre(i)
                        for g in range(8):
                            ssdpipe.tick(ssd_chain(i, g))
                    ssdpipe.drain()

            def first_b(wv, wk):
                tm_block(wv, wk, 32, 0, dt_cons)

            def xbc_block(blk):
                def f(wv, wk):
                    if samp:
                        DMA(AR_s["sc_tm"][0:48, :], scv[:, blk * 1024:(blk + 1) * 1024], (), ["sc_tm"])
                    fm_block(wv, wk, 8, blk * 8, c_last_cons)
                return f

            sched.append((w_in[:, C_DT:C_DT + 32], 8, 32, first_b))
            sched.append((w_in[:, C_Z:C_Z + 1024], 8, 1024, lambda wv, wk: tm_block(wv, wk, 1024, 0, z_cons_block(0))))
            sched.append((w_in[:, C_Z + 1024:C_Z + 2048], 8, 1024, lambda wv, wk: tm_block(wv, wk, 1024, 0, z_cons_block(1))))
            for blk in range(4):
                sched.append((w_in[:, C_XBC + blk * 1024:C_XBC + (blk + 1) * 1024], 8, 1024, xbc_block(blk)))
            def gate_b_consumer(c, ps_ap, k):
                gate_consumer(c, ps_ap, k)

            sched.append((w_in[:, C_MG + 1024:C_MG + 2048], 8, 1024, lambda wv, wk: fm_block(wv, wk, 8, 0, gate_b_consumer)))
            pb = proj_generic(hbT, [("hbT", i) for i in range(nT)], 16, False, True)
            sched.append((wpb[:, 0:512], 16, 512, lambda wv, wk: pb(wv, wk, 0, 4, 0)))
            sched.append((wpb[:, 512:1024], 16, 512, lambda wv, wk: pb(wv, wk, 4, 8, 512)))

            yres = F0

            def out_block(wv, wk):
                for i in range(nT):
                    j = i % 2
                    DMA(xt[:, j, :], tile_rows(xsrc, i), (), [("xt", j)])
                    for half in range(2):
                        k = BK.get()
                        for kc in range(8):
                            MM(psb[k][:], mTb[:, kc, i * 128:(i + 1) * 128], wv[:, kc, half * 512:(half + 1) * 512], kc == 0, kc == 7,
                               [wk] + [("mTb", c) for c in range(8)], [PK(k)])
                        STT(yres[:, j, half * 512:(half + 1) * 512], psb[k][:], 0.5, xt[:, j, half * 512:(half + 1) * 512], ALU.mult, ALU.add,
                            [PK(k), ("xt", j)], [("F0", j)])
                    ss, ssk = stat()
                    ACT(junk[:], yres[:, j, :], AF.Square, [("F0", j)], ["junk"] + ssk, accum_out=ss)
                    r, rk = stat()
                    rstd_from_ss(ss, ssk, D, r, rk)
                    ACT(yres[:, j, :], yres[:, j, :], AF.Identity, [("F0", j)] + rk, [("F0", j)], scale=r)
                    TT(yres[:, j, :], yres[:, j, :], fg_bc[:], ALU.mult, [("F0", j), "fg_bc"], [("F0", j)])
                    DMA(tile_rows(ydst, i), yres[:, j, :], [("F0", j)], [("F0", j)], q="pool")

            sched.append((wout, 8, 1024, out_block))

            if not cached:
                for bi, (src_, kc_, ncols_, _fn) in enumerate(sched):
                    n_ = kc_ * ncols_
                    DMA(wscr[bi][:, 0:n_].rearrange("p (kc n) -> p kc n", kc=kc_), src_.rearrange("(kc p) n -> p kc n", p=128),
                        (), [("scr", bi)], q="pool")
                    cached.add(bi)
            nxt = load_w(sched[0][0], sched[0][1], sched[0][2], 0)
            for si, (src, kc, ncols, fn) in enumerate(sched):
                cur = nxt
                if si + 1 < len(sched):
                    nxt = load_w(sched[si + 1][0], sched[si + 1][1], sched[si + 1][2], si + 1)
                fn(cur[0], cur[1])

        ARp = Arena("p")
        hT = ARp.take("hT", [2048], F32)
        hTb = ARp.take("hTb", [2048], BF16)
        KT = ARp.take("KT", [4, 256], BF16)
        Vt = ARp.take("Vt", [2, 512], BF16)
        ARp.base = ARp.off
        MSET(hT, 0.0, [("hT", g) for g in range(8)])
        MSET(hTb, 0.0, [("hTb", g) for g in range(8)])
        memT = ARp.take("memT", [8, 256], BF16, reg=True)
        kvf = ARp.take("kvf", [2, 512], F32, reg=True)
        for mt in range(2):
            norm_transpose(mem[mt * 128:(mt + 1) * 128, :], xt[:, mt, :], ("xt", mt), mgfm, ["pfm"],
                           memT[:, :, mt * 128:(mt + 1) * 128], [("memT", mt)])
        if SUB <= 1:
            P.barrier()
            P.emit()
            return nc
        wv, wk = load_w(wkv, 8, 1024)
        if SUB <= 2:
            P.barrier()
            P.emit()
            return nc
        for mt in range(2):
            for half in range(2):
                k = BK.get()
                for kc in range(8):
                    MM(psb[k][:], memT[:, kc, mt * 128:(mt + 1) * 128], wv[:, kc, half * 512:(half + 1) * 512], kc == 0, kc == 7,
                       [wk, ("memT", mt)], [PK(k)])
                if "nocp" not in KVAR:
                    CP(kvf[:, half, :], psb[k][:], [PK(k)], [("kvf", half)])
                if "nodma" not in KVAR:
                    DMA((mk if half == 0 else mv)[mt * 128:(mt + 1) * 128, :], kvf[:, half, :], [("kvf", half)], [("kvf", half)])
                if half == 1 and "noacp" not in KVAR:
                    ACP(Vt[:, mt, :], psb[k][:], [PK(k)] + ([("kvf", half)] if "acpdep" in KVAR else []), ["Vt"])
        if SUB <= 3:
            P.barrier()
            P.emit()
            return nc
        for h in range(4):
            k = BK.get()
            for kc in range(8):
                MM(psb[k][:, 0:256], wv[:, kc, h * 128:(h + 1) * 128], memT[:, kc, :], kc == 0, kc == 7,
                   [wk, ("memT", 0), ("memT", 1)], [PK(k)])
            CP(KT[:, h, :], psb[k][:, 0:256], [PK(k)], ["KT"])
        if STOP <= 2:
            P.emit()
            return nc

        NT_P = 2
        for st in range(SEQ // (128 * NT_P)):
            run_st("p", st * 128 * NT_P, NT_P, ARp, hT, hTb, KT, Vt, None)
            if STOP <= 3:
                P.emit()
                return nc
        P.barrier()
        ARp.off = ARp.base
        hpo = ARp.take("hpo", [16, 128], F32)
        rows_p = ARp.take("rows_p", [1024], F32)
        for q4 in range(4):
            k = BK.get()
            for cc in range(4):
                jj = q4 * 4 + cc
                TR(psb[k][:, cc * 128:(cc + 1) * 128], hT[:, jj * 128:(jj + 1) * 128], ident_f, [("hT", g) for g in range(8)] + ["cfs"], [PK(k)])
            CP(hpo[:, q4 * 4:(q4 + 1) * 4, :], psb[k][:].rearrange("p (c n) -> p c n", c=4), [PK(k)], [("hpo", q4)])
        DMA(hp.rearrange("(j p) n -> p j n", p=128), hpo, [("hpo", q4) for q4 in range(4)], [])
        fm_to_rows(lambda c: xhalo[:, c, :], 3, cpo, "cp", [("xhalo", c) for c in range(32)], rows_p)
        P.barrier()
        if STOP <= 4:
            P.emit()
            return nc

        ARs = Arena("s")
        AR_s = {}
        AR_s["sc_tm"] = ARs.take("sc_tm", [1024], F32)
        AR_s["xlast"] = ARs.take("xlast", [32, 16, 3], F32)
        AR_s["bsp_s"] = ARs.take("bsp_s", [8, 128], F32)
        AR_s["CTm"] = ARs.take("CTm", [8, 128], BF16)
        AR_s["h0T"] = ARs.take("h0T", [2048], BF16)
        h0n0 = ARs.take("h0n0", [16, 128], F32)
        AR_s["Bm"] = ARs.take("Bm", [2, 1024], BF16)
        AR_s["cdx"] = ARs.take("cdx", [2, 2, 64], F32)
        mark_s = ARs.off
        AR_s["KTs"] = ARs.take("KTs", [2, 4, 256], BF16, reg=True)
        AR_s["Kc"] = ARs.take("Kc", [2, 2, 512], BF16, reg=True)
        AR_s["Vc"] = ARs.take("Vc", [2, 2, 512], BF16, reg=True)
        AR_s["PTs"] = ARs.take("PTs", [2, 64], BF16, reg=True)
        AR_s["rd4"] = ARs.take("rd4", [512], F32, reg=True)
        off_att = ARs.off
        ARs.off = mark_s
        AR_s["yoff"] = ARs.take("yoff", [2048], F32, reg=True)
        AR_s["cdn"] = ARs.take("cdn", [256], F32, reg=True)
        h0n1 = ARs.take("h0n1", [16, 128], F32, reg=True)
        AR_s["h0n"] = [h0n0, h0n1]
        ARs.off = max(ARs.off, off_att)
        rows_s = AR_s["sc_tm"]
        ARs.base = ARs.off
        for g in range(8):
            CP(AR_s["bsp_s"][:, g, :].rearrange("p (b l) -> p b l", b=16), bsp_bc[:, g, None, 0:8].to_broadcast([128, 16, 8]),
               ["bsp_bc"], ["bsp_s"])
        run_st("s", 0, 1, ARs, None, None, None, None, AR_s)
        P.barrier()
        fm_to_rows(lambda c: AR_s["xlast"][:, c].rearrange("p b k -> p (b k)"), 48, cso, "cs", [("xlast", c) for c in range(32)], rows_s)

        P.emit()
    return nc


_NC_CACHE = {}


def kernel(x_prompt, x_sample, mem_prompt, cache_mem_k, cache_mem_v, state_ssm, state_conv, norm_g, w_in,
           conv_w, conv_b, dt_bias, a_log, d_skip, ssd_norm_g, ln_v_g, ln_v_b, w_spatial, b_spatial,
           mem_norm_g, w_mem_kv, w_proj_a, w_proj_b, w_proj_x, w_out, final_norm_g):
    f = lambda a: np.ascontiguousarray(np.asarray(a, dtype=np.float32))
    if "nc" not in _NC_CACHE:
        _NC_CACHE["nc"] = build_nc()
    nc = _NC_CACHE["nc"]
    cf, cb = _consts()
    pstage = np.zeros((8, 4096), np.float32)
    pstage[0:4] = f(conv_w)[0]
    pstage[4] = f(conv_b)[0]
    pstage[5, 0:2048] = f(ssd_norm_g)[0]
    pstage[5, 2048:3072] = f(norm_g)[0]
    pstage[5, 3072:4096] = f(mem_norm_g)[0]
    shared = dict(
        pstage=pstage, w_in=f(w_in)[0], dt_bias=f(dt_bias).reshape(1, 32), a_log=f(a_log).reshape(1, 32),
        d_skip=f(d_skip).reshape(1, 32), ln_v_g=f(ln_v_g).reshape(1, D), ln_v_b=f(ln_v_b).reshape(1, D),
        final_norm_g=f(final_norm_g).reshape(1, D), w_spatial=f(w_spatial)[0], b_spatial=f(b_spatial).reshape(1, 1024),
        w_mem_kv=f(w_mem_kv)[0], w_proj_a=f(w_proj_a)[0], w_proj_b=f(w_proj_b)[0], w_proj_x=f(w_proj_x)[0],
        w_out=f(w_out)[0], cf=cf, cb=cb,
    )
    xp_, xs_, mem_ = f(x_prompt), f(x_sample), f(mem_prompt)
    ck_, cv_, sst_, scv_ = f(cache_mem_k)[0], f(cache_mem_v)[0], f(state_ssm)[0], f(state_conv)[0]
    in_maps = []
    for c in range(NCORES):
        sl = slice(c * 16, (c + 1) * 16)
        m = dict(shared)
        m["xp"] = xp_[c]
        m["xsm"] = xs_[sl].reshape(128, D)
        m["mem"] = mem_[c]
        m["ck"] = ck_[sl].reshape(16, 256, 512)
        m["cv"] = cv_[sl].reshape(16, 256, 512)
        m["sst"] = sst_[sl].reshape(16, 2048, 128)
        m["scv"] = scv_[sl].reshape(48, 4096)
        in_maps.append(m)
    res = run_bass_kernel_spmd(nc, in_maps, core_ids=list(range(NCORES)))
    R = res.results
    y_prompt = np.stack([R[c]["yp"] for c in range(NCORES)]).reshape(8, SEQ, D)
    y_sample = np.concatenate([R[c]["ys"].reshape(16, 8, D) for c in range(NCORES)], 0)
    mk = np.stack([R[c]["mk"].reshape(256, 4, 128) for c in range(NCORES)])[None]
    mv = np.stack([R[c]["mv"].reshape(256, 4, 128) for c in range(NCORES)])[None]
    hpo = np.stack([R[c]["hp"].reshape(32, 64, 128) for c in range(NCORES)])[None]
    cpo = np.stack([R[c]["cp"] for c in range(NCORES)])[None]
    hso = np.concatenate([R[c]["hs"].reshape(16, 32, 64, 128) for c in range(NCORES)], 0)[None]
    cso = np.concatenate([R[c]["cs"].reshape(16, 3, 4096) for c in range(NCORES)], 0)[None]
    vso = np.concatenate([R[c]["vs"].reshape(16, 8, D) for c in range(NCORES)], 0)[None]
    return (y_prompt.astype(np.float32), y_sample.astype(np.float32), mk.astype(np.float32), mv.astype(np.float32),
            hpo.astype(np.float32), cpo.astype(np.float32), hso.astype(np.float32), cso.astype(np.float32),
            vso.astype(np.float32))
```
